# Optimizing a Trainium2 kernel written in Bass

```python
import math
import jax, jax.numpy as jnp
from jax import lax
import numpy as np

D_MODEL = 2048
BATCH = 4
SEQ = 2048
DEPTH = 1
DEC_BATCH = 32
DEC_SEQ = 8
PAST_LEN = 8192
PAGE_SIZE = 128

H_RET = 8
DK_RET = 256
DV_RET = 256
RET_CHUNK = 128
H_ATT = 8
HD_ATT = 128
MOBA_BLOCK = 256
MOBA_TOPK = 3
MOBA_QROWS = 64
D_FF = 4 * D_MODEL
RET_QK = H_RET * DK_RET
RET_V = H_RET * DV_RET
ATT_W = H_ATT * HD_ATT
W_IN_COLS = 2 * RET_QK + 2 * RET_V + 3 * ATT_W + 2 * D_MODEL
ROPE_BASE = 10000.0
LN_EPS = 1e-5
GN_EPS = 1e-5
ALPHA = (2 * DEPTH) ** 0.25
BETA = (8 * DEPTH) ** -0.25

kernel_name = 'retnet_moba_gated_hybrid_step'


def _layer_norm(x, g, b):
    x32 = x.astype(jnp.float32)
    mu = jnp.mean(x32, axis=-1, keepdims=True)
    var = jnp.mean(jnp.square(x32 - mu), axis=-1, keepdims=True)
    return ((x32 - mu) * lax.rsqrt(var + LN_EPS) * g + b).astype(x.dtype)


def _rotary(x, pos):
    half = x.shape[-1] // 2
    inv = ROPE_BASE ** (-jnp.arange(half, dtype=jnp.float32) / half)
    ang = pos.astype(jnp.float32)[:, None] * inv[None, :]
    cos = jnp.cos(ang)[None, :, None, :]
    sin = jnp.sin(ang)[None, :, None, :]
    x32 = x.astype(jnp.float32)
    x1, x2 = x32[..., :half], x32[..., half:]
    return jnp.concatenate([x1 * cos - x2 * sin, x1 * sin + x2 * cos], axis=-1).astype(x.dtype)


def _log_gamma():
    return jnp.log1p(-jnp.exp2(-5.0 - jnp.arange(H_RET, dtype=jnp.float32)))


def _mixer_in(x, pos, w_in):
    B, T = x.shape[0], x.shape[1]
    z = jnp.einsum('btd,de->bte', x, w_in)
    cuts = [int(c) for c in np.cumsum([RET_QK, RET_QK, RET_V, RET_V, ATT_W, ATT_W, ATT_W, D_MODEL])]
    q_r, k_r, v_r, g_r, q_a, k_a, v_a, gb_r, gb_a = jnp.split(z, cuts, axis=-1)
    q_r = _rotary(q_r.reshape(B, T, H_RET, DK_RET), pos)
    k_r = _rotary(k_r.reshape(B, T, H_RET, DK_RET), pos) * (DK_RET ** -0.5)
    v_r = v_r.reshape(B, T, H_RET, DV_RET)
    q_a = q_a.reshape(B, T, H_ATT, HD_ATT)
    k_a = k_a.reshape(B, T, H_ATT, HD_ATT)
    v_a = v_a.reshape(B, T, H_ATT, HD_ATT)
    return q_r, k_r, v_r, g_r, q_a, k_a, v_a, gb_r, gb_a


def _retention_chunk(q, k, v, S, log_gamma):
    f32 = jnp.float32
    q, k, v, S = (a.astype(f32) for a in (q, k, v, S))
    C = q.shape[1]
    idx = jnp.arange(C, dtype=f32)
    diff = idx[:, None] - idx[None, :]
    causal = diff >= 0
    decay = jnp.where(causal, jnp.exp(log_gamma[:, None, None] * jnp.where(causal, diff, 0.0)), 0.0)
    scores = jnp.einsum('bihd,bjhd->bhij', q, k) * decay
    inner = jnp.einsum('bhij,bjhe->bihe', scores, v)
    q_decay = jnp.exp(log_gamma[None, :] * (idx[:, None] + 1.0))
    cross = jnp.einsum('bihd,bhde->bihe', q, S) * q_decay[None, :, :, None]
    k_decay = jnp.exp(log_gamma[None, :] * (C - 1.0 - idx[:, None]))
    S_new = (jnp.exp(log_gamma * C)[None, :, None, None] * S
             + jnp.einsum('bjhd,bjhe->bhde', k * k_decay[None, :, :, None], v))
    return inner + cross, S_new


def _retention_prompt(q, k, v, log_gamma):
    B, T = q.shape[0], q.shape[1]
    n = T // RET_CHUNK

    def to_chunks(a):
        return a.reshape(B, n, RET_CHUNK, a.shape[2], a.shape[3]).swapaxes(0, 1)

    def step(S, qkv):
        o, S_new = _retention_chunk(qkv[0], qkv[1], qkv[2], S, log_gamma)
        return S_new, o

    S0 = jnp.zeros((B, H_RET, DK_RET, DV_RET), jnp.float32)
    S, o = lax.scan(step, S0, (to_chunks(q), to_chunks(k), to_chunks(v)))
    return o.swapaxes(0, 1).reshape(B, T, H_RET, DV_RET), S


def _pad_blocks(a):
    pad = -a.shape[1] % MOBA_BLOCK
    return jnp.pad(a, ((0, 0), (0, pad), (0, 0), (0, 0)))


def _moba(q, k, v, q_pos):
    B, Tq = q.shape[0], q.shape[1]
    nb = k.shape[1] // MOBA_BLOCK
    n_sel = min(MOBA_TOPK, nb)
    kb = k.reshape(B, nb, MOBA_BLOCK, H_ATT, HD_ATT)
    vb = v.reshape(B, nb, MOBA_BLOCK, H_ATT, HD_ATT)
    k_mean = jnp.mean(kb, axis=2, dtype=jnp.float32)
    bi = jnp.arange(B)[:, None, None, None]
    hi = jnp.arange(H_ATT)[None, :, None, None]
    offs = jnp.arange(MOBA_BLOCK, dtype=jnp.int32)
    is_sel = jnp.arange(n_sel + 1) < n_sel
    scale = HD_ATT ** -0.5

    def attend(args):
        qc, pc = args
        qblk = pc // MOBA_BLOCK
        s = jnp.einsum('bqhd,bnhd->bhqn', qc, k_mean, preferred_element_type=jnp.float32)
        eligible = jnp.arange(nb)[None, :] < qblk[:, None]
        s = jnp.where(eligible, s, -jnp.inf)
        _, sel = lax.top_k(s, n_sel)
        own = jnp.broadcast_to(qblk[None, None, :, None], sel.shape[:3] + (1,)).astype(sel.dtype)
        blocks = jnp.concatenate([sel, own], axis=-1)
        kg = kb[bi, blocks, :, hi]
        vg = vb[bi, blocks, :, hi]
        kpos = blocks[..., None] * MOBA_BLOCK + offs
        blk_ok = jnp.logical_or(~is_sel, blocks < qblk[:, None])
        valid = blk_ok[..., None] & (kpos <= pc[:, None, None])
        logits = jnp.einsum('bqhd,bhqjkd->bhqjk', qc, kg, preferred_element_type=jnp.float32) * scale
        logits = jnp.where(valid, logits, -jnp.inf)
        p = jax.nn.softmax(logits.reshape(logits.shape[0], logits.shape[1], logits.shape[2], -1), axis=-1)
        p = p.reshape(logits.shape).astype(vg.dtype)
        return jnp.einsum('bhqjk,bhqjkd->bqhd', p, vg)

    qb = math.gcd(Tq, max(1, MOBA_QROWS // B))
    n = Tq // qb
    qs = q.reshape(B, n, qb, H_ATT, HD_ATT).swapaxes(0, 1)
    ps = q_pos.reshape(n, qb)
    out = lax.map(attend, (qs, ps))
    return out.swapaxes(0, 1).reshape(B, Tq, H_ATT, HD_ATT)


def _merge(x, o_r, g_r, o_a, gb_r, gb_a, gn_gain, w_ret_br, w_att_br, w_out,
           ln1_g, ln1_b, w_up, w_down, ln2_g, ln2_b):
    B, T = x.shape[0], x.shape[1]
    o32 = o_r.astype(jnp.float32)
    mu = jnp.mean(o32, axis=-1, keepdims=True)
    var = jnp.mean(jnp.square(o32 - mu), axis=-1, keepdims=True)
    o_n = ((o32 - mu) * lax.rsqrt(var + GN_EPS) * gn_gain).reshape(B, T, RET_V).astype(x.dtype)
    ret = jnp.einsum('bte,ed->btd', o_n * jax.nn.silu(g_r), w_ret_br)
    att = jnp.einsum('bte,ed->btd', o_a.reshape(B, T, ATT_W), w_att_br)
    mix = jnp.einsum('btd,de->bte', jax.nn.sigmoid(gb_r) * ret + jax.nn.sigmoid(gb_a) * att, w_out)
    h = _layer_norm(ALPHA * x + mix, ln1_g, ln1_b)
    ff = jnp.einsum('btf,fd->btd', jnp.square(jax.nn.relu(jnp.einsum('btd,df->btf', h, w_up))), w_down)
    return _layer_norm(ALPHA * h + ff, ln2_g, ln2_b)


def setup_inputs(seed: int = 0) -> dict:
    key = jax.random.key(seed)
    ks = jax.random.split(key, 17)
    f32 = jnp.float32
    n_pages = PAST_LEN // PAGE_SIZE
    n_used = DEC_BATCH * n_pages
    n_phys = n_used + (n_used + 3) // 4

    def nrm(k, shape, s):
        return jax.random.normal(k, shape, f32) * s

    gam = 1.0 - jnp.exp2(-5.0 - jnp.arange(H_RET, dtype=f32))
    st_scale = lax.rsqrt(1.0 - gam * gam) * (DK_RET ** -0.5)
    page_table = jax.random.permutation(ks[5], n_phys)[:n_used].reshape(DEC_BATCH, n_pages).astype(jnp.int32)
    return {
        'x_prompt': nrm(ks[0], (BATCH, SEQ, D_MODEL), 1.0),
        'x_sample': nrm(ks[1], (DEC_BATCH, DEC_SEQ, D_MODEL), 1.0),
        'cache_k': nrm(ks[2], (DEPTH, n_phys, PAGE_SIZE, H_ATT, HD_ATT), 1.0),
        'cache_v': nrm(ks[3], (DEPTH, n_phys, PAGE_SIZE, H_ATT, HD_ATT), 1.0),
        'state_ret': nrm(ks[4], (DEPTH, DEC_BATCH, H_RET, DK_RET, DV_RET), 1.0) * st_scale[None, None, :, None, None],
        'page_table': page_table,
        'w_in': nrm(ks[6], (DEPTH, D_MODEL, W_IN_COLS), D_MODEL ** -0.5),
        'ret_gn_gain': 1.0 + nrm(ks[7], (DEPTH, H_RET, DV_RET), 0.02),
        'w_ret_br': nrm(ks[8], (DEPTH, RET_V, D_MODEL), BETA * RET_V ** -0.5),
        'w_att_br': nrm(ks[9], (DEPTH, ATT_W, D_MODEL), BETA * ATT_W ** -0.5),
        'w_out': nrm(ks[10], (DEPTH, D_MODEL, D_MODEL), BETA * D_MODEL ** -0.5),
        'ln1_g': 1.0 + nrm(ks[11], (DEPTH, D_MODEL), 0.02),
        'ln1_b': nrm(ks[12], (DEPTH, D_MODEL), 0.02),
        'w_up': nrm(ks[13], (DEPTH, D_MODEL, D_FF), D_MODEL ** -0.5),
        'w_down': nrm(ks[14], (DEPTH, D_FF, D_MODEL), BETA * D_FF ** -0.5),
        'ln2_g': 1.0 + nrm(ks[15], (DEPTH, D_MODEL), 0.02),
        'ln2_b': nrm(ks[16], (DEPTH, D_MODEL), 0.02),
    }


def reference(x_prompt, x_sample, cache_k, cache_v, state_ret, page_table,
              w_in, ret_gn_gain, w_ret_br, w_att_br, w_out, ln1_g, ln1_b,
              w_up, w_down, ln2_g, ln2_b):
    pos_p = jnp.arange(SEQ, dtype=jnp.int32)
    pos_s = PAST_LEN + jnp.arange(DEC_SEQ, dtype=jnp.int32)
    log_gamma = _log_gamma()
    pad_s = -(PAST_LEN + DEC_SEQ) % MOBA_BLOCK
    y_p, y_s = x_prompt, x_sample
    kp, vp, sp, ks, vs, ss = [], [], [], [], [], []
    for l in range(DEPTH):
        out_w = (ret_gn_gain[l], w_ret_br[l], w_att_br[l], w_out[l], ln1_g[l], ln1_b[l],
                 w_up[l], w_down[l], ln2_g[l], ln2_b[l])
        q_r, k_r, v_r, g_r, q_a, k_a, v_a, gb_r, gb_a = _mixer_in(y_p, pos_p, w_in[l])
        o_r, s_new = _retention_prompt(q_r, k_r, v_r, log_gamma)
        o_a = _moba(q_a, _pad_blocks(k_a), _pad_blocks(v_a), pos_p)
        y_p = _merge(y_p, o_r, g_r, o_a, gb_r, gb_a, *out_w)
        kp.append(k_a)
        vp.append(v_a)
        sp.append(s_new)
        q_r, k_r, v_r, g_r, q_a, k_a, v_a, gb_r, gb_a = _mixer_in(y_s, pos_s, w_in[l])
        o_r, s_new = _retention_chunk(q_r, k_r, v_r, state_ret[l], log_gamma)
        k_past = cache_k[l, page_table].reshape(DEC_BATCH, -1, H_ATT, HD_ATT).astype(k_a.dtype)
        v_past = cache_v[l, page_table].reshape(DEC_BATCH, -1, H_ATT, HD_ATT).astype(v_a.dtype)
        zpad = jnp.zeros((DEC_BATCH, pad_s, H_ATT, HD_ATT), k_a.dtype)
        k_all = jnp.concatenate([k_past, k_a, zpad], axis=1)
        v_all = jnp.concatenate([v_past, v_a, zpad], axis=1)
        o_a = _moba(q_a, k_all, v_all, pos_s)
        y_s = _merge(y_s, o_r, g_r, o_a, gb_r, gb_a, *out_w)
        ks.append(k_a)
        vs.append(v_a)
        ss.append(s_new)
    return (y_p, y_s, jnp.stack(kp), jnp.stack(vp), jnp.stack(sp), jnp.stack(ks), jnp.stack(vs), jnp.stack(ss))
```

```python
import contextlib
import numpy as np
import concourse.bass as bass
import concourse.mybir as mybir
from concourse.bass_utils import run_bass_kernel_spmd

F32 = mybir.dt.float32
BF16 = mybir.dt.bfloat16
I32 = mybir.dt.int32
AF = mybir.ActivationFunctionType
ALU = mybir.AluOpType
AX = mybir.AxisListType

COMPUTE = ("pe", "act", "dve", "pool")

D = 2048
SEQ = 2048
NCORE = 8
OWN = 1024
NS = 32
NTOK = OWN + NS
H = 8
DK = 256
HD = 128
DFF = 8192
PAST = 8192
PAGE = 128
NPG = 64
ALPHA = 2.0 ** 0.25
C_QR, C_KR, C_VR, C_GR = 0, 2048, 4096, 6144
C_QA, C_KA, C_VA = 8192, 9216, 10240
C_GBR, C_GBA = 11264, 13312
NEG = -30000.0


class _Rec:
    def __getattr__(self, name):
        return lambda *a, **k: (name, a, k)


_REC = _Rec()


class KB:
    def __init__(self, nc):
        self.nc = nc
        self.stack = contextlib.ExitStack()
        self.h = {"pe": nc.tensor, "act": nc.scalar, "dve": nc.vector,
                  "pool": nc.gpsimd, "sp": nc.sync}
        self.prog = {k: [] for k in self.h}
        self.sem = {}
        self.cnt = {}
        for k in COMPUTE:
            self.sem[k] = self.stack.enter_context(nc.semaphore("s_" + k))
            self.cnt[k] = 0
        self.dsem = {}
        self.waited = {}
        self.track = {}

    def sbuf(self, name, shape, dt):
        return self.stack.enter_context(self.nc.sbuf_tensor(name, list(shape), dt))

    def psum(self, name, shape, dt=F32):
        return self.stack.enter_context(self.nc.psum_tensor(name, list(shape), dt))

    def _dsem(self, key):
        if key not in self.dsem:
            self.dsem[key] = [self.stack.enter_context(self.nc.semaphore("d_" + str(key))), 0]
        return self.dsem[key]

    def _events_for(self, reads, writes):
        evs = []
        for k in list(reads) + list(writes):
            t = self.track.get(k)
            if t and t[0] is not None:
                evs.append(t[0])
        for k in writes:
            t = self.track.get(k)
            if t:
                evs.extend(t[1])
        return evs

    def _emit_waits(self, eng, evs):
        need = {}
        for ev in evs:
            kind, name, val = ev
            if kind == "eng" and name == eng and eng == "pe":
                continue
            semname = (kind, name)
            if self.waited.get((eng, semname), 0) >= val:
                continue
            if need.get(semname, 0) < val:
                need[semname] = val
        for semname, val in need.items():
            self.waited[(eng, semname)] = val
            sem = self.sem[semname[1]] if semname[0] == "eng" else self.dsem[semname[1]][0]
            self.prog[eng].append(("wait", sem, val))

    def _record(self, ev, reads, writes):
        for k in reads:
            t = self.track.setdefault(k, [None, []])
            t[1].append(ev)
        for k in writes:
            self.track[k] = [ev, []]

    def op(self, eng, fn, reads=(), writes=()):
        bank_r = [k for k in reads if isinstance(k, tuple) and k[0] == "bank"]
        if bank_r:
            reads = [k for k in reads if k not in bank_r]
            writes = list(writes) + [k for k in bank_r if k not in writes]
        evs = self._events_for(reads, writes)
        self._emit_waits(eng, evs)
        self.cnt[eng] += 1
        ev = ("eng", eng, self.cnt[eng])
        self.prog[eng].append(("op", fn(_REC), self.sem[eng]))
        self._record(ev, reads, writes)
        return ev

    def dma(self, q, out, in_, sem_key, reads=(), writes=(), **kw):
        evs = self._events_for(reads, writes)
        self._emit_waits(q, evs)
        ds = self._dsem(sem_key)
        ds[1] += 16
        ev = ("dma", sem_key, ds[1])
        self.prog[q].append(("dma", out, in_, ds[0], kw))
        self._record(ev, reads, writes)
        return ev

    def dma_fn(self, q, fn, sem_key, reads=(), writes=()):
        evs = self._events_for(reads, writes)
        self._emit_waits(q, evs)
        ds = self._dsem(sem_key)
        ds[1] += 16
        ev = ("dma", sem_key, ds[1])
        self.prog[q].append(("dmafn", fn(_REC), ds[0]))
        self._record(ev, reads, writes)
        return ev

    def barrier(self):
        self.phase_log = getattr(self, "phase_log", [])
        self.phase_log.append(dict(self.cnt))
        for eng in self.h:
            for k in COMPUTE:
                if k != eng and self.cnt[k] > self.waited.get((eng, ("eng", k)), 0):
                    self.waited[(eng, ("eng", k))] = self.cnt[k]
                    self.prog[eng].append(("wait", self.sem[k], self.cnt[k]))
            for key, (sem, cnt) in self.dsem.items():
                if cnt > self.waited.get((eng, ("dma", key)), 0):
                    self.waited[(eng, ("dma", key))] = cnt
                    self.prog[eng].append(("wait", sem, cnt))
        self.track = {}

    def finish(self, final_eng="sp"):
        for key, (sem, cnt) in self.dsem.items():
            if cnt > 0:
                self.prog[final_eng].append(("wait", sem, cnt))
        for k in COMPUTE:
            if self.cnt[k] > 0:
                self.prog[final_eng].append(("wait", self.sem[k], self.cnt[k]))

    def emit(self):
        nc, prog, h = self.nc, self.prog, self.h

        def run(name):
            e = h[name]
            for it in prog[name]:
                if it[0] == "wait":
                    e.wait_ge(it[1], it[2])
                elif it[0] == "op":
                    getattr(e, it[1][0])(*it[1][1], **it[1][2]).then_inc(it[2], 1)
                elif it[0] == "dmafn":
                    getattr(e, it[1][0])(*it[1][1], **it[1][2]).then_inc(it[2], 16)
                else:
                    e.dma_start(out=it[1], in_=it[2], **it[4]).then_inc(it[3], 16)

        with nc.Block() as block:
            @block.sync
            def _(e):
                run("sp")

            @block.tensor
            def _(e):
                run("pe")

            @block.scalar
            def _(e):
                run("act")

            @block.vector
            def _(e):
                run("dve")

            @block.gpsimd
            def _(e):
                run("pool")

    def close(self):
        self.stack.close()


class Arena:
    def __init__(self, t, nbytes):
        self.t = t
        self.n = nbytes
        self.top = 0
        self.marks = []

    def alloc(self, free_shape, dt, parts=128):
        esz = 4 if dt in (F32, I32) else 2
        n = int(np.prod(free_shape))
        nb = (n * esz + 31) // 32 * 32
        assert self.top + nb <= self.n, ("arena overflow", self.top, nb, self.n)
        off = self.top
        self.top += nb
        return self.at(off, free_shape, dt, parts)

    def at(self, off, free_shape, dt, parts=128):
        esz = 4 if dt in (F32, I32) else 2
        n = int(np.prod(free_shape))
        nb = n * esz
        assert off % 4 == 0 and off + nb <= self.n
        v = self.t[0:parts, off // 4:(off + (nb + 3) // 4 * 4) // 4]
        if dt != F32:
            v = v.bitcast(dt)
        v = v[:, 0:n]
        if len(free_shape) == 2:
            v = v.rearrange("p (a b) -> p a b", b=free_shape[1])
        elif len(free_shape) == 3:
            v = v.rearrange("p (a b c) -> p a b c", b=free_shape[1], c=free_shape[2])
        return v

    def push(self):
        self.marks.append(self.top)

    def pop(self):
        import os
        if os.environ.get("KB_LOG"):
            print("ARENA_TOP_KiB", self.top / 1024.0)
        self.top = self.marks.pop()


def bcast(ap, steps):
    a = ap.ap
    return bass.AP(ap.tensor, ap.offset, [list(a[0])] + [list(s) for s in steps])


def build_program(n_phys_rows, phases=("p1", "p3", "p2", "p6", "p4", "p5")):
    nc = bass.Bass("TRN2", target_bir_lowering=False)

    def din(name, shape, dt=F32):
        return nc.dram_tensor(name, list(shape), dt, kind="ExternalInput").ap()

    def dout(name, shape, dt=F32):
        return nc.dram_tensor(name, list(shape), dt, kind="ExternalOutput").ap()

    x_own = din("x_own", [OWN, D])
    x_prev = din("x_prev", [OWN, D])
    x_s = din("x_s", [NS, D])
    w_in = din("w_in", [D, 15360])
    w_ret = din("w_ret", [D, D])
    w_att = din("w_att", [1024, D])
    w_out = din("w_out", [D, D])
    w_up = din("w_up", [D, DFF])
    w_down = din("w_down", [DFF, D])
    cache_k = din("cache_k", [n_phys_rows, 1024])
    cache_v = din("cache_v", [n_phys_rows, 1024])
    state_s = din("state_s", [4, H, DK, DK])
    ptab = din("ptab", [1, 4 * NPG], I32)
    c_ident = din("c_ident", [128, 128])
    c_rot_own = din("c_rot_own", [8, 128, 256])
    c_rot_prev = din("c_rot_prev", [8, 128, 256])
    c_rot_s = din("c_rot_s", [NS, 256])
    c_qd = din("c_qd", [128, 8])
    c_kd = din("c_kd", [128, 8])
    c_maskT = din("c_maskT", [128, 8, 128])
    c_sqd = din("c_sqd", [NS, 8])
    c_skd = din("c_skd", [NS, 8])
    c_smaskT = din("c_smaskT", [NS, 8, NS])
    c_bmP = din("c_bmP", [NS, 4])
    c_bmF = din("c_bmF", [128, 4 * NS])
    c_gn = din("c_gn", [1, D])
    c_causal = din("c_causal", [128, 128])
    c_E = din("c_E", [32, 32 * 128])
    c_negm = din("c_negm", [128, 64])
    c_elig = din("c_elig", [128, 64])
    c_ln = din("c_ln", [4, D])
    c_sown = din("c_sown", [NS, 4 * 64])
    c_hmask = din("c_hmask", [64, 8 * 128])
    c_ident64 = din("c_i64", [64, 64])

    y_own = dout("y_own", [OWN, D])
    y_s = dout("y_s", [NS, D])
    k_own = dout("k_own", [OWN, 1024])
    v_own = dout("v_own", [OWN, 1024])
    ret_p = dout("ret_p", [H, DK, DK])
    k_s = dout("k_s", [NS, 1024])
    v_s = dout("v_s", [NS, 1024])
    ret_s = dout("ret_s", [4, H, DK, DK])

    kb = KB(nc)
    ARENA_BYTES = 204 * 1024
    arena_t = kb.sbuf("arena", [128, ARENA_BYTES // 4], F32)
    A = Arena(arena_t, ARENA_BYTES)
    banks = [kb.psum(f"bank{i}", [128, 512], F32) for i in range(8)]

    def bank_bf(i):
        return banks[i][:, 0:512].bitcast(BF16)

    ident = kb.sbuf("ident", [128, 128], BF16)[:]
    ones_bf = kb.sbuf("ones_bf", [128, 128], BF16)[:]
    kb.dma("pool", ident, c_ident, "c_ident", writes=["ident"])
    kb.op("dve", lambda e: e.memset(ones_bf, 1.0), writes=["ones"])

    KI = 1024
    xT = A.at(0 * KI, [16, NTOK], BF16)
    xTp = A.at(36 * KI, [16, NTOK], BF16)
    oaT = A.at(72 * KI, [8, NTOK], BF16)
    uT = A.at(90 * KI, [16, NTOK], BF16)

    w_in_v = w_in.rearrange("(kc p) n -> p kc n", p=128)

    class Ring:
        def __init__(self, name, slots):
            self.name, self.slots, self.i = name, slots, 0

        def load(self, src_ap, kcs, ncols):
            s = self.i % len(self.slots)
            self.i += 1
            dst = self.slots[s]
            key = (self.name, s)
            kb.dma("pool", dst[:, 0:kcs, 0:ncols], src_ap, f"{self.name}{s}", writes=[key])
            return dst, key

    def phase1():
        A.top = 126 * KI
        A.push()
        xb = [A.alloc([D], BF16) for _ in range(3)]
        srcs = [(x_own, t, 128, xT, t * 128) for t in range(8)] + \
               [(x_prev, t, 128, xTp, t * 128) for t in range(8)] + [(x_s, 0, NS, xT, OWN)]
        for i, (src, t, n, dstT, off) in enumerate(srcs):
            b = xb[i % 3]
            bk = ("xb", i % 3)
            kb.dma("pool", b[0:n, :], src[t * 128:t * 128 + n, :], f"xb{i%3}", writes=[bk], max_dma_last_dim=2048)
            for half in range(2):
                pb = i * 2 + half
                pv = bank_bf(pb % 4).rearrange("p (a b) -> p a b", b=128)
                for j in range(8):
                    kc = half * 8 + j
                    kb.op("pe", lambda e, pv=pv, j=j, kc=kc, b=b, n=n: e.transpose(
                        pv[:, j, 0:n], b[0:n, kc * 128:(kc + 1) * 128], ident[0:n, 0:n]),
                        reads=[bk, "ident"], writes=[("bank", pb % 4)])
                eng = "dve" if half == 0 else "act"
                if eng == "dve":
                    kb.op("dve", lambda e, pv=pv, dstT=dstT, half=half, off=off, n=n: e.tensor_copy(
                        dstT[:, half * 8:half * 8 + 8, off:off + n], pv[:, :, 0:n]),
                        reads=[("bank", pb % 4)], writes=[("xT", id(dstT), off, half)])
                else:
                    kb.op("act", lambda e, pv=pv, dstT=dstT, half=half, off=off, n=n: e.copy(
                        dstT[:, half * 8:half * 8 + 8, off:off + n], pv[:, :, 0:n]),
                        reads=[("bank", pb % 4)], writes=[("xT", id(dstT), off, half)])
        A.pop()
        kb.barrier()

    HG = 4

    def phase3(qaT_s, kaT_s, va_s):
        A.top = 90 * KI
        A.push()
        GC = HG * HD
        ring = Ring("w3", [A.alloc([16, GC], BF16) for _ in range(3)])
        kT = A.alloc([HG, 2048], BF16)
        vA = A.alloc([16, GC], BF16)
        qT = A.alloc([HG, OWN], BF16)
        stg = [A.alloc([GC], F32) for _ in range(2)]
        cbf = [A.alloc([GC], BF16) for _ in range(2)]
        biasT = A.alloc([OWN], BF16)
        pt_sb = [A.alloc([OWN], BF16) for _ in range(2)]
        rden = [A.alloc([128], F32) for _ in range(2)]
        ksf = A.alloc([HG, 8], F32)
        ksb = A.alloc([HG, 8], BF16)
        sm2 = A.alloc([2, HG, 8], F32)
        rank2 = A.alloc([2 * HG, 8], F32)
        bias2 = A.alloc([2, HG * 8], BF16)
        causal = A.alloc([128], BF16)
        Eoh = A.alloc([32 * 128], BF16)
        negm = A.alloc([8, 8], F32)
        elig = A.alloc([8, 8], F32)
        import os
        DBG = os.environ.get("P3_DBG", "")
        if "noconst" not in DBG:
          kb.dma("pool", causal, c_causal, "c_causal", writes=["causal"])
        if "noconst" not in DBG:
          kb.dma("pool", Eoh[0:32, :], c_E, "c_E", writes=["Eoh"], max_dma_last_dim=4096)
        kb.dma("sp", negm, c_negm.rearrange("p (a b) -> p a b", b=8), "c_negm", writes=["negm"])
        kb.dma("sp", elig, c_elig.rearrange("p (a b) -> p a b", b=8), "c_elig", writes=["elig"])
        scale = HD ** -0.5
        tiles_all = [("prev", t, 128, xTp, t * 128) for t in range(8)] + \
                    [("own", t, 128, xT, t * 128) for t in range(8)] + [("s", 0, NS, xT, OWN)]
        ectr = [0]
        pend3 = []

        for hp in range(H // HG):
            cg0 = hp * GC
            for which, cbase in (("k", C_KA), ("v", C_VA), ("q", C_QA)):
                W, wkey = ring.load(w_in_v[:, :, cbase + cg0:cbase + cg0 + GC], 16, GC)
                for ti, (kind, t, n, xsrc, off) in enumerate(tiles_all):
                    if which == "q" and kind == "prev":
                        continue
                    if "nos" in DBG and kind == "s":
                        continue
                    ectr[0] += 1
                    pb = ectr[0] % 2
                    ps = banks[pb]
                    for kc in range(16):
                        kb.op("pe", lambda e, ps=ps, xsrc=xsrc, kc=kc, off=off, n=n, W=W: e.matmul(
                            ps[0:n, :], xsrc[:, kc, off:off + n], W[:, kc, :], start=(kc == 0), stop=(kc == 15)),
                            reads=[("xT", id(xsrc), off, kc // 8), wkey], writes=[("bank", pb)])
                    while pend3:
                        pend3.pop(0)()
                    sl = ectr[0] % 2
                    if which in ("k", "v") and kind != "prev" and "nostg" not in DBG:
                        kb.op("act", lambda e, ps=ps, n=n, sl=sl: e.copy(stg[sl][0:n, :], ps[0:n, :]),
                              reads=[("bank", pb)], writes=[("stg", sl)])
                        dst = {("k", "own"): k_own, ("v", "own"): v_own, ("k", "s"): k_s, ("v", "s"): v_s}[(which, kind)]
                        if "nodmaout" not in DBG:
                            kb.dma(os.environ.get("STQ", "sp"), dst[t * 128:t * 128 + n, cg0:cg0 + GC], stg[sl][0:n, :], f"stg{sl}",
                                   reads=[("stg", sl)])
                    if which == "v":
                        if kind == "s":
                            kb.op("dve", lambda e, ps=ps: e.tensor_copy(va_s[0:NS, cg0:cg0 + GC], ps[0:NS, :]),
                                  reads=[("bank", pb)], writes=[("va_s", hp)])
                        else:
                            kb.op("dve", lambda e, ps=ps, ti=ti: e.tensor_copy(vA[:, ti, :], ps[:, :]),
                                  reads=[("bank", pb)], writes=[("vA", ti)])
                        continue
                    if "notr" in DBG:
                        continue
                    kb.op("dve", lambda e, ps=ps, n=n, sl=sl: e.tensor_copy(cbf[sl][0:n, :], ps[0:n, :]),
                          reads=[("bank", pb)], writes=[("cbf", sl)])
                    def fin3(ec=ectr[0], n=n, sl=sl, kind=kind, which=which, ti=ti, t=t, hp=hp):
                        tb = 2 + (ec % 2)
                        pv = bank_bf(tb).rearrange("p (a b) -> p a b", b=128)
                        for j in range(HG):
                            kb.op("pe", lambda e: e.transpose(
                                pv[:, j, 0:n], cbf[sl][0:n, j * 128:(j + 1) * 128], ident[0:n, 0:n]),
                                reads=[("cbf", sl), "ident"], writes=[("bank", tb)])
                        if kind == "s":
                            dstv = (kaT_s if which == "k" else qaT_s)[:, hp * HG:(hp + 1) * HG, :]
                            kb.op("act", lambda e: e.copy(dstv, pv[:, 0:HG, 0:NS]),
                                  reads=[("bank", tb)], writes=[(which + "aT_s", hp)])
                        elif which == "k":
                            kb.op("act", lambda e: e.copy(kT[:, :, ti * 128:(ti + 1) * 128], pv[:, 0:HG, :]),
                                  reads=[("bank", tb)], writes=[("kT", ti)])
                        else:
                            kb.op("act", lambda e: e.copy(qT[:, :, t * 128:(t + 1) * 128], pv[:, 0:HG, :]),
                                  reads=[("bank", tb)], writes=[("qT", t)])
                    pend3.append(fin3)
            while pend3:
                pend3.pop(0)()
            import os
            if os.environ.get("P3_STOP") == "proj":
                continue
            kb.op("dve", lambda e: e.tensor_reduce(ksf, kT.rearrange("p h (n k) -> p h n k", k=256), AX.X, ALU.add),
                  reads=[("kT", ti) for ti in range(16)], writes=["ksf"])
            kb.op("dve", lambda e: e.tensor_copy(ksb, ksf), reads=["ksf"], writes=["ksb"])
            ps = banks[4]
            for t in range(8):
                for hh in range(HG):
                    kb.op("pe", lambda e: e.matmul(
                        ps[:, (t * HG + hh) * 8:(t * HG + hh + 1) * 8], qT[:, hh, t * 128:(t + 1) * 128], ksb[:, hh, :],
                        start=True, stop=True), reads=[("qT", t), "ksb"], writes=[("bank", 4)])
            cmp2 = stg[0].rearrange("p (a b c) -> p a b c", b=8, c=8)
            for c in range(4):
                psv = ps[:, c * 64:(c + 1) * 64].rearrange("p (t h n) -> p t h n", h=HG, n=8)
                kb.op("dve", lambda e: e.tensor_tensor(
                    sm2, psv, bcast(negm[:, 2 * c, :], [[8, 2], [0, HG], [1, 8]]), ALU.add),
                    reads=[("bank", 4), "negm"], writes=["sm"])
                kb.op("dve", lambda e: e.tensor_tensor(
                    cmp2, bcast(sm2, [[8, 2 * HG], [0, 8], [1, 8]]), bcast(sm2, [[8, 2 * HG], [1, 8], [0, 8]]), ALU.is_gt),
                    reads=["sm"], writes=[("stg", 0)])
                kb.op("dve", lambda e: e.tensor_reduce(rank2, cmp2, AX.X, ALU.add), reads=[("stg", 0)], writes=["rank"])
                kb.op("dve", lambda e: e.tensor_scalar(rank2, rank2, 3.0, None, ALU.is_lt), reads=["rank"], writes=["rank"])
                r3 = rank2.rearrange("p (t h) n -> p t h n", h=HG)
                kb.op("dve", lambda e: e.tensor_tensor(
                    r3, r3, bcast(elig[:, 2 * c, :], [[8, 2], [0, HG], [1, 8]]), ALU.mult), reads=["rank", "elig"], writes=["rank"])
                kb.op("dve", lambda e: e.tensor_scalar(
                    bias2.rearrange("p t (h n) -> p (t h) n", n=8), rank2, -1.0, -NEG, ALU.add, ALU.mult),
                    reads=["rank"], writes=["bias_bf"])
                pvb = bank_bf(5)
                for tt in range(2):
                    kb.op("pe", lambda e: e.transpose(pvb[0:HG * 8, tt * 128:(tt + 1) * 128], bias2[:, tt, :], ident),
                          reads=["bias_bf", "ident"], writes=[("bank", 5)])
                kb.op("act", lambda e: e.copy(biasT[0:HG * 8, 2 * c * 128:(2 * c + 2) * 128], pvb[0:HG * 8, 0:256]),
                      reads=[("bank", 5)], writes=[("biasT", 2 * c), ("biasT", 2 * c + 1)])
            if os.environ.get("P3_STOP") == "sel":
                continue
            for hh in range(HG):
                for kt in range(16):
                    ko = kt - 8
                    lo = 0 if kt < 8 else ko * 128
                    n_blk = kt // 2
                    lbase = 2 * (kt % 2)
                    ptb = pt_sb[kt % 2]
                    for ch in range(2):
                        c0, c1 = max(lo, ch * 512), (ch + 1) * 512
                        if c0 >= c1:
                            continue
                        lb = lbase + ch
                        pl = banks[lb]
                        subs = []
                        if kt < 8:
                            subs.append(("sel", c0, c1))
                        else:
                            for tq in range(c0 // 128, c1 // 128):
                                if tq == ko:
                                    subs.append(("diag", tq * 128, (tq + 1) * 128))
                                elif n_blk < 4 + tq // 2:
                                    if subs and subs[-1][0] == "sel" and subs[-1][2] == tq * 128:
                                        subs[-1] = ("sel", subs[-1][1], (tq + 1) * 128)
                                    else:
                                        subs.append(("sel", tq * 128, (tq + 1) * 128))
                        kb.op("pe", lambda e: e.matmul(
                            pl[:, c0 - ch * 512:c1 - ch * 512], kT[:, hh, kt * 128:(kt + 1) * 128], qT[:, hh, c0:c1],
                            start=True, stop=(len(subs) == 0)),
                            reads=[("kT", kt)] + [("qT", tq) for tq in range(c0 // 128, c1 // 128)], writes=[("bank", lb)])
                        for si, (typ, a0, a1) in enumerate(subs):
                            lastb = si == len(subs) - 1
                            if typ == "sel":
                                r = hh * 8 + n_blk
                                kb.op("pe", lambda e: e.matmul(
                                    pl[:, a0 - ch * 512:a1 - ch * 512], Eoh[0:HG * 8, r * 128:(r + 1) * 128],
                                    biasT[0:HG * 8, a0:a1], start=False, stop=lastb),
                                    reads=["Eoh"] + [("biasT", tq) for tq in range(a0 // 128, a1 // 128)],
                                    writes=[("bank", lb)])
                            else:
                                kb.op("pe", lambda e: e.matmul(
                                    pl[:, a0 - ch * 512:a1 - ch * 512], ident, causal, start=False, stop=lastb),
                                    reads=["ident", "causal"], writes=[("bank", lb)])
                        kb.op("act", lambda e: e.activation(ptb[:, c0:c1], pl[:, c0 - ch * 512:c1 - ch * 512], AF.Exp, scale=scale),
                              reads=[("bank", lb)], writes=[("pt", kt % 2, ch)])
                    for ch in range(2):
                        c0, c1 = max(lo, ch * 512), (ch + 1) * 512
                        if c0 >= c1:
                            continue
                        kb.op("pe", lambda e: e.matmul(
                            banks[4 + ch][:, c0 - ch * 512:c1 - ch * 512], vA[:, kt, hh * 128:(hh + 1) * 128], ptb[:, c0:c1],
                            start=(kt == 0), stop=(kt == 15)),
                            reads=[("pt", kt % 2, ch), ("vA", kt)], writes=[("bank", 4 + ch)])
                        kb.op("pe", lambda e: e.matmul(
                            banks[6 + ch][:, c0 - ch * 512:c1 - ch * 512], ones_bf, ptb[:, c0:c1],
                            start=(kt == 0), stop=(kt == 15)),
                            reads=[("pt", kt % 2, ch), "ones"], writes=[("bank", 6 + ch)])
                for ch in range(2):
                    rd = stg[ch]
                    kb.op("dve", lambda e: e.reciprocal(rd, banks[6 + ch][:, :]), reads=[("bank", 6 + ch)], writes=[("stg", ch)])
                    kb.op("dve", lambda e: e.tensor_tensor(
                        oaT[:, hp * HG + hh, ch * 512:(ch + 1) * 512], banks[4 + ch][:, :], rd, ALU.mult),
                        reads=[("bank", 4 + ch), ("stg", ch)], writes=[("oaT", hp * HG + hh, ch)])
        A.pop()
        kb.barrier()

    LG = [float(np.log1p(-2.0 ** (-5.0 - h))) for h in range(H)]

    def phase2():
        A.top = 126 * KI
        A.push()
        ring = Ring("w2", [A.alloc([16, 256], BF16) for _ in range(4)])
        rot = [A.alloc([256], F32) for _ in range(2)]
        kf = [A.alloc([256], F32) for _ in range(2)]
        qf = [A.alloc([256], F32) for _ in range(2)]
        sgf = A.alloc([256], F32)
        ta = A.alloc([256], F32)
        tb_ = A.alloc([256], F32)
        qr = A.alloc([256], F32)
        onb = A.alloc([256], F32)
        k_bf = [A.alloc([256], BF16) for _ in range(2)]
        kp_bf = [A.alloc([256], BF16) for _ in range(2)]
        qp_bf = [A.alloc([256], BF16) for _ in range(2)]
        v_bf = [A.alloc([256], BF16) for _ in range(2)]
        sgn = [A.alloc([256], BF16) for _ in range(2)]
        u_bf = [A.alloc([256], BF16) for _ in range(2)]
        kqT = [A.alloc([4, 128], BF16) for _ in range(2)]
        PT = [A.alloc([128], BF16) for _ in range(2)]
        S_f = A.alloc([2, 256], F32)
        S_b = A.alloc([2, 256], BF16)
        qd = A.alloc([8], F32)
        kd = A.alloc([8], F32)
        maskT = [A.alloc([128], F32) for _ in range(2)]
        gn_h = [A.alloc([256], F32) for _ in range(2)]
        stats = A.alloc([6], F32)
        mv = A.alloc([2], F32)
        sd = A.alloc([2], F32)
        sS_f = [A.alloc([2, 256], F32) for _ in range(4)]
        sS_b = [A.alloc([2, 256], BF16) for _ in range(4)]
        skf = [("sS_f", i) for i in range(4)]
        skb = [("sS_b", i) for i in range(4)]
        sS_o = [A.alloc([2, 256], F32)] * 2
        qm = A.alloc([4, 2, NS], BF16)
        vm = A.alloc([4, 256], BF16)
        sqd = A.alloc([8], F32)
        skd = A.alloc([8], F32)
        smaskT = A.alloc([8, NS], F32)
        bmP = A.alloc([4], F32)
        bmF = A.alloc([4, NS], F32)
        rot_s = A.alloc([256], F32)
        kb.dma("sp", qd, c_qd, "c_qd", writes=["qd"])
        kb.dma("sp", kd, c_kd, "c_kd", writes=["kd"])
        kb.dma("sp", sqd[0:NS, :], c_sqd, "c_sqd", writes=["sqd"])
        kb.dma("sp", skd[0:NS, :], c_skd, "c_skd", writes=["skd"])
        kb.dma("sp", smaskT[0:NS], c_smaskT, "c_smaskT", writes=["smaskT"])
        kb.dma("sp", bmP[0:NS, :], c_bmP, "c_bmP", writes=["bmP"])
        kb.dma("sp", bmF, c_bmF.rearrange("p (b i) -> p b i", i=NS), "c_bmF", writes=["bmF"])
        kb.dma("sp", rot_s[0:NS, :], c_rot_s, "c_rot_s", writes=["rot_s"])

        def rotary(src, rt, dst, n, keys_r, key_w):
            cosb = bcast(rt[0:n, 0:128], [[0, 2], [1, 128]])
            sinb = bcast(rt[0:n, 128:256], [[0, 2], [1, 128]])
            s3 = src[0:n, :].rearrange("p (a b) -> p a b", b=128)
            kb.op("dve", lambda e: e.tensor_tensor(ta[0:n, :].rearrange("p (a b) -> p a b", b=128), s3, cosb, ALU.mult),
                  reads=keys_r, writes=["ta"])
            kb.op("dve", lambda e: e.tensor_tensor(tb_[0:n, :].rearrange("p (a b) -> p a b", b=128), s3, sinb, ALU.mult),
                  reads=keys_r, writes=["tb"])
            kb.op("dve", lambda e: e.tensor_tensor(dst[0:n, 0:128], ta[0:n, 0:128], tb_[0:n, 128:256], ALU.subtract),
                  reads=["ta", "tb"], writes=[key_w])
            kb.op("dve", lambda e: e.tensor_tensor(dst[0:n, 128:256], tb_[0:n, 0:128], ta[0:n, 128:256], ALU.add),
                  reads=["ta", "tb"], writes=[key_w])

        def group_norm_gate(po_ap, n, sg_ap, sgkey, u_ap, ukey, bank_key):
            kb.op("dve", lambda e: e.bn_stats(stats[0:n, :], po_ap), reads=[bank_key], writes=["gstats"])
            kb.op("dve", lambda e: e.bn_aggr(mv[0:n, :], stats[0:n, :]), reads=["gstats"], writes=["gmv"])
            kb.op("act", lambda e: e.activation(sd[0:n, 0:1], mv[0:n, 1:2], AF.Sqrt, bias=1e-5, scale=1.0),
                  reads=["gmv"], writes=["gsd"])
            kb.op("dve", lambda e: e.reciprocal(sd[0:n, 1:2], sd[0:n, 0:1]), reads=["gsd"], writes=["grs"])
            kb.op("dve", lambda e: e.tensor_scalar(onb[0:n, :], po_ap, mv[0:n, 0:1], sd[0:n, 1:2], ALU.subtract, ALU.mult),
                  reads=[bank_key, "gmv", "grs"], writes=["onb"])
            kb.op("dve", lambda e: e.tensor_tensor(u_ap, onb[0:n, :], sg_ap, ALU.mult),
                  reads=["onb", sgkey], writes=[ukey])

        it = 0
        for h in range(H):
            Wk, wkk = ring.load(w_in_v[:, :, C_KR + h * 256:C_KR + (h + 1) * 256], 16, 256)
            Wv, wvk = ring.load(w_in_v[:, :, C_VR + h * 256:C_VR + (h + 1) * 256], 16, 256)
            Wq, wqk = ring.load(w_in_v[:, :, C_QR + h * 256:C_QR + (h + 1) * 256], 16, 256)
            Wg, wgk = ring.load(w_in_v[:, :, C_GR + h * 256:C_GR + (h + 1) * 256], 16, 256)
            mT = maskT[h % 2]
            gnh = gn_h[h % 2]
            kb.dma("sp", mT, c_maskT[:, h, :], f"c_mT{h%2}", writes=[("mT", h % 2)])
            kb.dma("sp", gnh, c_gn[0:1, h * 256:(h + 1) * 256].partition_broadcast(128), f"c_gnh{h%2}",
                   writes=[("gnh", h % 2)])
            g128 = float(np.exp(LG[h] * 128.0))
            g8 = float(np.exp(LG[h] * 8.0))
            kb.op("dve", lambda e: e.memset(S_f, 0.0), writes=["S_f"])
            kb.op("act", lambda e: e.copy(S_b, S_f), reads=["S_f"], writes=["S_b"])
            for b in range(4):
                kb.dma("sp", sS_f[b], state_s[b, h].rearrange("(c p) e -> p c e", p=128), f"sSf{b}", writes=[skf[b]])
                kb.dma("pool", sS_b[b], state_s[b, h].rearrange("(c p) e -> p c e", p=128), f"sSb{b}", writes=[skb[b]])
            tiles = [("prev", t, 128, xTp, t * 128, c_rot_prev) for t in range(8)] + \
                    [("own", t, 128, xT, t * 128, c_rot_own) for t in range(8)]

            class Tile:
                pass

            def make_tile(kind, t, xsrc, off, rsrc, sl):
                T = Tile()
                own = kind == "own"
                rt = rot[sl]
                bA, bB = banks[sl], banks[2 + sl]
                pv4 = bank_bf(4).rearrange("p (a b) -> p a b", b=128)
                kq = kqT[sl]
                pt = PT[sl]
                ub = u_bf[sl]

                def proj(W, wk, dst, bk):
                    for kc in range(16):
                        kb.op("pe", lambda e: e.matmul(dst, xsrc[:, kc, off:off + 128], W[:, kc, :],
                                                       start=(kc == 0), stop=(kc == 15)), reads=[wk], writes=[bk])

                def A_kv():
                    kb.dma("sp", rt, rsrc[t], f"rot{sl}", writes=[("rot", sl)])
                    proj(Wk, wkk, bA[:, 0:256], ("bank", sl))
                    proj(Wv, wvk, bA[:, 256:512], ("bank", sl))
                    kb.op("act", lambda e: e.copy(kf[sl], bA[:, 0:256]), reads=[("bank", sl)], writes=[("kf", sl)])
                    kb.op("act", lambda e: e.copy(v_bf[sl], bA[:, 256:512]), reads=[("bank", sl)], writes=[("v_bf", sl)])
                    rotary(kf[sl], rt, k_bf[sl], 128, [("kf", sl), ("rot", sl)], ("k_bf", sl))
                    kb.op("dve", lambda e: e.tensor_scalar(kp_bf[sl], k_bf[sl], kd[:, h:h + 1], None, ALU.mult),
                          reads=[("k_bf", sl), "kd"], writes=[("kp_bf", sl)])

                def A_q():
                    if own:
                        proj(Wq, wqk, bB[:, 0:256], ("bank", 2 + sl))

                def A_g():
                    if not own:
                        return
                    proj(Wg, wgk, bB[:, 256:512], ("bank", 2 + sl))
                    kb.op("act", lambda e: e.copy(qf[sl], bB[:, 0:256]), reads=[("bank", 2 + sl)], writes=[("qf", sl)])
                    kb.op("act", lambda e: e.activation(sgf, bB[:, 256:512], AF.Silu), reads=[("bank", 2 + sl)], writes=["sgf"])
                    rotary(qf[sl], rt, qr, 128, [("qf", sl), ("rot", sl)], "qr")
                    kb.op("dve", lambda e: e.tensor_scalar(qp_bf[sl], qr, qd[:, h:h + 1], None, ALU.mult),
                          reads=["qr", "qd"], writes=[("qp_bf", sl)])
                    kb.op("dve", lambda e: e.tensor_tensor(sgn[sl], sgf, gnh, ALU.mult),
                          reads=["sgf", ("gnh", h % 2)], writes=[("sgn", sl)])

                def B1():
                    if not own:
                        return
                    for c in range(2):
                        kb.op("pe", lambda e: e.transpose(pv4[:, c, :], k_bf[sl][:, c * 128:(c + 1) * 128], ident),
                              reads=[("k_bf", sl), "ident"], writes=[("bank", 4)])
                    for c in range(2):
                        kb.op("pe", lambda e: e.transpose(pv4[:, 2 + c, :], qp_bf[sl][:, c * 128:(c + 1) * 128], ident),
                              reads=[("qp_bf", sl), "ident"], writes=[("bank", 4)])
                    kb.op("act", lambda e: e.copy(kq, pv4[:, 0:4, :]), reads=[("bank", 4)], writes=[("kqT", sl)])

                def B2():
                    if not own:
                        return
                    for c in range(2):
                        kb.op("pe", lambda e: e.matmul(banks[6][:, 0:128], kq[:, c, :], kq[:, 2 + c, :],
                                                       start=(c == 0), stop=(c == 1)),
                              reads=[("kqT", sl)], writes=[("bank", 6)])
                    kb.op("dve", lambda e: e.tensor_tensor(pt, banks[6][:, 0:128], mT, ALU.mult),
                          reads=[("bank", 6), ("mT", h % 2)], writes=[("PT", sl)])

                def B3():
                    for c in range(2):
                        kb.op("pe", lambda e: e.matmul(banks[5][:, c * 256:(c + 1) * 256],
                                                       kp_bf[sl][:, c * 128:(c + 1) * 128], v_bf[sl], start=True, stop=True),
                              reads=[("kp_bf", sl), ("v_bf", sl)], writes=[("bank", 5)])
                    if own:
                        kb.op("pe", lambda e: e.matmul(banks[7][:, 0:256], pt, v_bf[sl], start=True, stop=False),
                              reads=[("PT", sl), ("v_bf", sl)], writes=[("bank", 7)])
                        for c in range(2):
                            kb.op("pe", lambda e: e.matmul(banks[7][:, 0:256], kq[:, 2 + c, :], S_b[:, c, :],
                                                           start=False, stop=(c == 1)),
                                  reads=[("kqT", sl), "S_b"], writes=[("bank", 7)])
                    kb.op("dve", lambda e: e.scalar_tensor_tensor(
                        S_f.rearrange("p a b -> p (a b)"), S_f.rearrange("p a b -> p (a b)"), g128, banks[5][:, :],
                        ALU.mult, ALU.add), reads=[("bank", 5), "S_f"], writes=["S_f"])
                    kb.op("act", lambda e: e.copy(S_b, S_f), reads=["S_f"], writes=["S_b"])
                    if own:
                        group_norm_gate(banks[7][:, 0:256], 128, sgn[sl], ("sgn", sl), ub, ("u_bf", sl), ("bank", 7))

                def B4():
                    if not own:
                        return
                    for c in range(2):
                        kb.op("pe", lambda e: e.transpose(pv4[:, 4 + c, :], ub[:, c * 128:(c + 1) * 128], ident),
                              reads=[("u_bf", sl), "ident"], writes=[("bank", 4)])
                    kb.op("act", lambda e: e.copy(uT[:, h * 2:h * 2 + 2, t * 128:(t + 1) * 128], pv4[:, 4:6, :]),
                          reads=[("bank", 4)], writes=[("uT", t, h)])

                T.A_kv, T.A_q, T.A_g, T.B1, T.B2, T.B3, T.B4 = A_kv, A_q, A_g, B1, B2, B3, B4
                return T

            n = NS
            s_kf, s_qf = qf[1], qf[0]
            s_vbf, s_kbf, s_kpbf, s_qpbf, s_sgn, s_ubf = sgn[1], qp_bf[1], u_bf[1], qp_bf[0], sgn[0], u_bf[0]
            K_kf, K_qf, K_vbf, K_kbf, K_kpbf, K_qpbf, K_sgn, K_ubf = ("qf", 1), ("qf", 0), ("sgn", 1), ("qp_bf", 1), \
                ("u_bf", 1), ("qp_bf", 0), ("sgn", 0), ("u_bf", 0)
            pv4s = bank_bf(4).rearrange("p (a b) -> p a b", b=128)
            kqs = kqT[0]

            def S0():
                bA, bB = banks[2], banks[3]
                for (W, wk, dst, bk) in ((Wk, wkk, bA[0:n, 0:256], ("bank", 2)), (Wv, wvk, bA[0:n, 256:512], ("bank", 2)),
                                         (Wq, wqk, bB[0:n, 0:256], ("bank", 3)), (Wg, wgk, bB[0:n, 256:512], ("bank", 3))):
                    for kc in range(16):
                        kb.op("pe", lambda e: e.matmul(dst, xT[:, kc, OWN:OWN + NS], W[:, kc, :], start=(kc == 0), stop=(kc == 15)),
                              reads=[wk], writes=[bk])
                kb.op("act", lambda e: e.copy(s_kf[0:n, :], bA[0:n, 0:256]), reads=[("bank", 2)], writes=[K_kf])
                kb.op("act", lambda e: e.copy(s_vbf[0:n, :], bA[0:n, 256:512]), reads=[("bank", 2)], writes=[K_vbf])
                kb.op("act", lambda e: e.copy(s_qf[0:n, :], bB[0:n, 0:256]), reads=[("bank", 3)], writes=[K_qf])
                kb.op("act", lambda e: e.activation(sgf[0:n, :], bB[0:n, 256:512], AF.Silu), reads=[("bank", 3)], writes=["sgf"])
                rotary(s_kf, rot_s, s_kbf, n, [K_kf, "rot_s"], K_kbf)
                kb.op("dve", lambda e: e.tensor_scalar(s_kpbf[0:n, :], s_kbf[0:n, :], skd[0:n, h:h + 1], None, ALU.mult),
                      reads=[K_kbf, "skd"], writes=[K_kpbf])
                rotary(s_qf, rot_s, qr, n, [K_qf, "rot_s"], "qr")
                kb.op("dve", lambda e: e.tensor_scalar(s_qpbf[0:n, :], qr[0:n, :], sqd[0:n, h:h + 1], None, ALU.mult),
                      reads=["qr", "sqd"], writes=[K_qpbf])
                kb.op("dve", lambda e: e.tensor_tensor(s_sgn[0:n, :], sgf[0:n, :], gnh[0:n, :], ALU.mult),
                      reads=["sgf", ("gnh", h % 2)], writes=[K_sgn])

            def S1():
                for c in range(2):
                    kb.op("pe", lambda e: e.transpose(pv4s[:, c, 0:n], s_kbf[0:n, c * 128:(c + 1) * 128], ident[0:n, 0:n]),
                          reads=[K_kbf, "ident"], writes=[("bank", 4)])
                for c in range(2):
                    kb.op("pe", lambda e: e.transpose(pv4s[:, 2 + c, 0:n], s_qpbf[0:n, c * 128:(c + 1) * 128], ident[0:n, 0:n]),
                          reads=[K_qpbf, "ident"], writes=[("bank", 4)])
                kb.op("act", lambda e: e.copy(kqs[:, :, 0:n], pv4s[:, 0:4, 0:n]), reads=[("bank", 4)], writes=[("kqT", 0)])
                kb.op("dve", lambda e: e.tensor_tensor(
                    qm, bcast(kqs[:, 2:4, 0:n], [[0, 4], [128, 2], [1, n]]), bcast(bmF, [[n, 4], [0, 2], [1, n]]), ALU.mult),
                    reads=[("kqT", 0), "bmF"], writes=["qm"])
                kb.op("dve", lambda e: e.tensor_tensor(
                    vm[0:n], bcast(s_vbf[0:n, :], [[0, 4], [1, 256]]), bcast(bmP[0:n, :], [[1, 4], [0, 256]]), ALU.mult),
                    reads=[K_vbf, "bmP"], writes=["vm"])

            def S2():
                for c in range(2):
                    kb.op("pe", lambda e: e.matmul(banks[6][0:n, 0:n], kqs[:, c, 0:n], kqs[:, 2 + c, 0:n],
                                                   start=(c == 0), stop=(c == 1)),
                          reads=[("kqT", 0)], writes=[("bank", 6)])
                kb.op("dve", lambda e: e.tensor_tensor(PT[0][0:n, 0:n], banks[6][0:n, 0:n], smaskT[0:n, h, :], ALU.mult),
                      reads=[("bank", 6), "smaskT"], writes=[("PT", 0)])

            def S3():
                kb.op("pe", lambda e: e.matmul(banks[7][0:n, 0:256], PT[0][0:n, 0:n], s_vbf[0:n, :], start=True, stop=False),
                      reads=[("PT", 0), K_vbf], writes=[("bank", 7)])
                for b in range(4):
                    for c in range(2):
                        kb.op("pe", lambda e: e.matmul(
                            banks[7][0:n, 0:256], qm[:, b, c, :], sS_b[b][:, c, :], start=False, stop=(b == 3 and c == 1)),
                            reads=["qm", skb[b]], writes=[("bank", 7)])

            def S4(bs):
                for b in bs:
                    for c in range(2):
                        kb.op("pe", lambda e: e.matmul(
                            banks[5][:, c * 256:(c + 1) * 256], s_kpbf[0:n, c * 128:(c + 1) * 128], vm[0:n, b, :],
                            start=True, stop=True),
                            reads=[K_kpbf, "vm"], writes=[("bank", 5)])
                    kb.op("dve", lambda e: e.scalar_tensor_tensor(
                        sS_o[0].rearrange("p a b -> p (a b)"), sS_f[b].rearrange("p a b -> p (a b)"), g8, banks[5][:, :],
                        ALU.mult, ALU.add),
                        reads=[("bank", 5), skf[b]], writes=[("sS_o", 0)])
                    kb.dma("pool", ret_s[b, h].rearrange("(c p) e -> p c e", p=128), sS_o[0], "sSo0", reads=[("sS_o", 0)])

            def S5():
                group_norm_gate(banks[7][0:n, 0:256], n, s_sgn[0:n, :], K_sgn, s_ubf[0:n, :], K_ubf, ("bank", 7))

            def S6():
                for c in range(2):
                    kb.op("pe", lambda e: e.transpose(pv4s[:, 4 + c, 0:n], s_ubf[0:n, c * 128:(c + 1) * 128], ident[0:n, 0:n]),
                          reads=[K_ubf, "ident"], writes=[("bank", 4)])
                kb.op("act", lambda e: e.copy(uT[:, h * 2:h * 2 + 2, OWN:OWN + NS], pv4s[:, 4:6, 0:n]),
                      reads=[("bank", 4)], writes=[("uT", 8, h)])

            SST = [S0, S1, S2, S3, lambda: S4([0, 1]), lambda: S4([2, 3]), lambda: (S5(), S6())]
            tl = []
            for (kind, t, n_, xsrc, off, rsrc) in tiles:
                it += 1
                tl.append(make_tile(kind, t, xsrc, off, rsrc, it % 2))
            tl[0].A_kv(); tl[0].A_q(); tl[0].A_g()
            for i in range(16):
                nxt = tl[i + 1] if i + 1 < 16 else None
                if nxt:
                    nxt.A_kv()
                tl[i].B1()
                if nxt:
                    nxt.A_q()
                tl[i].B2()
                if nxt:
                    nxt.A_g()
                if i >= 1:
                    tl[i - 1].B4()
                tl[i].B3()
                if i < len(SST):
                    SST[i]()
            tl[15].B4()
            kb.dma("pool", ret_p[h].rearrange("(c p) e -> p c e", p=128), S_f, "retp", reads=["S_f"])

        A.pop()
        kb.barrier()

    def phase6(qaT_s, kaT_s, va_s):
        A.top = 126 * KI
        A.push()
        scale = HD ** -0.5
        NPB = 6
        Kpg = [A.alloc([1024], BF16) for _ in range(NPB)]
        Vpg = [A.alloc([1024], BF16) for _ in range(NPB)]
        KT = [A.alloc([8, 128], BF16) for _ in range(2)]
        PTb = [A.alloc([NPG, 64], BF16) for _ in range(2)]
        pti = A.alloc([4 * NPG], I32)
        ptf = A.alloc([4 * NPG], F32)
        iot = A.alloc([1], F32)
        idx = A.alloc([4 * NPG], I32)
        ksm = A.alloc([8, 32], F32)
        kmb = A.alloc([8, 32], BF16)
        Qm = A.alloc([8, 64], BF16)
        sm = A.alloc([32], F32)
        cmpb = A.alloc([32, 32], F32)
        rank = A.alloc([32], F32)
        Dm = A.alloc([32, 64], BF16)
        selrep = A.alloc([32, 64], BF16)
        hmask = A.alloc([1024], BF16)
        i64 = A.alloc([64], F32)
        sown = A.alloc([4, 64], F32)
        pown = A.alloc([64], F32)
        PTo = A.alloc([64], BF16)
        tmp = A.alloc([1024], F32)
        osel = A.alloc([128], F32)
        rdn = A.alloc([1], F32)
        o_bf = A.alloc([128], BF16)
        kb.dma("sp", pti, ptab.partition_broadcast(128), "c_pt", writes=["pti"])
        kb.dma("pool", hmask[0:64, :], c_hmask, "c_hmask", writes=["hmask"], max_dma_last_dim=4096)
        kb.dma("sp", i64[0:64, :], c_ident64, "c_i64", writes=["i64"])
        kb.dma("sp", sown[0:NS], c_sown.rearrange("p (b c) -> p b c", c=64), "c_sown", writes=["sown"])
        kb.op("pool", lambda e: e.iota(iot, [[0, 1]], base=0, channel_multiplier=1, allow_small_or_imprecise_dtypes=True),
              writes=["iot"])
        kb.op("dve", lambda e: e.tensor_copy(ptf, pti), reads=["pti"], writes=["ptf"])
        kb.op("dve", lambda e: e.tensor_scalar(idx, ptf, 128.0, iot[:, 0:1], ALU.mult, ALU.add),
              reads=["ptf", "iot"], writes=["idx"])
        kb.op("dve", lambda e: e.memset(Qm, 0.0), writes=["Qm"])
        ig = 0
        for b in range(4):
            PT = PTb[b % 2]
            ptk = ("PTb", b % 2)
            kb.op("dve", lambda e: e.tensor_copy(
                bcast(Qm, [[64 + 8, 8], [1, 8]]), qaT_s[:, :, b * 8:(b + 1) * 8]),
                reads=[("qaT_s", 0), ("qaT_s", 1)], writes=["Qm"])
            pinfo = {}

            def T1(pg):
                nonlocal ig
                ig += 1
                ks = ig % NPB
                kp = Kpg[ks]
                j = b * NPG + pg
                kb.dma_fn("pool", lambda e: e.indirect_dma_start(
                    out=kp, out_offset=None, in_=cache_k,
                    in_offset=bass.IndirectOffsetOnAxis(ap=idx[:, j:j + 1], axis=0)),
                    f"kpg{ks}", reads=["idx"], writes=[("Kpg", ks)])
                tbk = ig % 2
                pvT = bank_bf(tbk).rearrange("p (a b) -> p a b", b=128)
                for h in range(8):
                    kb.op("pe", lambda e: e.transpose(pvT[:, h, :], kp[:, h * 128:(h + 1) * 128], ident),
                          reads=[("Kpg", ks), "ident"], writes=[("bank", tbk)])
                kt = KT[ig % 2]
                if ig % 2 == 0:
                    kb.op("act", lambda e: e.copy(kt, pvT), reads=[("bank", tbk)], writes=[("KT", ig % 2)])
                else:
                    kb.op("dve", lambda e: e.tensor_copy(kt, pvT), reads=[("bank", tbk)], writes=[("KT", ig % 2)])
                pinfo[pg] = (ks, kp, kt, ig % 2)

            def L1(pg):
                ks, kp, kt, par = pinfo[pg]
                for h in range(8):
                    kb.op("pe", lambda e: e.matmul(banks[2][:, h * 64 + pg:h * 64 + pg + 1],
                                                   kp[:, h * 128:(h + 1) * 128], ones_bf[:, 0:1], start=True, stop=True),
                          reads=[("Kpg", ks), "ones"], writes=[("bank", 2)])
                for h in range(8):
                    kb.op("pe", lambda e: e.matmul(banks[3][:, h * 8:(h + 1) * 8], kt[:, h, :],
                                                   qaT_s[:, h, b * 8:(b + 1) * 8], start=True, stop=True),
                          reads=[("KT", par)], writes=[("bank", 3)])
                kb.op("act", lambda e: e.activation(PT[:, pg, :], banks[3][:, 0:64], AF.Exp, scale=scale),
                      reads=[("bank", 3)], writes=[ptk])

            T1(0)
            for pg in range(NPG):
                if pg + 1 < NPG:
                    T1(pg + 1)
                L1(pg)
            kb.op("dve", lambda e: e.tensor_reduce(
                ksm, banks[2][:, :].rearrange("p (h n t) -> p h n t", n=32, t=2), AX.X, ALU.add),
                reads=[("bank", 2)], writes=["ksm"])
            kb.op("dve", lambda e: e.tensor_copy(kmb, ksm), reads=["ksm"], writes=["kmb"])
            for h in range(8):
                kb.op("pe", lambda e, h=h: e.matmul(banks[7][0:64, 0:32], Qm[:, h, :], kmb[:, h, :],
                                                    start=(h == 0), stop=(h == 7)),
                      reads=["Qm", "kmb"], writes=[("bank", 7)])
            kb.op("dve", lambda e: e.tensor_copy(sm[0:64, :], banks[7][0:64, 0:32]), reads=[("bank", 7)], writes=["sm6"])
            kb.op("dve", lambda e: e.tensor_tensor(
                cmpb[0:64], bcast(sm[0:64, :], [[0, 32], [1, 32]]), bcast(sm[0:64, :], [[1, 32], [0, 32]]), ALU.is_gt),
                reads=["sm6"], writes=["cmp6"])
            kb.op("dve", lambda e: e.tensor_reduce(rank[0:64, :], cmpb[0:64], AX.X, ALU.add), reads=["cmp6"], writes=["rank6"])
            kb.op("dve", lambda e: e.tensor_scalar(rank[0:64, :], rank[0:64, :], 3.0, None, ALU.is_lt),
                  reads=["rank6"], writes=["rank6"])
            kb.op("dve", lambda e: e.tensor_tensor(
                Dm[0:64], bcast(rank[0:64, :], [[1, 32], [0, 64]]), bcast(i64[0:64, :], [[0, 32], [1, 64]]), ALU.mult),
                reads=["rank6", "i64"], writes=["Dm"])
            Dflat = Dm[0:64].rearrange("p a b -> p (a b)")
            for c4 in range(4):
                kb.op("pe", lambda e, c4=c4: e.matmul(banks[4 + c4][:, :], ones_bf[0:64, :], Dflat[:, c4 * 512:(c4 + 1) * 512],
                                                      start=True, stop=True),
                      reads=["Dm", "ones"], writes=[("bank", 4 + c4)])
                kb.op("act" if c4 % 2 else "dve", (lambda e, c4=c4: e.copy(
                    selrep.rearrange("p a b -> p (a b)")[:, c4 * 512:(c4 + 1) * 512], banks[4 + c4][:, :])) if c4 % 2 else
                    (lambda e, c4=c4: e.tensor_copy(
                        selrep.rearrange("p a b -> p (a b)")[:, c4 * 512:(c4 + 1) * 512], banks[4 + c4][:, :])),
                    reads=[("bank", 4 + c4)], writes=[("selrep", c4)])
            kb.op("dve", lambda e: e.tensor_tensor(
                PT.rearrange("p (n t) c -> p n t c", t=2), PT.rearrange("p (n t) c -> p n t c", t=2),
                bcast(selrep, [[64, 32], [0, 2], [1, 64]]), ALU.mult),
                reads=[ptk] + [("selrep", c4) for c4 in range(4)], writes=[ptk])
            for h in range(8):
                kb.op("pe", lambda e, h=h: e.matmul(banks[7][0:NS, 64 + h * 8:64 + (h + 1) * 8], kaT_s[:, h, :],
                                                    qaT_s[:, h, b * 8:(b + 1) * 8], start=True, stop=True),
                      reads=[("kaT_s", 0), ("kaT_s", 1)], writes=[("bank", 7)])
            kb.op("act", lambda e: e.activation(pown[0:NS, :], banks[7][0:NS, 64:128], AF.Exp, scale=scale),
                  reads=[("bank", 7)], writes=["pown"])
            kb.op("dve", lambda e: e.tensor_tensor(PTo[0:NS, :], pown[0:NS, :], sown[0:NS, b, :], ALU.mult),
                  reads=["pown", "sown"], writes=["PTo"])
            for pg in range(NPG):
                ig += 1
                vs = ig % NPB
                vp = Vpg[vs]
                j = b * NPG + pg
                kb.dma_fn("pool", lambda e, vp=vp, j=j: e.indirect_dma_start(
                    out=vp, out_offset=None, in_=cache_v,
                    in_offset=bass.IndirectOffsetOnAxis(ap=idx[:, j:j + 1], axis=0)),
                    f"vpg{vs}", reads=["idx"], writes=[("Vpg", vs)])
                for c2 in range(2):
                    kb.op("pe", lambda e, c2=c2: e.matmul(banks[4 + c2][0:64, :], PT[:, pg, :], vp[:, c2 * 512:(c2 + 1) * 512],
                                                          start=(pg == 0), stop=False),
                          reads=[ptk, ("Vpg", vs)], writes=[("bank", 4 + c2)])
                kb.op("pe", lambda e: e.matmul(banks[6][0:64, 0:1], PT[:, pg, :], ones_bf[:, 0:1], start=(pg == 0), stop=False),
                      reads=[ptk, "ones"], writes=[("bank", 6)])
            for c2 in range(2):
                kb.op("pe", lambda e, c2=c2: e.matmul(banks[4 + c2][0:64, :], PTo[0:NS, :], va_s[0:NS, c2 * 512:(c2 + 1) * 512],
                                                      start=False, stop=True),
                      reads=["PTo", ("va_s", 0), ("va_s", 1)], writes=[("bank", 4 + c2)])
            kb.op("pe", lambda e: e.matmul(banks[6][0:64, 0:1], PTo[0:NS, :], ones_bf[0:NS, 0:1], start=False, stop=True),
                  reads=["PTo", "ones"], writes=[("bank", 6)])
            for c2 in range(2):
                kb.op("dve", lambda e, c2=c2: e.tensor_tensor(tmp[0:64, c2 * 512:(c2 + 1) * 512], banks[4 + c2][0:64, :],
                                                              hmask[0:64, c2 * 512:(c2 + 1) * 512], ALU.mult),
                      reads=[("bank", 4 + c2), "hmask"], writes=[("tmp6", c2)])
            kb.op("dve", lambda e: e.tensor_reduce(osel[0:64, :], tmp[0:64, :].rearrange("p (h d) -> p d h", d=128), AX.X, ALU.add),
                  reads=[("tmp6", 0), ("tmp6", 1)], writes=["osel"])
            kb.op("dve", lambda e: e.reciprocal(rdn[0:64, :], banks[6][0:64, 0:1]), reads=[("bank", 6)], writes=["rdn"])
            kb.op("dve", lambda e: e.tensor_scalar(o_bf[0:64, :], osel[0:64, :], rdn[0:64, 0:1], None, ALU.mult),
                  reads=["osel", "rdn"], writes=["o_bf6"])
            pvo = bank_bf(7)
            kb.op("pe", lambda e: e.transpose(pvo[:, 512:576], o_bf[0:64, :], ident[0:64, 0:64]),
                  reads=["o_bf6", "ident"], writes=[("bank", 7)])
            kb.op("act", lambda e: e.copy(oaT[:, :, OWN + b * 8:OWN + (b + 1) * 8],
                                          pvo[:, 512:576].rearrange("p (h q) -> p h q", q=8)),
                  reads=[("bank", 7)], writes=[("oaT_s", b)])
        A.pop()
        kb.barrier()

    def phase6_stub(qaT_s, kaT_s, va_s):
        kb.op("dve", lambda e: e.memset(oaT[:, :, OWN:OWN + NS], 0.0), writes=[("oaT_s",)])
        kb.barrier()

    TOKT = [(t, 128, t * 128) for t in range(8)] + [(8, NS, OWN)]
    mixT = xTp
    hT = xT
    racc = A.at(72 * KI, [9, D], F32)
    w_ret_v = w_ret.rearrange("(kc p) n -> p kc n", p=128)
    w_att_v = w_att.rearrange("(kc p) n -> p kc n", p=128)
    w_out_v = w_out.rearrange("(kc p) n -> p kc n", p=128)
    w_up_v = w_up.rearrange("(kc p) n -> p kc n", p=128)
    w_down_v = w_down.rearrange("(fc p) n -> p fc n", p=128)

    def layer_norm_tile(t, n, lng, lnb, scr):
        stats, mv, sd = scr
        r = racc[0:n, t, :]
        for c4 in range(4):
            kb.op("dve", lambda e, c4=c4, r=r: e.bn_stats(stats[0:n, c4, :], r[:, c4 * 512:(c4 + 1) * 512]),
                  reads=[("racc", t)], writes=["ln_stats"])
        kb.op("dve", lambda e: e.bn_aggr(mv[0:n, :], stats[0:n, :, :].rearrange("p a b -> p (a b)")),
              reads=["ln_stats"], writes=["ln_mv"])
        kb.op("act", lambda e: e.activation(sd[0:n, 0:1], mv[0:n, 1:2], AF.Sqrt, bias=1e-5, scale=1.0),
              reads=["ln_mv"], writes=["ln_sd"])
        kb.op("dve", lambda e: e.reciprocal(sd[0:n, 1:2], sd[0:n, 0:1]), reads=["ln_sd"], writes=["ln_rs"])
        kb.op("dve", lambda e, r=r: e.scalar_tensor_tensor(r, r, mv[0:n, 0:1], lng[0:n, :], ALU.subtract, ALU.mult),
              reads=[("racc", t), "ln_mv", "lnp"], writes=[("racc", t)])
        kb.op("dve", lambda e, r=r: e.scalar_tensor_tensor(r, r, sd[0:n, 1:2], lnb[0:n, :], ALU.mult, ALU.add),
              reads=[("racc", t), "ln_rs", "lnp"], writes=[("racc", t)])

    def phase4a():
        A.top = 126 * KI
        A.push()
        GW = 256
        Wgr = [A.alloc([16, GW], BF16) for _ in range(2)]
        Wga = [A.alloc([16, GW], BF16) for _ in range(2)]
        Wr = [A.alloc([16, GW], BF16) for _ in range(2)]
        Wa = [A.alloc([8, GW], BF16) for _ in range(2)]
        sg = [A.alloc([2 * GW], F32) for _ in range(2)]
        prod = A.alloc([2 * GW], F32)
        mixb = [A.alloc([GW], BF16) for _ in range(2)]
        it = 0
        pending = []
        for g in range(D // GW):
            cs = slice(g * GW, (g + 1) * GW)
            s2 = g % 2
            kb.dma("pool", Wgr[s2], w_in_v[:, :, C_GBR + g * GW:C_GBR + (g + 1) * GW], f"Wgr{s2}", writes=[("Wgr", s2)])
            kb.dma("pool", Wga[s2], w_in_v[:, :, C_GBA + g * GW:C_GBA + (g + 1) * GW], f"Wga{s2}", writes=[("Wga", s2)])
            kb.dma("pool", Wr[s2], w_ret_v[:, :, cs], f"Wr{s2}", writes=[("Wr", s2)])
            kb.dma("pool", Wa[s2], w_att_v[:, :, cs], f"Wa{s2}", writes=[("Wa", s2)])
            for (t, n, off) in TOKT:
                it += 1
                bx, by = (it % 2) * 2, (it % 2) * 2 + 1
                _run_pending_after_mm = True
                for (bi, c0, src, W, wk, kcs) in ((bx, 0, xT, Wgr[s2], ("Wgr", s2), 16), (bx, GW, xT, Wga[s2], ("Wga", s2), 16),
                                                  (by, 0, uT, Wr[s2], ("Wr", s2), 16), (by, GW, oaT, Wa[s2], ("Wa", s2), 8)):
                    for kc in range(kcs):
                        kb.op("pe", lambda e: e.matmul(
                            banks[bi][0:n, c0:c0 + GW], src[:, kc, off:off + n], W[:, kc, :], start=(kc == 0), stop=(kc == kcs - 1)),
                            reads=[wk], writes=[("bank", bi)])
                while pending:
                    pending.pop(0)()
                sgt = sg[it % 2]
                kb.op("act", lambda e: e.activation(sgt[0:n, :], banks[bx][0:n, :], AF.Sigmoid),
                      reads=[("bank", bx)], writes=[("sg", it % 2)])
                kb.op("dve", lambda e: e.tensor_tensor(prod[0:n, :], sgt[0:n, :], banks[by][0:n, :], ALU.mult),
                      reads=[("sg", it % 2), ("bank", by)], writes=["prod"])
                mb = mixb[it % 2]
                kb.op("dve", lambda e: e.tensor_tensor(mb[0:n, :], prod[0:n, 0:GW], prod[0:n, GW:2 * GW], ALU.add),
                      reads=["prod"], writes=[("mixb", it % 2)])
                def fin(it=it, mb=mb, n=n, off=off, g=g, t=t):
                    tb = 4 + it % 2
                    pv = bank_bf(tb).rearrange("p (a b) -> p a b", b=128)
                    for j in range(2):
                        kb.op("pe", lambda e: e.transpose(pv[:, j, 0:n], mb[0:n, j * 128:(j + 1) * 128], ident[0:n, 0:n]),
                              reads=[("mixb", it % 2), "ident"], writes=[("bank", tb)])
                    kb.op("act", lambda e: e.copy(mixT[:, g * 2:(g + 1) * 2, off:off + n], pv[:, 0:2, 0:n]),
                          reads=[("bank", tb)], writes=[("mixT", t, g)])
                pending.append(fin)
        while pending:
            pending.pop(0)()
        A.pop()
        kb.barrier()

    def phase4b():
        A.top = 144 * KI
        A.push()
        ring = Ring("w4b", [A.alloc([16, 512], BF16) for _ in range(2)])
        lng = A.alloc([D], F32)
        lnb = A.alloc([D], F32)
        xs = [A.alloc([512], F32) for _ in range(2)]
        hb = A.alloc([D], BF16)
        stats = A.alloc([4, 6], F32)
        mv = A.alloc([2], F32)
        sd = A.alloc([2], F32)
        kb.dma("sp", lng, c_ln[0:1, :].partition_broadcast(128), "c_lng", writes=["lnp"])
        kb.dma("sp", lnb, c_ln[1:2, :].partition_broadcast(128), "c_lnb", writes=["lnp"])
        kb.op("act", lambda e: e.activation(lng, lng, AF.Copy, scale=ALPHA), reads=["lnp"], writes=["lnp"])
        kb.op("act", lambda e: e.activation(lnb, lnb, AF.Copy, scale=ALPHA), reads=["lnp"], writes=["lnp"])
        it = 0
        for g in range(4):
            cs = slice(g * 512, (g + 1) * 512)
            W, wkey = ring.load(w_out_v[:, :, cs], 16, 512)
            for (t, n, off) in TOKT:
                it += 1
                pb = it % 2
                for kc in range(16):
                    kb.op("pe", lambda e, pb=pb, kc=kc, n=n, off=off, W=W: e.matmul(
                        banks[pb][0:n, :], mixT[:, kc, off:off + n], W[:, kc, :], start=(kc == 0), stop=(kc == 15)),
                        reads=[wkey], writes=[("bank", pb)])
                xsl = xs[it % 2]
                src = x_own[t * 128:t * 128 + n, cs] if t < 8 else x_s[0:n, cs]
                kb.dma("sp", xsl[0:n, :], src, f"xs{it%2}", writes=[("xs", it % 2)])
                kb.op("dve", lambda e, xsl=xsl, n=n, pb=pb, t=t, cs=cs: e.scalar_tensor_tensor(
                    racc[0:n, t, cs], xsl[0:n, :], ALPHA, banks[pb][0:n, :], ALU.mult, ALU.add),
                    reads=[("xs", it % 2), ("bank", pb)], writes=[("racc", t)])
        for (t, n, off) in TOKT:
            layer_norm_tile(t, n, lng, lnb, (stats, mv, sd))
            kb.op("act", lambda e, n=n, t=t: e.activation(hb[0:n, :], racc[0:n, t, :], AF.Copy, scale=1.0 / ALPHA),
                  reads=[("racc", t)], writes=["hb"])
            for half in range(2):
                tb = 2 + half
                pv = bank_bf(tb).rearrange("p (a b) -> p a b", b=128)
                for j in range(8):
                    kc = half * 8 + j
                    kb.op("pe", lambda e, pv=pv, j=j, kc=kc, n=n: e.transpose(
                        pv[:, j, 0:n], hb[0:n, kc * 128:(kc + 1) * 128], ident[0:n, 0:n]),
                        reads=["hb", "ident"], writes=[("bank", tb)])
                kb.op("act", lambda e, pv=pv, half=half, off=off, n=n: e.copy(
                    hT[:, half * 8:half * 8 + 8, off:off + n], pv[:, :, 0:n]),
                    reads=[("bank", tb)], writes=[("hT", t, half)])
        A.pop()
        kb.barrier()

    def phase5():
        A.top = 144 * KI
        A.push()
        ring_up = Ring("wup", [A.at(36 * KI, [16, 512], BF16), A.at(52 * KI, [16, 512], BF16)])
        ring_dn = Ring("wdn", [A.alloc([4, D], BF16) for _ in range(2)])
        aT = [A.alloc([4, NTOK], BF16) for _ in range(2)]
        rl = [A.at(68 * KI, [512], F32), A.at(70 * KI, [512], F32)]
        TG = [(0, 512), (512, 512), (1024, NS)]
        iu = 0
        idn = 0
        for fg in range(16):
            Wu, uk = ring_up.load(w_up_v[:, :, fg * 512:(fg + 1) * 512], 16, 512)
            Wd, dk = ring_dn.load(w_down_v[:, fg * 4:(fg + 1) * 4, :], 4, D)
            a = aT[fg % 2]
            for fc in range(4):
                for (t0, n) in TG:
                    iu += 1
                    pb = iu % 2
                    for kc in range(16):
                        kb.op("pe", lambda e, pb=pb, kc=kc, fc=fc, t0=t0, n=n, Wu=Wu: e.matmul(
                            banks[pb][:, 0:n], Wu[:, kc, fc * 128:(fc + 1) * 128], hT[:, kc, t0:t0 + n],
                            start=(kc == 0), stop=(kc == 15)),
                            reads=[uk], writes=[("bank", pb)])
                    r = rl[iu % 2]
                    kb.op("act", lambda e, pb=pb, n=n, r=r: e.activation(r[:, 0:n], banks[pb][:, 0:n], AF.Relu),
                          reads=[("bank", pb)], writes=[("rl", iu % 2)])
                    kb.op("pool", lambda e, a=a, fc=fc, t0=t0, n=n, r=r: e.tensor_tensor(
                        a[:, fc, t0:t0 + n], r[:, 0:n], r[:, 0:n], ALU.mult),
                        reads=[("rl", iu % 2)], writes=[("aT", fg % 2, fc, t0)])
            for (t, n, off) in TOKT:
                t0 = 0 if off < 512 else (512 if off < 1024 else 1024)
                for cg in range(4):
                    idn += 1
                    pb = 2 + idn % 6
                    for fc in range(4):
                        kb.op("pe", lambda e, pb=pb, fc=fc, n=n, off=off, cg=cg, a=a, Wd=Wd: e.matmul(
                            banks[pb][0:n, :], a[:, fc, off:off + n], Wd[:, fc, cg * 512:(cg + 1) * 512],
                            start=(fc == 0), stop=(fc == 3)),
                            reads=[dk, ("aT", fg % 2, fc, t0)], writes=[("bank", pb)])
                    kb.op("dve", lambda e, pb=pb, n=n, t=t, cg=cg: e.tensor_tensor(
                        racc[0:n, t, cg * 512:(cg + 1) * 512], racc[0:n, t, cg * 512:(cg + 1) * 512],
                        banks[pb][0:n, :], ALU.add),
                        reads=[("bank", pb), ("racc", t)], writes=[("racc", t)])
        A.pop()
        kb.barrier()
        A.top = 144 * KI
        A.push()
        lng = A.alloc([D], F32)
        lnb = A.alloc([D], F32)
        stats = A.alloc([4, 6], F32)
        mv = A.alloc([2], F32)
        sd = A.alloc([2], F32)
        kb.dma("sp", lng, c_ln[2:3, :].partition_broadcast(128), "c_lng2", writes=["lnp"])
        kb.dma("sp", lnb, c_ln[3:4, :].partition_broadcast(128), "c_lnb2", writes=["lnp"])
        for (t, n, off) in TOKT:
            layer_norm_tile(t, n, lng, lnb, (stats, mv, sd))
            dst = y_own[t * 128:(t + 1) * 128, :] if t < 8 else y_s[:, :]
            kb.dma("sp", dst, racc[0:n, t, :], "yout", reads=[("racc", t)])
        A.pop()
        kb.barrier()

    qaT_s = kb.sbuf("qaT_s", [128, H, NS], BF16)[:]
    kaT_s = kb.sbuf("kaT_s", [128, H, NS], BF16)[:]
    va_s = kb.sbuf("va_s", [128, 1024], BF16)[:]

    if "p1" in phases:
        phase1()
    if "p3" in phases:
        phase3(qaT_s, kaT_s, va_s)
    if "p2" in phases:
        phase2()
    if "p6" in phases:
        phase6(qaT_s, kaT_s, va_s)
    if "p6s" in phases:
        phase6_stub(qaT_s, kaT_s, va_s)
    if "p4" in phases:
        phase4a()
        phase4b()
    if "p5" in phases:
        phase5()

    kb.finish("sp")
    import os
    if os.environ.get("KB_LOG"):
        print("PHASE_LOG", kb.phase_log)
        print("QUEUE_LEN", {k: len(v) for k, v in kb.prog.items()})
    kb.emit()
    kb.close()
    return nc


def _host_consts(core):
    half = core % 2
    f32 = np.float32
    c = {}
    c["c_ident"] = np.eye(128, dtype=f32)
    c["c_i64"] = np.eye(64, dtype=f32)
    inv = (np.float32(10000.0) ** (-np.arange(128, dtype=f32) / np.float32(128))).astype(f32)

    def rot(pos):
        ang = pos.astype(f32)[:, None] * inv[None, :]
        return np.concatenate([np.cos(ang), np.sin(ang)], axis=-1).astype(f32)

    p_own = half * OWN + np.arange(OWN)
    c["c_rot_own"] = rot(p_own).reshape(8, 128, 256)
    c["c_rot_prev"] = rot(np.arange(OWN)).reshape(8, 128, 256)
    c["c_rot_s"] = rot(PAST + (np.arange(NS) % 8))
    lg = np.log1p(-np.exp2(-5.0 - np.arange(H, dtype=np.float64)))
    i = np.arange(128, dtype=np.float64)
    c["c_qd"] = np.exp(lg[None, :] * (i[:, None] + 1.0)).astype(f32)
    c["c_kd"] = (np.exp(lg[None, :] * (127.0 - i[:, None])) / 16.0).astype(f32)
    jj, ii = np.meshgrid(i, i, indexing="ij")
    m = np.where(ii[:, None, :] >= jj[:, None, :], np.exp(-lg[None, :, None] * (jj[:, None, :] + 1.0)), 0.0) / 16.0
    c["c_maskT"] = m.astype(f32)
    il = (np.arange(NS) % 8).astype(np.float64)
    bb = np.arange(NS) // 8
    c["c_sqd"] = np.exp(lg[None, :] * (il[:, None] + 1.0)).astype(f32)
    c["c_skd"] = (np.exp(lg[None, :] * (7.0 - il[:, None])) / 16.0).astype(f32)
    same = (bb[:, None] == bb[None, :])
    caus = (il[None, :] >= il[:, None])
    sm = np.where((same & caus)[:, None, :], np.exp(-lg[None, :, None] * (il[:, None, None] + 1.0)), 0.0) / 16.0
    c["c_smaskT"] = sm.astype(f32)
    c["c_bmP"] = (bb[:, None] == np.arange(4)[None, :]).astype(f32)
    c["c_bmF"] = np.tile((np.arange(4)[:, None] == bb[None, :]).astype(f32).reshape(1, 4 * NS), (128, 1))
    kk, qq = np.meshgrid(np.arange(128), np.arange(128), indexing="ij")
    c["c_causal"] = np.where(qq >= kk, 0.0, NEG).astype(f32)
    E = np.zeros((32, 32, 128), f32)
    for r in range(32):
        E[r, r, :] = 1.0
    c["c_E"] = E.reshape(32, 32 * 128)
    el = np.zeros((8, 8), f32)
    for t in range(8):
        for n in range(8):
            if n < 4:
                el[t, n] = 1.0 if half == 1 else 0.0
            else:
                el[t, n] = 1.0 if (n - 4) < t // 2 else 0.0
    c["c_elig"] = np.tile(el.reshape(1, 64), (128, 1)).astype(f32)
    c["c_negm"] = np.where(c["c_elig"] > 0, 0.0, -1e30).astype(f32)
    so = np.zeros((NS, 4, 8, 8), f32)
    for j in range(NS):
        for q in range(8):
            if (j % 8) <= q:
                so[j, j // 8, :, q] = 1.0
    c["c_sown"] = so.reshape(NS, 256)
    hm = np.zeros((8, 8, 8, 128), f32)
    for h in range(8):
        hm[h, :, h, :] = 1.0
    c["c_hmask"] = hm.reshape(64, 1024)
    return c


_PROG_CACHE = {}


def kernel(x_prompt, x_sample, cache_k, cache_v, state_ret, page_table,
           w_in, ret_gn_gain, w_ret_br, w_att_br, w_out, ln1_g, ln1_b,
           w_up, w_down, ln2_g, ln2_b):
    f32 = np.float32
    n_phys = cache_k.shape[1]
    ck = np.ascontiguousarray(cache_k[0].reshape(n_phys * PAGE, 1024))
    cv = np.ascontiguousarray(cache_v[0].reshape(n_phys * PAGE, 1024))
    key = n_phys
    if key not in _PROG_CACHE:
        import os
        ph = os.environ.get("KPHASES")
        _PROG_CACHE[key] = build_program(n_phys * PAGE, tuple(ph.split(","))) if ph else build_program(n_phys * PAGE)
    nc = _PROG_CACHE[key]
    shared = {
        "w_in": np.ascontiguousarray(w_in[0]), "w_ret": np.ascontiguousarray(w_ret_br[0]),
        "w_att": np.ascontiguousarray(w_att_br[0]), "w_out": np.ascontiguousarray(w_out[0]),
        "w_up": np.ascontiguousarray(w_up[0]), "w_down": np.ascontiguousarray(w_down[0]),
        "cache_k": ck, "cache_v": cv,
        "c_gn": np.ascontiguousarray(ret_gn_gain[0].reshape(1, D)),
        "c_ln": np.ascontiguousarray(np.stack([ln1_g[0], ln1_b[0], ln2_g[0], ln2_b[0]], 0)),
    }
    in_maps = []
    zeros_prev = np.zeros((OWN, D), f32)
    for c in range(NCORE):
        b, half = c // 2, c % 2
        m = dict(shared)
        m["x_own"] = np.ascontiguousarray(x_prompt[b, half * OWN:(half + 1) * OWN])
        m["x_prev"] = np.ascontiguousarray(x_prompt[b, 0:OWN]) if half == 1 else zeros_prev
        m["x_s"] = np.ascontiguousarray(x_sample[4 * c:4 * c + 4].reshape(NS, D))
        m["state_s"] = np.ascontiguousarray(state_ret[0, 4 * c:4 * c + 4])
        m["ptab"] = np.ascontiguousarray(page_table[4 * c:4 * c + 4].reshape(1, 4 * NPG)).astype(np.int32)
        m.update(_host_consts(c))
        in_maps.append(m)
    import os as _os
    if _os.environ.get("KTRACE"):
        res = run_bass_kernel_spmd(nc, in_maps, core_ids=list(range(NCORE)), trace=True)
        print("KTRACE exec_time_ns", res.exec_time_ns)
    else:
        res = run_bass_kernel_spmd(nc, in_maps, core_ids=list(range(NCORE)))
    R = res.results
    B = x_prompt.shape[0]
    y_p = np.zeros((B, SEQ, D), f32)
    k_p = np.zeros((1, B, SEQ, H, HD), f32)
    v_p = np.zeros((1, B, SEQ, H, HD), f32)
    s_p = np.zeros((1, B, H, DK, DK), f32)
    y_s = np.zeros((32, 8, D), f32)
    k_s = np.zeros((1, 32, 8, H, HD), f32)
    v_s = np.zeros((1, 32, 8, H, HD), f32)
    s_s = np.zeros((1, 32, H, DK, DK), f32)
    for c in range(NCORE):
        b, half = c // 2, c % 2
        sl = slice(half * OWN, (half + 1) * OWN)
        y_p[b, sl] = R[c]["y_own"]
        k_p[0, b, sl] = R[c]["k_own"].reshape(OWN, H, HD)
        v_p[0, b, sl] = R[c]["v_own"].reshape(OWN, H, HD)
        if half == 1:
            s_p[0, b] = R[c]["ret_p"]
        y_s[4 * c:4 * c + 4] = R[c]["y_s"].reshape(4, 8, D)
        k_s[0, 4 * c:4 * c + 4] = R[c]["k_s"].reshape(4, 8, H, HD)
        v_s[0, 4 * c:4 * c + 4] = R[c]["v_s"].reshape(4, 8, H, HD)
        s_s[0, 4 * c:4 * c + 4] = R[c]["ret_s"]
    return (y_p, y_s, k_p, v_p, s_p, k_s, v_s, s_s)
```

```python
import contextlib
import numpy as np
import concourse.bass as bass
import concourse.mybir as mybir
from concourse.bass_utils import run_bass_kernel_spmd

F32 = mybir.dt.float32
BF16 = mybir.dt.bfloat16
I32 = mybir.dt.int32
AF = mybir.ActivationFunctionType
ALU = mybir.AluOpType
AX = mybir.AxisListType

COMPUTE = ("pe", "act", "dve", "pool")

D = 2048
SEQ = 2048
NCORE = 8
OWN = 1024
NS = 32
NTOK = OWN + NS
H = 8
DK = 256
HD = 128
DFF = 8192
PAST = 8192
PAGE = 128
NPG = 64
ALPHA = 2.0 ** 0.25
C_QR, C_KR, C_VR, C_GR = 0, 2048, 4096, 6144
C_QA, C_KA, C_VA = 8192, 9216, 10240
C_GBR, C_GBA = 11264, 13312
NEG = -30000.0


class _Rec:
    def __getattr__(self, name):
        return lambda *a, **k: (name, a, k)


_REC = _Rec()


class KB:
    def __init__(self, nc):
        self.nc = nc
        self.stack = contextlib.ExitStack()
        self.h = {"pe": nc.tensor, "act": nc.scalar, "dve": nc.vector,
                  "pool": nc.gpsimd, "sp": nc.sync}
        self.prog = {k: [] for k in self.h}
        self.sem = {}
        self.cnt = {}
        for k in COMPUTE:
            self.sem[k] = self.stack.enter_context(nc.semaphore("s_" + k))
            self.cnt[k] = 0
        self.dsem = {}
        self.waited = {}
        self.track = {}

    def sbuf(self, name, shape, dt):
        return self.stack.enter_context(self.nc.sbuf_tensor(name, list(shape), dt))

    def psum(self, name, shape, dt=F32):
        return self.stack.enter_context(self.nc.psum_tensor(name, list(shape), dt))

    def _dsem(self, key):
        if key not in self.dsem:
            self.dsem[key] = [self.stack.enter_context(self.nc.semaphore("d_" + str(key))), 0]
        return self.dsem[key]

    def _events_for(self, reads, writes):
        evs = []
        for k in list(reads) + list(writes):
            t = self.track.get(k)
            if t and t[0] is not None:
                evs.append(t[0])
        for k in writes:
            t = self.track.get(k)
            if t:
                evs.extend(t[1])
        return evs

    def _emit_waits(self, eng, evs):
        need = {}
        for ev in evs:
            kind, name, val = ev
            if kind == "eng" and name == eng and eng == "pe":
                continue
            semname = (kind, name)
            if self.waited.get((eng, semname), 0) >= val:
                continue
            if need.get(semname, 0) < val:
                need[semname] = val
        for semname, val in need.items():
            self.waited[(eng, semname)] = val
            sem = self.sem[semname[1]] if semname[0] == "eng" else self.dsem[semname[1]][0]
            self.prog[eng].append(("wait", sem, val))

    def _record(self, ev, reads, writes):
        for k in reads:
            t = self.track.setdefault(k, [None, []])
            t[1].append(ev)
        for k in writes:
            self.track[k] = [ev, []]

    def op(self, eng, fn, reads=(), writes=()):
        bank_r = [k for k in reads if isinstance(k, tuple) and k[0] == "bank"]
        if bank_r:
            reads = [k for k in reads if k not in bank_r]
            writes = list(writes) + [k for k in bank_r if k not in writes]
        evs = self._events_for(reads, writes)
        self._emit_waits(eng, evs)
        self.cnt[eng] += 1
        ev = ("eng", eng, self.cnt[eng])
        self.prog[eng].append(("op", fn(_REC), self.sem[eng]))
        self._record(ev, reads, writes)
        return ev

    def dma(self, q, out, in_, sem_key, reads=(), writes=(), **kw):
        evs = self._events_for(reads, writes)
        self._emit_waits(q, evs)
        ds = self._dsem(sem_key)
        ds[1] += 16
        ev = ("dma", sem_key, ds[1])
        self.prog[q].append(("dma", out, in_, ds[0], kw))
        self._record(ev, reads, writes)
        return ev

    def dma_fn(self, q, fn, sem_key, reads=(), writes=()):
        evs = self._events_for(reads, writes)
        self._emit_waits(q, evs)
        ds = self._dsem(sem_key)
        ds[1] += 16
        ev = ("dma", sem_key, ds[1])
        self.prog[q].append(("dmafn", fn(_REC), ds[0]))
        self._record(ev, reads, writes)
        return ev

    def barrier(self):
        self.phase_log = getattr(self, "phase_log", [])
        self.phase_log.append(dict(self.cnt))
        for eng in self.h:
            for k in COMPUTE:
                if k != eng and self.cnt[k] > self.waited.get((eng, ("eng", k)), 0):
                    self.waited[(eng, ("eng", k))] = self.cnt[k]
                    self.prog[eng].append(("wait", self.sem[k], self.cnt[k]))
            for key, (sem, cnt) in self.dsem.items():
                if cnt > self.waited.get((eng, ("dma", key)), 0):
                    self.waited[(eng, ("dma", key))] = cnt
                    self.prog[eng].append(("wait", sem, cnt))
        self.track = {}

    def finish(self, final_eng="sp"):
        for key, (sem, cnt) in self.dsem.items():
            if cnt > 0:
                self.prog[final_eng].append(("wait", sem, cnt))
        for k in COMPUTE:
            if self.cnt[k] > 0:
                self.prog[final_eng].append(("wait", self.sem[k], self.cnt[k]))

    def emit(self):
        nc, prog, h = self.nc, self.prog, self.h

        def run(name):
            e = h[name]
            for it in prog[name]:
                if it[0] == "wait":
                    e.wait_ge(it[1], it[2])
                elif it[0] == "op":
                    getattr(e, it[1][0])(*it[1][1], **it[1][2]).then_inc(it[2], 1)
                elif it[0] == "dmafn":
                    getattr(e, it[1][0])(*it[1][1], **it[1][2]).then_inc(it[2], 16)
                else:
                    e.dma_start(out=it[1], in_=it[2], **it[4]).then_inc(it[3], 16)

        with nc.Block() as block:
            @block.sync
            def _(e):
                run("sp")

            @block.tensor
            def _(e):
                run("pe")

            @block.scalar
            def _(e):
                run("act")

            @block.vector
            def _(e):
                run("dve")

            @block.gpsimd
            def _(e):
                run("pool")

    def close(self):
        self.stack.close()


class Arena:
    def __init__(self, t, nbytes):
        self.t = t
        self.n = nbytes
        self.top = 0
        self.marks = []

    def alloc(self, free_shape, dt, parts=128):
        esz = 4 if dt in (F32, I32) else 2
        n = int(np.prod(free_shape))
        nb = (n * esz + 31) // 32 * 32
        assert self.top + nb <= self.n, ("arena overflow", self.top, nb, self.n)
        off = self.top
        self.top += nb
        return self.at(off, free_shape, dt, parts)

    def at(self, off, free_shape, dt, parts=128):
        esz = 4 if dt in (F32, I32) else 2
        n = int(np.prod(free_shape))
        nb = n * esz
        assert off % 4 == 0 and off + nb <= self.n
        v = self.t[0:parts, off // 4:(off + (nb + 3) // 4 * 4) // 4]
        if dt != F32:
            v = v.bitcast(dt)
        v = v[:, 0:n]
        if len(free_shape) == 2:
            v = v.rearrange("p (a b) -> p a b", b=free_shape[1])
        elif len(free_shape) == 3:
            v = v.rearrange("p (a b c) -> p a b c", b=free_shape[1], c=free_shape[2])
        return v

    def push(self):
        self.marks.append(self.top)

    def pop(self):
        import os
        if os.environ.get("KB_LOG"):
            print("ARENA_TOP_KiB", self.top / 1024.0)
        self.top = self.marks.pop()


def bcast(ap, steps):
    a = ap.ap
    return bass.AP(ap.tensor, ap.offset, [list(a[0])] + [list(s) for s in steps])


def build_program(n_phys_rows, phases=("p1", "p3", "p2", "p6", "p4", "p5")):
    nc = bass.Bass("TRN2", target_bir_lowering=False)

    def din(name, shape, dt=F32):
        return nc.dram_tensor(name, list(shape), dt, kind="ExternalInput").ap()

    def dout(name, shape, dt=F32):
        return nc.dram_tensor(name, list(shape), dt, kind="ExternalOutput").ap()

    x_own = din("x_own", [OWN, D])
    x_prev = din("x_prev", [OWN, D])
    x_s = din("x_s", [NS, D])
    w_in = din("w_in", [D, 15360])
    w_ret = din("w_ret", [D, D])
    w_att = din("w_att", [1024, D])
    w_out = din("w_out", [D, D])
    w_up = din("w_up", [D, DFF])
    w_down = din("w_down", [DFF, D])
    cache_k = din("cache_k", [n_phys_rows, 1024])
    cache_v = din("cache_v", [n_phys_rows, 1024])
    state_s = din("state_s", [4, H, DK, DK])
    ptab = din("ptab", [1, 4 * NPG], I32)
    c_ident = din("c_ident", [128, 128])
    c_rot_own = din("c_rot_own", [8, 128, 256])
    c_rot_prev = din("c_rot_prev", [8, 128, 256])
    c_rot_s = din("c_rot_s", [NS, 256])
    c_qd = din("c_qd", [128, 8])
    c_kd = din("c_kd", [128, 8])
    c_maskT = din("c_maskT", [128, 8, 128])
    c_sqd = din("c_sqd", [NS, 8])
    c_skd = din("c_skd", [NS, 8])
    c_smaskT = din("c_smaskT", [NS, 8, NS])
    c_bmP = din("c_bmP", [NS, 4])
    c_bmF = din("c_bmF", [128, 4 * NS])
    c_gn = din("c_gn", [1, D])
    c_causal = din("c_causal", [128, 128])
    c_E = din("c_E", [32, 32 * 128])
    c_negm = din("c_negm", [128, 64])
    c_elig = din("c_elig", [128, 64])
    c_ln = din("c_ln", [4, D])
    c_sown = din("c_sown", [NS, 4 * 64])
    c_hmask = din("c_hmask", [64, 8 * 128])
    c_ident64 = din("c_i64", [64, 64])

    y_own = dout("y_own", [OWN, D])
    y_s = dout("y_s", [NS, D])
    k_own = dout("k_own", [OWN, 1024])
    v_own = dout("v_own", [OWN, 1024])
    ret_p = dout("ret_p", [H, DK, DK])
    k_s = dout("k_s", [NS, 1024])
    v_s = dout("v_s", [NS, 1024])
    ret_s = dout("ret_s", [4, H, DK, DK])

    kb = KB(nc)
    ARENA_BYTES = 204 * 1024
    arena_t = kb.sbuf("arena", [128, ARENA_BYTES // 4], F32)
    A = Arena(arena_t, ARENA_BYTES)
    banks = [kb.psum(f"bank{i}", [128, 512], F32) for i in range(8)]

    def bank_bf(i):
        return banks[i][:, 0:512].bitcast(BF16)

    ident = kb.sbuf("ident", [128, 128], BF16)[:]
    ones_bf = kb.sbuf("ones_bf", [128, 128], BF16)[:]
    kb.dma("pool", ident, c_ident, "c_ident", writes=["ident"])
    kb.op("dve", lambda e: e.memset(ones_bf, 1.0), writes=["ones"])

    KI = 1024
    xT = A.at(0 * KI, [16, NTOK], BF16)
    xTp = A.at(36 * KI, [16, NTOK], BF16)
    oaT = A.at(72 * KI, [8, NTOK], BF16)
    uT = A.at(90 * KI, [16, NTOK], BF16)

    w_in_v = w_in.rearrange("(kc p) n -> p kc n", p=128)

    class Ring:
        def __init__(self, name, slots):
            self.name, self.slots, self.i = name, slots, 0

        def load(self, src_ap, kcs, ncols):
            s = self.i % len(self.slots)
            self.i += 1
            dst = self.slots[s]
            key = (self.name, s)
            kb.dma("pool", dst[:, 0:kcs, 0:ncols], src_ap, f"{self.name}{s}", writes=[key])
            return dst, key

    def phase1():
        A.top = 126 * KI
        A.push()
        xb = [A.alloc([D], BF16) for _ in range(3)]
        srcs = [(x_own, t, 128, xT, t * 128) for t in range(8)] + \
               [(x_prev, t, 128, xTp, t * 128) for t in range(8)] + [(x_s, 0, NS, xT, OWN)]
        for i, (src, t, n, dstT, off) in enumerate(srcs):
            b = xb[i % 3]
            bk = ("xb", i % 3)
            kb.dma("pool", b[0:n, :], src[t * 128:t * 128 + n, :], f"xb{i%3}", writes=[bk], max_dma_last_dim=2048)
            for half in range(2):
                pb = i * 2 + half
                pv = bank_bf(pb % 4).rearrange("p (a b) -> p a b", b=128)
                for j in range(8):
                    kc = half * 8 + j
                    kb.op("pe", lambda e, pv=pv, j=j, kc=kc, b=b, n=n: e.transpose(
                        pv[:, j, 0:n], b[0:n, kc * 128:(kc + 1) * 128], ident[0:n, 0:n]),
                        reads=[bk, "ident"], writes=[("bank", pb % 4)])
                eng = "dve" if half == 0 else "act"
                if eng == "dve":
                    kb.op("dve", lambda e, pv=pv, dstT=dstT, half=half, off=off, n=n: e.tensor_copy(
                        dstT[:, half * 8:half * 8 + 8, off:off + n], pv[:, :, 0:n]),
                        reads=[("bank", pb % 4)], writes=[("xT", id(dstT), off, half)])
                else:
                    kb.op("act", lambda e, pv=pv, dstT=dstT, half=half, off=off, n=n: e.copy(
                        dstT[:, half * 8:half * 8 + 8, off:off + n], pv[:, :, 0:n]),
                        reads=[("bank", pb % 4)], writes=[("xT", id(dstT), off, half)])
        A.pop()
        kb.barrier()

    HG = 4

    def phase3(qaT_s, kaT_s, va_s):
        A.top = 90 * KI
        A.push()
        GC = HG * HD
        ring = Ring("w3", [A.alloc([16, GC], BF16) for _ in range(3)])
        kT = A.alloc([HG, 2048], BF16)
        vA = A.alloc([16, GC], BF16)
        qT = A.alloc([HG, OWN], BF16)
        stg = [A.alloc([GC], F32) for _ in range(2)]
        cbf = [A.alloc([GC], BF16) for _ in range(2)]
        biasT = A.alloc([OWN], BF16)
        pt_sb = [A.alloc([OWN], BF16) for _ in range(2)]
        rden = [A.alloc([128], F32) for _ in range(2)]
        ksf = A.alloc([HG, 8], F32)
        ksb = A.alloc([HG, 8], BF16)
        sm2 = A.alloc([2, HG, 8], F32)
        rank2 = A.alloc([2 * HG, 8], F32)
        bias2 = A.alloc([2, HG * 8], BF16)
        causal = A.alloc([128], BF16)
        Eoh = A.alloc([32 * 128], BF16)
        negm = A.alloc([8, 8], F32)
        elig = A.alloc([8, 8], F32)
        import os
        DBG = os.environ.get("P3_DBG", "")
        if "noconst" not in DBG:
          kb.dma("pool", causal, c_causal, "c_causal", writes=["causal"])
        if "noconst" not in DBG:
          kb.dma("pool", Eoh[0:32, :], c_E, "c_E", writes=["Eoh"], max_dma_last_dim=4096)
        kb.dma("sp", negm, c_negm.rearrange("p (a b) -> p a b", b=8), "c_negm", writes=["negm"])
        kb.dma("sp", elig, c_elig.rearrange("p (a b) -> p a b", b=8), "c_elig", writes=["elig"])
        scale = HD ** -0.5
        tiles_all = [("prev", t, 128, xTp, t * 128) for t in range(8)] + \
                    [("own", t, 128, xT, t * 128) for t in range(8)] + [("s", 0, NS, xT, OWN)]
        ectr = [0]
        pend3 = []

        for hp in range(H // HG):
            cg0 = hp * GC
            for which, cbase in (("k", C_KA), ("v", C_VA), ("q", C_QA)):
                W, wkey = ring.load(w_in_v[:, :, cbase + cg0:cbase + cg0 + GC], 16, GC)
                for ti, (kind, t, n, xsrc, off) in enumerate(tiles_all):
                    if which == "q" and kind == "prev":
                        continue
                    if "nos" in DBG and kind == "s":
                        continue
                    ectr[0] += 1
                    pb = ectr[0] % 2
                    ps = banks[pb]
                    for kc in range(16):
                        kb.op("pe", lambda e, ps=ps, xsrc=xsrc, kc=kc, off=off, n=n, W=W: e.matmul(
                            ps[0:n, :], xsrc[:, kc, off:off + n], W[:, kc, :], start=(kc == 0), stop=(kc == 15)),
                            reads=[("xT", id(xsrc), off, kc // 8), wkey], writes=[("bank", pb)])
                    while pend3:
                        pend3.pop(0)()
                    sl = ectr[0] % 2
                    if which in ("k", "v") and kind != "prev" and "nostg" not in DBG:
                        kb.op("act", lambda e, ps=ps, n=n, sl=sl: e.copy(stg[sl][0:n, :], ps[0:n, :]),
                              reads=[("bank", pb)], writes=[("stg", sl)])
                        dst = {("k", "own"): k_own, ("v", "own"): v_own, ("k", "s"): k_s, ("v", "s"): v_s}[(which, kind)]
                        if "nodmaout" not in DBG:
                            kb.dma(os.environ.get("STQ", "sp"), dst[t * 128:t * 128 + n, cg0:cg0 + GC], stg[sl][0:n, :], f"stg{sl}",
                                   reads=[("stg", sl)])
                    if which == "v":
                        if kind == "s":
                            kb.op("dve", lambda e, ps=ps: e.tensor_copy(va_s[0:NS, cg0:cg0 + GC], ps[0:NS, :]),
                                  reads=[("bank", pb)], writes=[("va_s", hp)])
                        else:
                            kb.op("dve", lambda e, ps=ps, ti=ti: e.tensor_copy(vA[:, ti, :], ps[:, :]),
                                  reads=[("bank", pb)], writes=[("vA", ti)])
                        continue
                    if "notr" in DBG:
                        continue
                    kb.op("dve", lambda e, ps=ps, n=n, sl=sl: e.tensor_copy(cbf[sl][0:n, :], ps[0:n, :]),
                          reads=[("bank", pb)], writes=[("cbf", sl)])
                    def fin3(ec=ectr[0], n=n, sl=sl, kind=kind, which=which, ti=ti, t=t, hp=hp):
                        tb = 2 + (ec % 2)
                        pv = bank_bf(tb).rearrange("p (a b) -> p a b", b=128)
                        for j in range(HG):
                            kb.op("pe", lambda e: e.transpose(
                                pv[:, j, 0:n], cbf[sl][0:n, j * 128:(j + 1) * 128], ident[0:n, 0:n]),
                                reads=[("cbf", sl), "ident"], writes=[("bank", tb)])
                        if kind == "s":
                            dstv = (kaT_s if which == "k" else qaT_s)[:, hp * HG:(hp + 1) * HG, :]
                            kb.op("act", lambda e: e.copy(dstv, pv[:, 0:HG, 0:NS]),
                                  reads=[("bank", tb)], writes=[(which + "aT_s", hp)])
                        elif which == "k":
                            kb.op("act", lambda e: e.copy(kT[:, :, ti * 128:(ti + 1) * 128], pv[:, 0:HG, :]),
                                  reads=[("bank", tb)], writes=[("kT", ti)])
                        else:
                            kb.op("act", lambda e: e.copy(qT[:, :, t * 128:(t + 1) * 128], pv[:, 0:HG, :]),
                                  reads=[("bank", tb)], writes=[("qT", t)])
                    pend3.append(fin3)
            while pend3:
                pend3.pop(0)()
            import os
            if os.environ.get("P3_STOP") == "proj":
                continue
            kb.op("dve", lambda e: e.tensor_reduce(ksf, kT.rearrange("p h (n k) -> p h n k", k=256), AX.X, ALU.add),
                  reads=[("kT", ti) for ti in range(16)], writes=["ksf"])
            kb.op("dve", lambda e: e.tensor_copy(ksb, ksf), reads=["ksf"], writes=["ksb"])
            ps = banks[4]
            for t in range(8):
                for hh in range(HG):
                    kb.op("pe", lambda e: e.matmul(
                        ps[:, (t * HG + hh) * 8:(t * HG + hh + 1) * 8], qT[:, hh, t * 128:(t + 1) * 128], ksb[:, hh, :],
                        start=True, stop=True), reads=[("qT", t), "ksb"], writes=[("bank", 4)])
            cmp2 = stg[0].rearrange("p (a b c) -> p a b c", b=8, c=8)
            for c in range(4):
                psv = ps[:, c * 64:(c + 1) * 64].rearrange("p (t h n) -> p t h n", h=HG, n=8)
                kb.op("dve", lambda e: e.tensor_tensor(
                    sm2, psv, bcast(negm[:, 2 * c, :], [[8, 2], [0, HG], [1, 8]]), ALU.add),
                    reads=[("bank", 4), "negm"], writes=["sm"])
                kb.op("dve", lambda e: e.tensor_tensor(
                    cmp2, bcast(sm2, [[8, 2 * HG], [0, 8], [1, 8]]), bcast(sm2, [[8, 2 * HG], [1, 8], [0, 8]]), ALU.is_gt),
                    reads=["sm"], writes=[("stg", 0)])
                kb.op("dve", lambda e: e.tensor_reduce(rank2, cmp2, AX.X, ALU.add), reads=[("stg", 0)], writes=["rank"])
                kb.op("dve", lambda e: e.tensor_scalar(rank2, rank2, 3.0, None, ALU.is_lt), reads=["rank"], writes=["rank"])
                r3 = rank2.rearrange("p (t h) n -> p t h n", h=HG)
                kb.op("dve", lambda e: e.tensor_tensor(
                    r3, r3, bcast(elig[:, 2 * c, :], [[8, 2], [0, HG], [1, 8]]), ALU.mult), reads=["rank", "elig"], writes=["rank"])
                kb.op("dve", lambda e: e.tensor_scalar(
                    bias2.rearrange("p t (h n) -> p (t h) n", n=8), rank2, -1.0, -NEG, ALU.add, ALU.mult),
                    reads=["rank"], writes=["bias_bf"])
                pvb = bank_bf(5)
                for tt in range(2):
                    kb.op("pe", lambda e: e.transpose(pvb[0:HG * 8, tt * 128:(tt + 1) * 128], bias2[:, tt, :], ident),
                          reads=["bias_bf", "ident"], writes=[("bank", 5)])
                kb.op("act", lambda e: e.copy(biasT[0:HG * 8, 2 * c * 128:(2 * c + 2) * 128], pvb[0:HG * 8, 0:256]),
                      reads=[("bank", 5)], writes=[("biasT", 2 * c), ("biasT", 2 * c + 1)])
            if os.environ.get("P3_STOP") == "sel":
                continue
            for hh in range(HG):
                def LG_(kt, hh=hh):
                    ko = kt - 8
                    lo = 0 if kt < 8 else ko * 128
                    n_blk = kt // 2
                    lbase = 2 * (kt % 2)
                    ptb = pt_sb[kt % 2]
                    for ch in range(2):
                        c0, c1 = max(lo, ch * 512), (ch + 1) * 512
                        if c0 >= c1:
                            continue
                        lb = lbase + ch
                        pl = banks[lb]
                        subs = []
                        if kt < 8:
                            subs.append(("sel", c0, c1))
                        else:
                            for tq in range(c0 // 128, c1 // 128):
                                if tq == ko:
                                    subs.append(("diag", tq * 128, (tq + 1) * 128))
                                elif n_blk < 4 + tq // 2:
                                    if subs and subs[-1][0] == "sel" and subs[-1][2] == tq * 128:
                                        subs[-1] = ("sel", subs[-1][1], (tq + 1) * 128)
                                    else:
                                        subs.append(("sel", tq * 128, (tq + 1) * 128))
                        kb.op("pe", lambda e: e.matmul(
                            pl[:, c0 - ch * 512:c1 - ch * 512], kT[:, hh, kt * 128:(kt + 1) * 128], qT[:, hh, c0:c1],
                            start=True, stop=(len(subs) == 0)),
                            reads=[("kT", kt)] + [("qT", tq) for tq in range(c0 // 128, c1 // 128)], writes=[("bank", lb)])
                        for si, (typ, a0, a1) in enumerate(subs):
                            lastb = si == len(subs) - 1
                            if typ == "sel":
                                r = hh * 8 + n_blk
                                kb.op("pe", lambda e: e.matmul(
                                    pl[:, a0 - ch * 512:a1 - ch * 512], Eoh[0:HG * 8, r * 128:(r + 1) * 128],
                                    biasT[0:HG * 8, a0:a1], start=False, stop=lastb),
                                    reads=["Eoh"] + [("biasT", tq) for tq in range(a0 // 128, a1 // 128)],
                                    writes=[("bank", lb)])
                            else:
                                kb.op("pe", lambda e: e.matmul(
                                    pl[:, a0 - ch * 512:a1 - ch * 512], ident, causal, start=False, stop=lastb),
                                    reads=["ident", "causal"], writes=[("bank", lb)])
                        kb.op("act", lambda e: e.activation(ptb[:, c0:c1], pl[:, c0 - ch * 512:c1 - ch * 512], AF.Exp, scale=scale),
                              reads=[("bank", lb)], writes=[("pt", kt % 2, ch)])

                def PV_(kt, hh=hh):
                    lo = 0 if kt < 8 else (kt - 8) * 128
                    ptb = pt_sb[kt % 2]
                    for ch in range(2):
                        c0, c1 = max(lo, ch * 512), (ch + 1) * 512
                        if c0 >= c1:
                            continue
                        kb.op("pe", lambda e: e.matmul(
                            banks[4 + ch][:, c0 - ch * 512:c1 - ch * 512], vA[:, kt, hh * 128:(hh + 1) * 128], ptb[:, c0:c1],
                            start=(kt == 0), stop=(kt == 15)),
                            reads=[("pt", kt % 2, ch), ("vA", kt)], writes=[("bank", 4 + ch)])
                        kb.op("pe", lambda e: e.matmul(
                            banks[6 + ch][:, c0 - ch * 512:c1 - ch * 512], ones_bf, ptb[:, c0:c1],
                            start=(kt == 0), stop=(kt == 15)),
                            reads=[("pt", kt % 2, ch), "ones"], writes=[("bank", 6 + ch)])
                LG_(0)
                for kt in range(16):
                    if kt + 1 < 16:
                        LG_(kt + 1)
                    PV_(kt)
                for ch in range(2):
                    rd = stg[ch]
                    kb.op("dve", lambda e: e.reciprocal(rd, banks[6 + ch][:, :]), reads=[("bank", 6 + ch)], writes=[("stg", ch)])
                    kb.op("dve", lambda e: e.tensor_tensor(
                        oaT[:, hp * HG + hh, ch * 512:(ch + 1) * 512], banks[4 + ch][:, :], rd, ALU.mult),
                        reads=[("bank", 4 + ch), ("stg", ch)], writes=[("oaT", hp * HG + hh, ch)])
        A.pop()
        kb.barrier()

    LG = [float(np.log1p(-2.0 ** (-5.0 - h))) for h in range(H)]

    def phase2():
        A.top = 126 * KI
        A.push()
        ring = Ring("w2", [A.alloc([16, 256], BF16) for _ in range(4)])
        rot = [A.alloc([256], F32) for _ in range(2)]
        kf = [A.alloc([256], F32) for _ in range(2)]
        qf = [A.alloc([256], F32) for _ in range(2)]
        sgf = A.alloc([256], F32)
        ta = A.alloc([256], F32)
        tb_ = A.alloc([256], F32)
        qr = A.alloc([256], F32)
        onb = A.alloc([256], F32)
        k_bf = [A.alloc([256], BF16) for _ in range(2)]
        kp_bf = [A.alloc([256], BF16) for _ in range(2)]
        qp_bf = [A.alloc([256], BF16) for _ in range(2)]
        v_bf = [A.alloc([256], BF16) for _ in range(2)]
        sgn = [A.alloc([256], BF16) for _ in range(2)]
        u_bf = [A.alloc([256], BF16) for _ in range(2)]
        kqT = [A.alloc([4, 128], BF16) for _ in range(2)]
        PT = [A.alloc([128], BF16) for _ in range(2)]
        S_f = A.alloc([2, 256], F32)
        S_b = A.alloc([2, 256], BF16)
        qd = A.alloc([8], F32)
        kd = A.alloc([8], F32)
        maskT = [A.alloc([128], F32) for _ in range(2)]
        gn_h = [A.alloc([256], F32) for _ in range(2)]
        stats = A.alloc([6], F32)
        mv = A.alloc([2], F32)
        sd = A.alloc([2], F32)
        sS_f = [A.alloc([2, 256], F32) for _ in range(4)]
        sS_b = [A.alloc([2, 256], BF16) for _ in range(4)]
        skf = [("sS_f", i) for i in range(4)]
        skb = [("sS_b", i) for i in range(4)]
        sS_o = [A.alloc([2, 256], F32)] * 2
        qm = A.alloc([4, 2, NS], BF16)
        vm = A.alloc([4, 256], BF16)
        sqd = A.alloc([8], F32)
        skd = A.alloc([8], F32)
        smaskT = A.alloc([8, NS], F32)
        bmP = A.alloc([4], F32)
        bmF = A.alloc([4, NS], F32)
        rot_s = A.alloc([256], F32)
        kb.dma("sp", qd, c_qd, "c_qd", writes=["qd"])
        kb.dma("sp", kd, c_kd, "c_kd", writes=["kd"])
        kb.dma("sp", sqd[0:NS, :], c_sqd, "c_sqd", writes=["sqd"])
        kb.dma("sp", skd[0:NS, :], c_skd, "c_skd", writes=["skd"])
        kb.dma("sp", smaskT[0:NS], c_smaskT, "c_smaskT", writes=["smaskT"])
        kb.dma("sp", bmP[0:NS, :], c_bmP, "c_bmP", writes=["bmP"])
        kb.dma("sp", bmF, c_bmF.rearrange("p (b i) -> p b i", i=NS), "c_bmF", writes=["bmF"])
        kb.dma("sp", rot_s[0:NS, :], c_rot_s, "c_rot_s", writes=["rot_s"])

        def rotary(src, rt, dst, n, keys_r, key_w):
            cosb = bcast(rt[0:n, 0:128], [[0, 2], [1, 128]])
            sinb = bcast(rt[0:n, 128:256], [[0, 2], [1, 128]])
            s3 = src[0:n, :].rearrange("p (a b) -> p a b", b=128)
            kb.op("dve", lambda e: e.tensor_tensor(ta[0:n, :].rearrange("p (a b) -> p a b", b=128), s3, cosb, ALU.mult),
                  reads=keys_r, writes=["ta"])
            kb.op("dve", lambda e: e.tensor_tensor(tb_[0:n, :].rearrange("p (a b) -> p a b", b=128), s3, sinb, ALU.mult),
                  reads=keys_r, writes=["tb"])
            kb.op("dve", lambda e: e.tensor_tensor(dst[0:n, 0:128], ta[0:n, 0:128], tb_[0:n, 128:256], ALU.subtract),
                  reads=["ta", "tb"], writes=[key_w])
            kb.op("dve", lambda e: e.tensor_tensor(dst[0:n, 128:256], tb_[0:n, 0:128], ta[0:n, 128:256], ALU.add),
                  reads=["ta", "tb"], writes=[key_w])

        def group_norm_gate(po_ap, n, sg_ap, sgkey, u_ap, ukey, bank_key):
            kb.op("dve", lambda e: e.bn_stats(stats[0:n, :], po_ap), reads=[bank_key], writes=["gstats"])
            kb.op("dve", lambda e: e.bn_aggr(mv[0:n, :], stats[0:n, :]), reads=["gstats"], writes=["gmv"])
            kb.op("act", lambda e: e.activation(sd[0:n, 0:1], mv[0:n, 1:2], AF.Sqrt, bias=1e-5, scale=1.0),
                  reads=["gmv"], writes=["gsd"])
            kb.op("dve", lambda e: e.reciprocal(sd[0:n, 1:2], sd[0:n, 0:1]), reads=["gsd"], writes=["grs"])
            kb.op("dve", lambda e: e.tensor_scalar(onb[0:n, :], po_ap, mv[0:n, 0:1], sd[0:n, 1:2], ALU.subtract, ALU.mult),
                  reads=[bank_key, "gmv", "grs"], writes=["onb"])
            kb.op("dve", lambda e: e.tensor_tensor(u_ap, onb[0:n, :], sg_ap, ALU.mult),
                  reads=["onb", sgkey], writes=[ukey])

        it = 0
        for h in range(H):
            Wk, wkk = ring.load(w_in_v[:, :, C_KR + h * 256:C_KR + (h + 1) * 256], 16, 256)
            Wv, wvk = ring.load(w_in_v[:, :, C_VR + h * 256:C_VR + (h + 1) * 256], 16, 256)
            Wq, wqk = ring.load(w_in_v[:, :, C_QR + h * 256:C_QR + (h + 1) * 256], 16, 256)
            Wg, wgk = ring.load(w_in_v[:, :, C_GR + h * 256:C_GR + (h + 1) * 256], 16, 256)
            mT = maskT[h % 2]
            gnh = gn_h[h % 2]
            kb.dma("sp", mT, c_maskT[:, h, :], f"c_mT{h%2}", writes=[("mT", h % 2)])
            kb.dma("sp", gnh, c_gn[0:1, h * 256:(h + 1) * 256].partition_broadcast(128), f"c_gnh{h%2}",
                   writes=[("gnh", h % 2)])
            g128 = float(np.exp(LG[h] * 128.0))
            g8 = float(np.exp(LG[h] * 8.0))
            kb.op("dve", lambda e: e.memset(S_f, 0.0), writes=["S_f"])
            kb.op("act", lambda e: e.copy(S_b, S_f), reads=["S_f"], writes=["S_b"])
            for b in range(4):
                kb.dma("sp", sS_f[b], state_s[b, h].rearrange("(c p) e -> p c e", p=128), f"sSf{b}", writes=[skf[b]])
                kb.dma("pool", sS_b[b], state_s[b, h].rearrange("(c p) e -> p c e", p=128), f"sSb{b}", writes=[skb[b]])
            tiles = [("prev", t, 128, xTp, t * 128, c_rot_prev) for t in range(8)] + \
                    [("own", t, 128, xT, t * 128, c_rot_own) for t in range(8)]

            class Tile:
                pass

            def make_tile(kind, t, xsrc, off, rsrc, sl):
                T = Tile()
                own = kind == "own"
                rt = rot[sl]
                bA, bB = banks[sl], banks[2 + sl]
                pv4 = bank_bf(4).rearrange("p (a b) -> p a b", b=128)
                kq = kqT[sl]
                pt = PT[sl]
                ub = u_bf[sl]

                def proj(W, wk, dst, bk):
                    for kc in range(16):
                        kb.op("pe", lambda e: e.matmul(dst, xsrc[:, kc, off:off + 128], W[:, kc, :],
                                                       start=(kc == 0), stop=(kc == 15)), reads=[wk], writes=[bk])

                def A_kv():
                    kb.dma("sp", rt, rsrc[t], f"rot{sl}", writes=[("rot", sl)])
                    proj(Wk, wkk, bA[:, 0:256], ("bank", sl))
                    proj(Wv, wvk, bA[:, 256:512], ("bank", sl))
                    kb.op("act", lambda e: e.copy(kf[sl], bA[:, 0:256]), reads=[("bank", sl)], writes=[("kf", sl)])
                    kb.op("act", lambda e: e.copy(v_bf[sl], bA[:, 256:512]), reads=[("bank", sl)], writes=[("v_bf", sl)])
                    rotary(kf[sl], rt, k_bf[sl], 128, [("kf", sl), ("rot", sl)], ("k_bf", sl))
                    kb.op("dve", lambda e: e.tensor_scalar(kp_bf[sl], k_bf[sl], kd[:, h:h + 1], None, ALU.mult),
                          reads=[("k_bf", sl), "kd"], writes=[("kp_bf", sl)])

                def A_q():
                    if own:
                        proj(Wq, wqk, bB[:, 0:256], ("bank", 2 + sl))

                def A_g():
                    if not own:
                        return
                    proj(Wg, wgk, bB[:, 256:512], ("bank", 2 + sl))
                    kb.op("act", lambda e: e.copy(qf[sl], bB[:, 0:256]), reads=[("bank", 2 + sl)], writes=[("qf", sl)])
                    kb.op("act", lambda e: e.activation(sgf, bB[:, 256:512], AF.Silu), reads=[("bank", 2 + sl)], writes=["sgf"])
                    rotary(qf[sl], rt, qr, 128, [("qf", sl), ("rot", sl)], "qr")
                    kb.op("dve", lambda e: e.tensor_scalar(qp_bf[sl], qr, qd[:, h:h + 1], None, ALU.mult),
                          reads=["qr", "qd"], writes=[("qp_bf", sl)])
                    kb.op("dve", lambda e: e.tensor_tensor(sgn[sl], sgf, gnh, ALU.mult),
                          reads=["sgf", ("gnh", h % 2)], writes=[("sgn", sl)])

                def B1():
                    if not own:
                        return
                    for c in range(2):
                        kb.op("pe", lambda e: e.transpose(pv4[:, c, :], k_bf[sl][:, c * 128:(c + 1) * 128], ident),
                              reads=[("k_bf", sl), "ident"], writes=[("bank", 4)])
                    for c in range(2):
                        kb.op("pe", lambda e: e.transpose(pv4[:, 2 + c, :], qp_bf[sl][:, c * 128:(c + 1) * 128], ident),
                              reads=[("qp_bf", sl), "ident"], writes=[("bank", 4)])
                    kb.op("act", lambda e: e.copy(kq, pv4[:, 0:4, :]), reads=[("bank", 4)], writes=[("kqT", sl)])

                def B2():
                    if not own:
                        return
                    for c in range(2):
                        kb.op("pe", lambda e: e.matmul(banks[6][:, 0:128], kq[:, c, :], kq[:, 2 + c, :],
                                                       start=(c == 0), stop=(c == 1)),
                              reads=[("kqT", sl)], writes=[("bank", 6)])
                    kb.op("dve", lambda e: e.tensor_tensor(pt, banks[6][:, 0:128], mT, ALU.mult),
                          reads=[("bank", 6), ("mT", h % 2)], writes=[("PT", sl)])

                def B3():
                    for c in range(2):
                        kb.op("pe", lambda e: e.matmul(banks[5][:, c * 256:(c + 1) * 256],
                                                       kp_bf[sl][:, c * 128:(c + 1) * 128], v_bf[sl], start=True, stop=True),
                              reads=[("kp_bf", sl), ("v_bf", sl)], writes=[("bank", 5)])
                    if own:
                        kb.op("pe", lambda e: e.matmul(banks[7][:, 0:256], pt, v_bf[sl], start=True, stop=False),
                              reads=[("PT", sl), ("v_bf", sl)], writes=[("bank", 7)])
                        for c in range(2):
                            kb.op("pe", lambda e: e.matmul(banks[7][:, 0:256], kq[:, 2 + c, :], S_b[:, c, :],
                                                           start=False, stop=(c == 1)),
                                  reads=[("kqT", sl), "S_b"], writes=[("bank", 7)])
                    kb.op("dve", lambda e: e.scalar_tensor_tensor(
                        S_f.rearrange("p a b -> p (a b)"), S_f.rearrange("p a b -> p (a b)"), g128, banks[5][:, :],
                        ALU.mult, ALU.add), reads=[("bank", 5), "S_f"], writes=["S_f"])
                    kb.op("act", lambda e: e.copy(S_b, S_f), reads=["S_f"], writes=["S_b"])
                    if own:
                        group_norm_gate(banks[7][:, 0:256], 128, sgn[sl], ("sgn", sl), ub, ("u_bf", sl), ("bank", 7))

                def B4():
                    if not own:
                        return
                    for c in range(2):
                        kb.op("pe", lambda e: e.transpose(pv4[:, 4 + c, :], ub[:, c * 128:(c + 1) * 128], ident),
                              reads=[("u_bf", sl), "ident"], writes=[("bank", 4)])
                    kb.op("act", lambda e: e.copy(uT[:, h * 2:h * 2 + 2, t * 128:(t + 1) * 128], pv4[:, 4:6, :]),
                          reads=[("bank", 4)], writes=[("uT", t, h)])

                T.A_kv, T.A_q, T.A_g, T.B1, T.B2, T.B3, T.B4 = A_kv, A_q, A_g, B1, B2, B3, B4
                return T

            n = NS
            s_kf, s_qf = qf[1], qf[0]
            s_vbf, s_kbf, s_kpbf, s_qpbf, s_sgn, s_ubf = sgn[1], qp_bf[1], u_bf[1], qp_bf[0], sgn[0], u_bf[0]
            K_kf, K_qf, K_vbf, K_kbf, K_kpbf, K_qpbf, K_sgn, K_ubf = ("qf", 1), ("qf", 0), ("sgn", 1), ("qp_bf", 1), \
                ("u_bf", 1), ("qp_bf", 0), ("sgn", 0), ("u_bf", 0)
            pv4s = bank_bf(4).rearrange("p (a b) -> p a b", b=128)
            kqs = kqT[0]

            def S0():
                bA, bB = banks[2], banks[3]
                for (W, wk, dst, bk) in ((Wk, wkk, bA[0:n, 0:256], ("bank", 2)), (Wv, wvk, bA[0:n, 256:512], ("bank", 2)),
                                         (Wq, wqk, bB[0:n, 0:256], ("bank", 3)), (Wg, wgk, bB[0:n, 256:512], ("bank", 3))):
                    for kc in range(16):
                        kb.op("pe", lambda e: e.matmul(dst, xT[:, kc, OWN:OWN + NS], W[:, kc, :], start=(kc == 0), stop=(kc == 15)),
                              reads=[wk], writes=[bk])
                kb.op("act", lambda e: e.copy(s_kf[0:n, :], bA[0:n, 0:256]), reads=[("bank", 2)], writes=[K_kf])
                kb.op("act", lambda e: e.copy(s_vbf[0:n, :], bA[0:n, 256:512]), reads=[("bank", 2)], writes=[K_vbf])
                kb.op("act", lambda e: e.copy(s_qf[0:n, :], bB[0:n, 0:256]), reads=[("bank", 3)], writes=[K_qf])
                kb.op("act", lambda e: e.activation(sgf[0:n, :], bB[0:n, 256:512], AF.Silu), reads=[("bank", 3)], writes=["sgf"])
                rotary(s_kf, rot_s, s_kbf, n, [K_kf, "rot_s"], K_kbf)
                kb.op("dve", lambda e: e.tensor_scalar(s_kpbf[0:n, :], s_kbf[0:n, :], skd[0:n, h:h + 1], None, ALU.mult),
                      reads=[K_kbf, "skd"], writes=[K_kpbf])
                rotary(s_qf, rot_s, qr, n, [K_qf, "rot_s"], "qr")
                kb.op("dve", lambda e: e.tensor_scalar(s_qpbf[0:n, :], qr[0:n, :], sqd[0:n, h:h + 1], None, ALU.mult),
                      reads=["qr", "sqd"], writes=[K_qpbf])
                kb.op("dve", lambda e: e.tensor_tensor(s_sgn[0:n, :], sgf[0:n, :], gnh[0:n, :], ALU.mult),
                      reads=["sgf", ("gnh", h % 2)], writes=[K_sgn])

            def S1():
                for c in range(2):
                    kb.op("pe", lambda e: e.transpose(pv4s[:, c, 0:n], s_kbf[0:n, c * 128:(c + 1) * 128], ident[0:n, 0:n]),
                          reads=[K_kbf, "ident"], writes=[("bank", 4)])
                for c in range(2):
                    kb.op("pe", lambda e: e.transpose(pv4s[:, 2 + c, 0:n], s_qpbf[0:n, c * 128:(c + 1) * 128], ident[0:n, 0:n]),
                          reads=[K_qpbf, "ident"], writes=[("bank", 4)])
                kb.op("act", lambda e: e.copy(kqs[:, :, 0:n], pv4s[:, 0:4, 0:n]), reads=[("bank", 4)], writes=[("kqT", 0)])
                kb.op("dve", lambda e: e.tensor_tensor(
                    qm, bcast(kqs[:, 2:4, 0:n], [[0, 4], [128, 2], [1, n]]), bcast(bmF, [[n, 4], [0, 2], [1, n]]), ALU.mult),
                    reads=[("kqT", 0), "bmF"], writes=["qm"])
                kb.op("dve", lambda e: e.tensor_tensor(
                    vm[0:n], bcast(s_vbf[0:n, :], [[0, 4], [1, 256]]), bcast(bmP[0:n, :], [[1, 4], [0, 256]]), ALU.mult),
                    reads=[K_vbf, "bmP"], writes=["vm"])

            def S2():
                for c in range(2):
                    kb.op("pe", lambda e: e.matmul(banks[6][0:n, 0:n], kqs[:, c, 0:n], kqs[:, 2 + c, 0:n],
                                                   start=(c == 0), stop=(c == 1)),
                          reads=[("kqT", 0)], writes=[("bank", 6)])
                kb.op("dve", lambda e: e.tensor_tensor(PT[0][0:n, 0:n], banks[6][0:n, 0:n], smaskT[0:n, h, :], ALU.mult),
                      reads=[("bank", 6), "smaskT"], writes=[("PT", 0)])

            def S3():
                kb.op("pe", lambda e: e.matmul(banks[7][0:n, 0:256], PT[0][0:n, 0:n], s_vbf[0:n, :], start=True, stop=False),
                      reads=[("PT", 0), K_vbf], writes=[("bank", 7)])
                for b in range(4):
                    for c in range(2):
                        kb.op("pe", lambda e: e.matmul(
                            banks[7][0:n, 0:256], qm[:, b, c, :], sS_b[b][:, c, :], start=False, stop=(b == 3 and c == 1)),
                            reads=["qm", skb[b]], writes=[("bank", 7)])

            def S4(bs):
                for b in bs:
                    for c in range(2):
                        kb.op("pe", lambda e: e.matmul(
                            banks[5][:, c * 256:(c + 1) * 256], s_kpbf[0:n, c * 128:(c + 1) * 128], vm[0:n, b, :],
                            start=True, stop=True),
                            reads=[K_kpbf, "vm"], writes=[("bank", 5)])
                    kb.op("dve", lambda e: e.scalar_tensor_tensor(
                        sS_o[0].rearrange("p a b -> p (a b)"), sS_f[b].rearrange("p a b -> p (a b)"), g8, banks[5][:, :],
                        ALU.mult, ALU.add),
                        reads=[("bank", 5), skf[b]], writes=[("sS_o", 0)])
                    kb.dma("pool", ret_s[b, h].rearrange("(c p) e -> p c e", p=128), sS_o[0], "sSo0", reads=[("sS_o", 0)])

            def S5():
                group_norm_gate(banks[7][0:n, 0:256], n, s_sgn[0:n, :], K_sgn, s_ubf[0:n, :], K_ubf, ("bank", 7))

            def S6():
                for c in range(2):
                    kb.op("pe", lambda e: e.transpose(pv4s[:, 4 + c, 0:n], s_ubf[0:n, c * 128:(c + 1) * 128], ident[0:n, 0:n]),
                          reads=[K_ubf, "ident"], writes=[("bank", 4)])
                kb.op("act", lambda e: e.copy(uT[:, h * 2:h * 2 + 2, OWN:OWN + NS], pv4s[:, 4:6, 0:n]),
                      reads=[("bank", 4)], writes=[("uT", 8, h)])

            SST = [S0, S1, S2, S3, lambda: S4([0, 1]), lambda: S4([2, 3]), lambda: (S5(), S6())]
            tl = []
            for (kind, t, n_, xsrc, off, rsrc) in tiles:
                it += 1
                tl.append(make_tile(kind, t, xsrc, off, rsrc, it % 2))
            tl[0].A_kv(); tl[0].A_q(); tl[0].A_g()
            for i in range(16):
                nxt = tl[i + 1] if i + 1 < 16 else None
                if nxt:
                    nxt.A_kv()
                tl[i].B1()
                if nxt:
                    nxt.A_q()
                tl[i].B2()
                if nxt:
                    nxt.A_g()
                if i >= 1:
                    tl[i - 1].B4()
                tl[i].B3()
                if i < len(SST):
                    SST[i]()
            tl[15].B4()
            kb.dma("pool", ret_p[h].rearrange("(c p) e -> p c e", p=128), S_f, "retp", reads=["S_f"])

        A.pop()
        kb.barrier()

    def phase6(qaT_s, kaT_s, va_s):
        A.top = 126 * KI
        A.push()
        scale = HD ** -0.5
        NPB = 6
        Kpg = [A.alloc([1024], BF16) for _ in range(NPB)]
        Vpg = [A.alloc([1024], BF16) for _ in range(NPB)]
        KT = [A.alloc([8, 128], BF16) for _ in range(2)]
        PTb = [A.alloc([NPG, 64], BF16) for _ in range(2)]
        pti = A.alloc([4 * NPG], I32)
        ptf = A.alloc([4 * NPG], F32)
        iot = A.alloc([1], F32)
        idx = A.alloc([4 * NPG], I32)
        ksm = A.alloc([8, 32], F32)
        kmb = A.alloc([8, 32], BF16)
        Qm = A.alloc([8, 64], BF16)
        sm = A.alloc([32], F32)
        cmpb = A.alloc([32, 32], F32)
        rank = A.alloc([32], F32)
        Dm = A.alloc([32, 64], BF16)
        selrep = A.alloc([32, 64], BF16)
        hmask = A.alloc([1024], BF16)
        i64 = A.alloc([64], F32)
        sown = A.alloc([4, 64], F32)
        pown = A.alloc([64], F32)
        PTo = A.alloc([64], BF16)
        tmp = A.alloc([1024], F32)
        osel = A.alloc([128], F32)
        rdn = A.alloc([1], F32)
        o_bf = A.alloc([128], BF16)
        kb.dma("sp", pti, ptab.partition_broadcast(128), "c_pt", writes=["pti"])
        kb.dma("pool", hmask[0:64, :], c_hmask, "c_hmask", writes=["hmask"], max_dma_last_dim=4096)
        kb.dma("sp", i64[0:64, :], c_ident64, "c_i64", writes=["i64"])
        kb.dma("sp", sown[0:NS], c_sown.rearrange("p (b c) -> p b c", c=64), "c_sown", writes=["sown"])
        kb.op("pool", lambda e: e.iota(iot, [[0, 1]], base=0, channel_multiplier=1, allow_small_or_imprecise_dtypes=True),
              writes=["iot"])
        kb.op("dve", lambda e: e.tensor_copy(ptf, pti), reads=["pti"], writes=["ptf"])
        kb.op("dve", lambda e: e.tensor_scalar(idx, ptf, 128.0, iot[:, 0:1], ALU.mult, ALU.add),
              reads=["ptf", "iot"], writes=["idx"])
        kb.op("dve", lambda e: e.memset(Qm, 0.0), writes=["Qm"])
        ig = 0
        for b in range(4):
            PT = PTb[b % 2]
            ptk = ("PTb", b % 2)
            kb.op("dve", lambda e: e.tensor_copy(
                bcast(Qm, [[64 + 8, 8], [1, 8]]), qaT_s[:, :, b * 8:(b + 1) * 8]),
                reads=[("qaT_s", 0), ("qaT_s", 1)], writes=["Qm"])
            pinfo = {}

            def T1(pg):
                nonlocal ig
                ig += 1
                ks = ig % NPB
                kp = Kpg[ks]
                j = b * NPG + pg
                kb.dma_fn("pool", lambda e: e.indirect_dma_start(
                    out=kp, out_offset=None, in_=cache_k,
                    in_offset=bass.IndirectOffsetOnAxis(ap=idx[:, j:j + 1], axis=0)),
                    f"kpg{ks}", reads=["idx"], writes=[("Kpg", ks)])
                tbk = ig % 2
                pvT = bank_bf(tbk).rearrange("p (a b) -> p a b", b=128)
                for h in range(8):
                    kb.op("pe", lambda e: e.transpose(pvT[:, h, :], kp[:, h * 128:(h + 1) * 128], ident),
                          reads=[("Kpg", ks), "ident"], writes=[("bank", tbk)])
                kt = KT[ig % 2]
                if ig % 2 == 0:
                    kb.op("act", lambda e: e.copy(kt, pvT), reads=[("bank", tbk)], writes=[("KT", ig % 2)])
                else:
                    kb.op("dve", lambda e: e.tensor_copy(kt, pvT), reads=[("bank", tbk)], writes=[("KT", ig % 2)])
                pinfo[pg] = (ks, kp, kt, ig % 2)

            def L1(pg):
                ks, kp, kt, par = pinfo[pg]
                for h in range(8):
                    kb.op("pe", lambda e: e.matmul(banks[2][:, h * 64 + pg:h * 64 + pg + 1],
                                                   kp[:, h * 128:(h + 1) * 128], ones_bf[:, 0:1], start=True, stop=True),
                          reads=[("Kpg", ks), "ones"], writes=[("bank", 2)])
                for h in range(8):
                    kb.op("pe", lambda e: e.matmul(banks[3][:, h * 8:(h + 1) * 8], kt[:, h, :],
                                                   qaT_s[:, h, b * 8:(b + 1) * 8], start=True, stop=True),
                          reads=[("KT", par)], writes=[("bank", 3)])
                kb.op("act", lambda e: e.activation(PT[:, pg, :], banks[3][:, 0:64], AF.Exp, scale=scale),
                      reads=[("bank", 3)], writes=[ptk])

            T1(0)
            for pg in range(NPG):
                if pg + 1 < NPG:
                    T1(pg + 1)
                L1(pg)
            kb.op("dve", lambda e: e.tensor_reduce(
                ksm, banks[2][:, :].rearrange("p (h n t) -> p h n t", n=32, t=2), AX.X, ALU.add),
                reads=[("bank", 2)], writes=["ksm"])
            kb.op("dve", lambda e: e.tensor_copy(kmb, ksm), reads=["ksm"], writes=["kmb"])
            for h in range(8):
                kb.op("pe", lambda e, h=h: e.matmul(banks[7][0:64, 0:32], Qm[:, h, :], kmb[:, h, :],
                                                    start=(h == 0), stop=(h == 7)),
                      reads=["Qm", "kmb"], writes=[("bank", 7)])
            kb.op("dve", lambda e: e.tensor_copy(sm[0:64, :], banks[7][0:64, 0:32]), reads=[("bank", 7)], writes=["sm6"])
            kb.op("dve", lambda e: e.tensor_tensor(
                cmpb[0:64], bcast(sm[0:64, :], [[0, 32], [1, 32]]), bcast(sm[0:64, :], [[1, 32], [0, 32]]), ALU.is_gt),
                reads=["sm6"], writes=["cmp6"])
            kb.op("dve", lambda e: e.tensor_reduce(rank[0:64, :], cmpb[0:64], AX.X, ALU.add), reads=["cmp6"], writes=["rank6"])
            kb.op("dve", lambda e: e.tensor_scalar(rank[0:64, :], rank[0:64, :], 3.0, None, ALU.is_lt),
                  reads=["rank6"], writes=["rank6"])
            kb.op("dve", lambda e: e.tensor_tensor(
                Dm[0:64], bcast(rank[0:64, :], [[1, 32], [0, 64]]), bcast(i64[0:64, :], [[0, 32], [1, 64]]), ALU.mult),
                reads=["rank6", "i64"], writes=["Dm"])
            Dflat = Dm[0:64].rearrange("p a b -> p (a b)")
            for c4 in range(4):
                kb.op("pe", lambda e, c4=c4: e.matmul(banks[4 + c4][:, :], ones_bf[0:64, :], Dflat[:, c4 * 512:(c4 + 1) * 512],
                                                      start=True, stop=True),
                      reads=["Dm", "ones"], writes=[("bank", 4 + c4)])
                kb.op("act" if c4 % 2 else "dve", (lambda e, c4=c4: e.copy(
                    selrep.rearrange("p a b -> p (a b)")[:, c4 * 512:(c4 + 1) * 512], banks[4 + c4][:, :])) if c4 % 2 else
                    (lambda e, c4=c4: e.tensor_copy(
                        selrep.rearrange("p a b -> p (a b)")[:, c4 * 512:(c4 + 1) * 512], banks[4 + c4][:, :])),
                    reads=[("bank", 4 + c4)], writes=[("selrep", c4)])
            kb.op("dve", lambda e: e.tensor_tensor(
                PT.rearrange("p (n t) c -> p n t c", t=2), PT.rearrange("p (n t) c -> p n t c", t=2),
                bcast(selrep, [[64, 32], [0, 2], [1, 64]]), ALU.mult),
                reads=[ptk] + [("selrep", c4) for c4 in range(4)], writes=[ptk])
            for h in range(8):
                kb.op("pe", lambda e, h=h: e.matmul(banks[7][0:NS, 64 + h * 8:64 + (h + 1) * 8], kaT_s[:, h, :],
                                                    qaT_s[:, h, b * 8:(b + 1) * 8], start=True, stop=True),
                      reads=[("kaT_s", 0), ("kaT_s", 1)], writes=[("bank", 7)])
            kb.op("act", lambda e: e.activation(pown[0:NS, :], banks[7][0:NS, 64:128], AF.Exp, scale=scale),
                  reads=[("bank", 7)], writes=["pown"])
            kb.op("dve", lambda e: e.tensor_tensor(PTo[0:NS, :], pown[0:NS, :], sown[0:NS, b, :], ALU.mult),
                  reads=["pown", "sown"], writes=["PTo"])
            for pg in range(NPG):
                ig += 1
                vs = ig % NPB
                vp = Vpg[vs]
                j = b * NPG + pg
                kb.dma_fn("pool", lambda e, vp=vp, j=j: e.indirect_dma_start(
                    out=vp, out_offset=None, in_=cache_v,
                    in_offset=bass.IndirectOffsetOnAxis(ap=idx[:, j:j + 1], axis=0)),
                    f"vpg{vs}", reads=["idx"], writes=[("Vpg", vs)])
                for c2 in range(2):
                    kb.op("pe", lambda e, c2=c2: e.matmul(banks[4 + c2][0:64, :], PT[:, pg, :], vp[:, c2 * 512:(c2 + 1) * 512],
                                                          start=(pg == 0), stop=False),
                          reads=[ptk, ("Vpg", vs)], writes=[("bank", 4 + c2)])
                kb.op("pe", lambda e: e.matmul(banks[6][0:64, 0:1], PT[:, pg, :], ones_bf[:, 0:1], start=(pg == 0), stop=False),
                      reads=[ptk, "ones"], writes=[("bank", 6)])
            for c2 in range(2):
                kb.op("pe", lambda e, c2=c2: e.matmul(banks[4 + c2][0:64, :], PTo[0:NS, :], va_s[0:NS, c2 * 512:(c2 + 1) * 512],
                                                      start=False, stop=True),
                      reads=["PTo", ("va_s", 0), ("va_s", 1)], writes=[("bank", 4 + c2)])
            kb.op("pe", lambda e: e.matmul(banks[6][0:64, 0:1], PTo[0:NS, :], ones_bf[0:NS, 0:1], start=False, stop=True),
                  reads=["PTo", "ones"], writes=[("bank", 6)])
            for c2 in range(2):
                kb.op("dve", lambda e, c2=c2: e.tensor_tensor(tmp[0:64, c2 * 512:(c2 + 1) * 512], banks[4 + c2][0:64, :],
                                                              hmask[0:64, c2 * 512:(c2 + 1) * 512], ALU.mult),
                      reads=[("bank", 4 + c2), "hmask"], writes=[("tmp6", c2)])
            kb.op("dve", lambda e: e.tensor_reduce(osel[0:64, :], tmp[0:64, :].rearrange("p (h d) -> p d h", d=128), AX.X, ALU.add),
                  reads=[("tmp6", 0), ("tmp6", 1)], writes=["osel"])
            kb.op("dve", lambda e: e.reciprocal(rdn[0:64, :], banks[6][0:64, 0:1]), reads=[("bank", 6)], writes=["rdn"])
            kb.op("dve", lambda e: e.tensor_scalar(o_bf[0:64, :], osel[0:64, :], rdn[0:64, 0:1], None, ALU.mult),
                  reads=["osel", "rdn"], writes=["o_bf6"])
            pvo = bank_bf(7)
            kb.op("pe", lambda e: e.transpose(pvo[:, 512:576], o_bf[0:64, :], ident[0:64, 0:64]),
                  reads=["o_bf6", "ident"], writes=[("bank", 7)])
            kb.op("act", lambda e: e.copy(oaT[:, :, OWN + b * 8:OWN + (b + 1) * 8],
                                          pvo[:, 512:576].rearrange("p (h q) -> p h q", q=8)),
                  reads=[("bank", 7)], writes=[("oaT_s", b)])
        A.pop()
        kb.barrier()

    def phase6_stub(qaT_s, kaT_s, va_s):
        kb.op("dve", lambda e: e.memset(oaT[:, :, OWN:OWN + NS], 0.0), writes=[("oaT_s",)])
        kb.barrier()

    TOKT = [(t, 128, t * 128) for t in range(8)] + [(8, NS, OWN)]
    mixT = xTp
    hT = xT
    racc = A.at(72 * KI, [9, D], F32)
    w_ret_v = w_ret.rearrange("(kc p) n -> p kc n", p=128)
    w_att_v = w_att.rearrange("(kc p) n -> p kc n", p=128)
    w_out_v = w_out.rearrange("(kc p) n -> p kc n", p=128)
    w_up_v = w_up.rearrange("(kc p) n -> p kc n", p=128)
    w_down_v = w_down.rearrange("(fc p) n -> p fc n", p=128)

    def layer_norm_tile(t, n, lng, lnb, scr):
        stats, mv, sd = scr
        r = racc[0:n, t, :]
        for c4 in range(4):
            kb.op("dve", lambda e, c4=c4, r=r: e.bn_stats(stats[0:n, c4, :], r[:, c4 * 512:(c4 + 1) * 512]),
                  reads=[("racc", t)], writes=["ln_stats"])
        kb.op("dve", lambda e: e.bn_aggr(mv[0:n, :], stats[0:n, :, :].rearrange("p a b -> p (a b)")),
              reads=["ln_stats"], writes=["ln_mv"])
        kb.op("act", lambda e: e.activation(sd[0:n, 0:1], mv[0:n, 1:2], AF.Sqrt, bias=1e-5, scale=1.0),
              reads=["ln_mv"], writes=["ln_sd"])
        kb.op("dve", lambda e: e.reciprocal(sd[0:n, 1:2], sd[0:n, 0:1]), reads=["ln_sd"], writes=["ln_rs"])
        kb.op("dve", lambda e, r=r: e.scalar_tensor_tensor(r, r, mv[0:n, 0:1], lng[0:n, :], ALU.subtract, ALU.mult),
              reads=[("racc", t), "ln_mv", "lnp"], writes=[("racc", t)])
        kb.op("dve", lambda e, r=r: e.scalar_tensor_tensor(r, r, sd[0:n, 1:2], lnb[0:n, :], ALU.mult, ALU.add),
              reads=[("racc", t), "ln_rs", "lnp"], writes=[("racc", t)])

    def phase4a():
        A.top = 126 * KI
        A.push()
        GW = 256
        Wgr = [A.alloc([16, GW], BF16) for _ in range(2)]
        Wga = [A.alloc([16, GW], BF16) for _ in range(2)]
        Wr = [A.alloc([16, GW], BF16) for _ in range(2)]
        Wa = [A.alloc([8, GW], BF16) for _ in range(2)]
        sg = [A.alloc([2 * GW], F32) for _ in range(2)]
        prod = A.alloc([2 * GW], F32)
        mixb = [A.alloc([GW], BF16) for _ in range(2)]
        it = 0
        pending = []
        for g in range(D // GW):
            cs = slice(g * GW, (g + 1) * GW)
            s2 = g % 2
            kb.dma("pool", Wgr[s2], w_in_v[:, :, C_GBR + g * GW:C_GBR + (g + 1) * GW], f"Wgr{s2}", writes=[("Wgr", s2)])
            kb.dma("pool", Wga[s2], w_in_v[:, :, C_GBA + g * GW:C_GBA + (g + 1) * GW], f"Wga{s2}", writes=[("Wga", s2)])
            kb.dma("pool", Wr[s2], w_ret_v[:, :, cs], f"Wr{s2}", writes=[("Wr", s2)])
            kb.dma("pool", Wa[s2], w_att_v[:, :, cs], f"Wa{s2}", writes=[("Wa", s2)])
            for (t, n, off) in TOKT:
                it += 1
                bx, by = (it % 2) * 2, (it % 2) * 2 + 1
                _run_pending_after_mm = True
                for (bi, c0, src, W, wk, kcs) in ((bx, 0, xT, Wgr[s2], ("Wgr", s2), 16), (bx, GW, xT, Wga[s2], ("Wga", s2), 16),
                                                  (by, 0, uT, Wr[s2], ("Wr", s2), 16), (by, GW, oaT, Wa[s2], ("Wa", s2), 8)):
                    for kc in range(kcs):
                        kb.op("pe", lambda e: e.matmul(
                            banks[bi][0:n, c0:c0 + GW], src[:, kc, off:off + n], W[:, kc, :], start=(kc == 0), stop=(kc == kcs - 1)),
                            reads=[wk], writes=[("bank", bi)])
                while pending:
                    pending.pop(0)()
                sgt = sg[it % 2]
                kb.op("act", lambda e: e.activation(sgt[0:n, :], banks[bx][0:n, :], AF.Sigmoid),
                      reads=[("bank", bx)], writes=[("sg", it % 2)])
                kb.op("dve", lambda e: e.tensor_tensor(prod[0:n, :], sgt[0:n, :], banks[by][0:n, :], ALU.mult),
                      reads=[("sg", it % 2), ("bank", by)], writes=["prod"])
                mb = mixb[it % 2]
                kb.op("dve", lambda e: e.tensor_tensor(mb[0:n, :], prod[0:n, 0:GW], prod[0:n, GW:2 * GW], ALU.add),
                      reads=["prod"], writes=[("mixb", it % 2)])
                def fin(it=it, mb=mb, n=n, off=off, g=g, t=t):
                    tb = 4 + it % 2
                    pv = bank_bf(tb).rearrange("p (a b) -> p a b", b=128)
                    for j in range(2):
                        kb.op("pe", lambda e: e.transpose(pv[:, j, 0:n], mb[0:n, j * 128:(j + 1) * 128], ident[0:n, 0:n]),
                              reads=[("mixb", it % 2), "ident"], writes=[("bank", tb)])
                    kb.op("act", lambda e: e.copy(mixT[:, g * 2:(g + 1) * 2, off:off + n], pv[:, 0:2, 0:n]),
                          reads=[("bank", tb)], writes=[("mixT", t, g)])
                pending.append(fin)
        while pending:
            pending.pop(0)()
        A.pop()
        kb.barrier()

    def phase4b():
        A.top = 144 * KI
        A.push()
        ring = Ring("w4b", [A.alloc([16, 512], BF16) for _ in range(2)])
        lng = A.alloc([D], F32)
        lnb = A.alloc([D], F32)
        xs = [A.alloc([512], F32) for _ in range(2)]
        hb = A.alloc([D], BF16)
        stats = A.alloc([4, 6], F32)
        mv = A.alloc([2], F32)
        sd = A.alloc([2], F32)
        kb.dma("sp", lng, c_ln[0:1, :].partition_broadcast(128), "c_lng", writes=["lnp"])
        kb.dma("sp", lnb, c_ln[1:2, :].partition_broadcast(128), "c_lnb", writes=["lnp"])
        kb.op("act", lambda e: e.activation(lng, lng, AF.Copy, scale=ALPHA), reads=["lnp"], writes=["lnp"])
        kb.op("act", lambda e: e.activation(lnb, lnb, AF.Copy, scale=ALPHA), reads=["lnp"], writes=["lnp"])
        it = 0
        for g in range(4):
            cs = slice(g * 512, (g + 1) * 512)
            W, wkey = ring.load(w_out_v[:, :, cs], 16, 512)
            for (t, n, off) in TOKT:
                it += 1
                pb = it % 2
                for kc in range(16):
                    kb.op("pe", lambda e, pb=pb, kc=kc, n=n, off=off, W=W: e.matmul(
                        banks[pb][0:n, :], mixT[:, kc, off:off + n], W[:, kc, :], start=(kc == 0), stop=(kc == 15)),
                        reads=[wkey], writes=[("bank", pb)])
                xsl = xs[it % 2]
                src = x_own[t * 128:t * 128 + n, cs] if t < 8 else x_s[0:n, cs]
                kb.dma("sp", xsl[0:n, :], src, f"xs{it%2}", writes=[("xs", it % 2)])
                kb.op("dve", lambda e, xsl=xsl, n=n, pb=pb, t=t, cs=cs: e.scalar_tensor_tensor(
                    racc[0:n, t, cs], xsl[0:n, :], ALPHA, banks[pb][0:n, :], ALU.mult, ALU.add),
                    reads=[("xs", it % 2), ("bank", pb)], writes=[("racc", t)])
        for (t, n, off) in TOKT:
            layer_norm_tile(t, n, lng, lnb, (stats, mv, sd))
            kb.op("act", lambda e, n=n, t=t: e.activation(hb[0:n, :], racc[0:n, t, :], AF.Copy, scale=1.0 / ALPHA),
                  reads=[("racc", t)], writes=["hb"])
            for half in range(2):
                tb = 2 + half
                pv = bank_bf(tb).rearrange("p (a b) -> p a b", b=128)
                for j in range(8):
                    kc = half * 8 + j
                    kb.op("pe", lambda e, pv=pv, j=j, kc=kc, n=n: e.transpose(
                        pv[:, j, 0:n], hb[0:n, kc * 128:(kc + 1) * 128], ident[0:n, 0:n]),
                        reads=["hb", "ident"], writes=[("bank", tb)])
                kb.op("act", lambda e, pv=pv, half=half, off=off, n=n: e.copy(
                    hT[:, half * 8:half * 8 + 8, off:off + n], pv[:, :, 0:n]),
                    reads=[("bank", tb)], writes=[("hT", t, half)])
        A.pop()
        kb.barrier()

    def phase5():
        A.top = 144 * KI
        A.push()
        ring_up = Ring("wup", [A.at(36 * KI, [16, 512], BF16), A.at(52 * KI, [16, 512], BF16)])
        ring_dn = Ring("wdn", [A.alloc([4, D], BF16) for _ in range(2)])
        aT = [A.alloc([4, NTOK], BF16) for _ in range(2)]
        rl = [A.at(68 * KI, [512], F32), A.at(70 * KI, [512], F32)]
        TG = [(0, 512), (512, 512), (1024, NS)]
        iu = 0
        idn = 0
        for fg in range(16):
            Wu, uk = ring_up.load(w_up_v[:, :, fg * 512:(fg + 1) * 512], 16, 512)
            Wd, dk = ring_dn.load(w_down_v[:, fg * 4:(fg + 1) * 4, :], 4, D)
            a = aT[fg % 2]
            for fc in range(4):
                for (t0, n) in TG:
                    iu += 1
                    pb = iu % 2
                    for kc in range(16):
                        kb.op("pe", lambda e, pb=pb, kc=kc, fc=fc, t0=t0, n=n, Wu=Wu: e.matmul(
                            banks[pb][:, 0:n], Wu[:, kc, fc * 128:(fc + 1) * 128], hT[:, kc, t0:t0 + n],
                            start=(kc == 0), stop=(kc == 15)),
                            reads=[uk], writes=[("bank", pb)])
                    r = rl[iu % 2]
                    kb.op("act", lambda e, pb=pb, n=n, r=r: e.activation(r[:, 0:n], banks[pb][:, 0:n], AF.Relu),
                          reads=[("bank", pb)], writes=[("rl", iu % 2)])
                    kb.op("pool", lambda e, a=a, fc=fc, t0=t0, n=n, r=r: e.tensor_tensor(
                        a[:, fc, t0:t0 + n], r[:, 0:n], r[:, 0:n], ALU.mult),
                        reads=[("rl", iu % 2)], writes=[("aT", fg % 2, fc, t0)])
            for (t, n, off) in TOKT:
                t0 = 0 if off < 512 else (512 if off < 1024 else 1024)
                for cg in range(4):
                    idn += 1
                    pb = 2 + idn % 6
                    for fc in range(4):
                        kb.op("pe", lambda e, pb=pb, fc=fc, n=n, off=off, cg=cg, a=a, Wd=Wd: e.matmul(
                            banks[pb][0:n, :], a[:, fc, off:off + n], Wd[:, fc, cg * 512:(cg + 1) * 512],
                            start=(fc == 0), stop=(fc == 3)),
                            reads=[dk, ("aT", fg % 2, fc, t0)], writes=[("bank", pb)])
                    kb.op("dve", lambda e, pb=pb, n=n, t=t, cg=cg: e.tensor_tensor(
                        racc[0:n, t, cg * 512:(cg + 1) * 512], racc[0:n, t, cg * 512:(cg + 1) * 512],
                        banks[pb][0:n, :], ALU.add),
                        reads=[("bank", pb), ("racc", t)], writes=[("racc", t)])
        A.pop()
        kb.barrier()
        A.top = 144 * KI
        A.push()
        lng = A.alloc([D], F32)
        lnb = A.alloc([D], F32)
        stats = A.alloc([4, 6], F32)
        mv = A.alloc([2], F32)
        sd = A.alloc([2], F32)
        kb.dma("sp", lng, c_ln[2:3, :].partition_broadcast(128), "c_lng2", writes=["lnp"])
        kb.dma("sp", lnb, c_ln[3:4, :].partition_broadcast(128), "c_lnb2", writes=["lnp"])
        for (t, n, off) in TOKT:
            layer_norm_tile(t, n, lng, lnb, (stats, mv, sd))
            dst = y_own[t * 128:(t + 1) * 128, :] if t < 8 else y_s[:, :]
            kb.dma("sp", dst, racc[0:n, t, :], "yout", reads=[("racc", t)])
        A.pop()
        kb.barrier()

    qaT_s = kb.sbuf("qaT_s", [128, H, NS], BF16)[:]
    kaT_s = kb.sbuf("kaT_s", [128, H, NS], BF16)[:]
    va_s = kb.sbuf("va_s", [128, 1024], BF16)[:]

    if "p1" in phases:
        phase1()
    if "p3" in phases:
        phase3(qaT_s, kaT_s, va_s)
    if "p2" in phases:
        phase2()
    if "p6" in phases:
        phase6(qaT_s, kaT_s, va_s)
    if "p6s" in phases:
        phase6_stub(qaT_s, kaT_s, va_s)
    if "p4" in phases:
        phase4a()
        phase4b()
    if "p5" in phases:
        phase5()

    kb.finish("sp")
    import os
    if os.environ.get("KB_LOG"):
        print("PHASE_LOG", kb.phase_log)
        print("QUEUE_LEN", {k: len(v) for k, v in kb.prog.items()})
    kb.emit()
    kb.close()
    return nc


def _host_consts(core):
    half = core % 2
    f32 = np.float32
    c = {}
    c["c_ident"] = np.eye(128, dtype=f32)
    c["c_i64"] = np.eye(64, dtype=f32)
    inv = (np.float32(10000.0) ** (-np.arange(128, dtype=f32) / np.float32(128))).astype(f32)

    def rot(pos):
        ang = pos.astype(f32)[:, None] * inv[None, :]
        return np.concatenate([np.cos(ang), np.sin(ang)], axis=-1).astype(f32)

    p_own = half * OWN + np.arange(OWN)
    c["c_rot_own"] = rot(p_own).reshape(8, 128, 256)
    c["c_rot_prev"] = rot(np.arange(OWN)).reshape(8, 128, 256)
    c["c_rot_s"] = rot(PAST + (np.arange(NS) % 8))
    lg = np.log1p(-np.exp2(-5.0 - np.arange(H, dtype=np.float64)))
    i = np.arange(128, dtype=np.float64)
    c["c_qd"] = np.exp(lg[None, :] * (i[:, None] + 1.0)).astype(f32)
    c["c_kd"] = (np.exp(lg[None, :] * (127.0 - i[:, None])) / 16.0).astype(f32)
    jj, ii = np.meshgrid(i, i, indexing="ij")
    m = np.where(ii[:, None, :] >= jj[:, None, :], np.exp(-lg[None, :, None] * (jj[:, None, :] + 1.0)), 0.0) / 16.0
    c["c_maskT"] = m.astype(f32)
    il = (np.arange(NS) % 8).astype(np.float64)
    bb = np.arange(NS) // 8
    c["c_sqd"] = np.exp(lg[None, :] * (il[:, None] + 1.0)).astype(f32)
    c["c_skd"] = (np.exp(lg[None, :] * (7.0 - il[:, None])) / 16.0).astype(f32)
    same = (bb[:, None] == bb[None, :])
    caus = (il[None, :] >= il[:, None])
    sm = np.where((same & caus)[:, None, :], np.exp(-lg[None, :, None] * (il[:, None, None] + 1.0)), 0.0) / 16.0
    c["c_smaskT"] = sm.astype(f32)
    c["c_bmP"] = (bb[:, None] == np.arange(4)[None, :]).astype(f32)
    c["c_bmF"] = np.tile((np.arange(4)[:, None] == bb[None, :]).astype(f32).reshape(1, 4 * NS), (128, 1))
    kk, qq = np.meshgrid(np.arange(128), np.arange(128), indexing="ij")
    c["c_causal"] = np.where(qq >= kk, 0.0, NEG).astype(f32)
    E = np.zeros((32, 32, 128), f32)
    for r in range(32):
        E[r, r, :] = 1.0
    c["c_E"] = E.reshape(32, 32 * 128)
    el = np.zeros((8, 8), f32)
    for t in range(8):
        for n in range(8):
            if n < 4:
                el[t, n] = 1.0 if half == 1 else 0.0
            else:
                el[t, n] = 1.0 if (n - 4) < t // 2 else 0.0
    c["c_elig"] = np.tile(el.reshape(1, 64), (128, 1)).astype(f32)
    c["c_negm"] = np.where(c["c_elig"] > 0, 0.0, -1e30).astype(f32)
    so = np.zeros((NS, 4, 8, 8), f32)
    for j in range(NS):
        for q in range(8):
            if (j % 8) <= q:
                so[j, j // 8, :, q] = 1.0
    c["c_sown"] = so.reshape(NS, 256)
    hm = np.zeros((8, 8, 8, 128), f32)
    for h in range(8):
        hm[h, :, h, :] = 1.0
    c["c_hmask"] = hm.reshape(64, 1024)
    return c


_PROG_CACHE = {}


def kernel(x_prompt, x_sample, cache_k, cache_v, state_ret, page_table,
           w_in, ret_gn_gain, w_ret_br, w_att_br, w_out, ln1_g, ln1_b,
           w_up, w_down, ln2_g, ln2_b):
    f32 = np.float32
    n_phys = cache_k.shape[1]
    ck = np.ascontiguousarray(cache_k[0].reshape(n_phys * PAGE, 1024))
    cv = np.ascontiguousarray(cache_v[0].reshape(n_phys * PAGE, 1024))
    key = n_phys
    if key not in _PROG_CACHE:
        import os
        ph = os.environ.get("KPHASES")
        _PROG_CACHE[key] = build_program(n_phys * PAGE, tuple(ph.split(","))) if ph else build_program(n_phys * PAGE)
    nc = _PROG_CACHE[key]
    shared = {
        "w_in": np.ascontiguousarray(w_in[0]), "w_ret": np.ascontiguousarray(w_ret_br[0]),
        "w_att": np.ascontiguousarray(w_att_br[0]), "w_out": np.ascontiguousarray(w_out[0]),
        "w_up": np.ascontiguousarray(w_up[0]), "w_down": np.ascontiguousarray(w_down[0]),
        "cache_k": ck, "cache_v": cv,
        "c_gn": np.ascontiguousarray(ret_gn_gain[0].reshape(1, D)),
        "c_ln": np.ascontiguousarray(np.stack([ln1_g[0], ln1_b[0], ln2_g[0], ln2_b[0]], 0)),
    }
    in_maps = []
    zeros_prev = np.zeros((OWN, D), f32)
    for c in range(NCORE):
        b, half = c // 2, c % 2
        m = dict(shared)
        m["x_own"] = np.ascontiguousarray(x_prompt[b, half * OWN:(half + 1) * OWN])
        m["x_prev"] = np.ascontiguousarray(x_prompt[b, 0:OWN]) if half == 1 else zeros_prev
        m["x_s"] = np.ascontiguousarray(x_sample[4 * c:4 * c + 4].reshape(NS, D))
        m["state_s"] = np.ascontiguousarray(state_ret[0, 4 * c:4 * c + 4])
        m["ptab"] = np.ascontiguousarray(page_table[4 * c:4 * c + 4].reshape(1, 4 * NPG)).astype(np.int32)
        m.update(_host_consts(c))
        in_maps.append(m)
    import os as _os
    if _os.environ.get("KTRACE"):
        res = run_bass_kernel_spmd(nc, in_maps, core_ids=list(range(NCORE)), trace=True)
        print("KTRACE exec_time_ns", res.exec_time_ns)
    else:
        res = run_bass_kernel_spmd(nc, in_maps, core_ids=list(range(NCORE)))
    R = res.results
    B = x_prompt.shape[0]
    y_p = np.zeros((B, SEQ, D), f32)
    k_p = np.zeros((1, B, SEQ, H, HD), f32)
    v_p = np.zeros((1, B, SEQ, H, HD), f32)
    s_p = np.zeros((1, B, H, DK, DK), f32)
    y_s = np.zeros((32, 8, D), f32)
    k_s = np.zeros((1, 32, 8, H, HD), f32)
    v_s = np.zeros((1, 32, 8, H, HD), f32)
    s_s = np.zeros((1, 32, H, DK, DK), f32)
    for c in range(NCORE):
        b, half = c // 2, c % 2
        sl = slice(half * OWN, (half + 1) * OWN)
        y_p[b, sl] = R[c]["y_own"]
        k_p[0, b, sl] = R[c]["k_own"].reshape(OWN, H, HD)
        v_p[0, b, sl] = R[c]["v_own"].reshape(OWN, H, HD)
        if half == 1:
            s_p[0, b] = R[c]["ret_p"]
        y_s[4 * c:4 * c + 4] = R[c]["y_s"].reshape(4, 8, D)
        k_s[0, 4 * c:4 * c + 4] = R[c]["k_s"].reshape(4, 8, H, HD)
        v_s[0, 4 * c:4 * c + 4] = R[c]["v_s"].reshape(4, 8, H, HD)
        s_s[0, 4 * c:4 * c + 4] = R[c]["ret_s"]
    return (y_p, y_s, k_p, v_p, s_p, k_s, v_s, s_s)
```

```python
import contextlib
import numpy as np
import concourse.bass as bass
import concourse.mybir as mybir
from concourse.bass_utils import run_bass_kernel_spmd

F32 = mybir.dt.float32
BF16 = mybir.dt.bfloat16
I32 = mybir.dt.int32
AF = mybir.ActivationFunctionType
ALU = mybir.AluOpType
AX = mybir.AxisListType

COMPUTE = ("pe", "act", "dve", "pool")

D = 2048
SEQ = 2048
NCORE = 8
OWN = 1024
NS = 32
NTOK = OWN + NS
H = 8
DK = 256
HD = 128
DFF = 8192
PAST = 8192
PAGE = 128
NPG = 64
ALPHA = 2.0 ** 0.25
C_QR, C_KR, C_VR, C_GR = 0, 2048, 4096, 6144
C_QA, C_KA, C_VA = 8192, 9216, 10240
C_GBR, C_GBA = 11264, 13312
NEG = -30000.0


class _Rec:
    def __getattr__(self, name):
        return lambda *a, **k: (name, a, k)


_REC = _Rec()


class KB:
    def __init__(self, nc):
        self.nc = nc
        self.stack = contextlib.ExitStack()
        self.h = {"pe": nc.tensor, "act": nc.scalar, "dve": nc.vector,
                  "pool": nc.gpsimd, "sp": nc.sync}
        self.prog = {k: [] for k in self.h}
        self.sem = {}
        self.cnt = {}
        for k in COMPUTE:
            self.sem[k] = self.stack.enter_context(nc.semaphore("s_" + k))
            self.cnt[k] = 0
        self.dsem = {}
        self.waited = {}
        self.track = {}

    def sbuf(self, name, shape, dt):
        return self.stack.enter_context(self.nc.sbuf_tensor(name, list(shape), dt))

    def psum(self, name, shape, dt=F32):
        return self.stack.enter_context(self.nc.psum_tensor(name, list(shape), dt))

    def _dsem(self, key):
        if key not in self.dsem:
            self.dsem[key] = [self.stack.enter_context(self.nc.semaphore("d_" + str(key))), 0]
        return self.dsem[key]

    def _events_for(self, reads, writes):
        evs = []
        for k in list(reads) + list(writes):
            t = self.track.get(k)
            if t and t[0] is not None:
                evs.append(t[0])
        for k in writes:
            t = self.track.get(k)
            if t:
                evs.extend(t[1])
        return evs

    def _emit_waits(self, eng, evs):
        need = {}
        for ev in evs:
            kind, name, val = ev
            if kind == "eng" and name == eng and eng == "pe":
                continue
            semname = (kind, name)
            if self.waited.get((eng, semname), 0) >= val:
                continue
            if need.get(semname, 0) < val:
                need[semname] = val
        for semname, val in need.items():
            self.waited[(eng, semname)] = val
            sem = self.sem[semname[1]] if semname[0] == "eng" else self.dsem[semname[1]][0]
            self.prog[eng].append(("wait", sem, val))

    def _record(self, ev, reads, writes):
        for k in reads:
            t = self.track.setdefault(k, [None, []])
            t[1].append(ev)
        for k in writes:
            self.track[k] = [ev, []]

    def op(self, eng, fn, reads=(), writes=()):
        bank_r = [k for k in reads if isinstance(k, tuple) and k[0] == "bank"]
        if bank_r:
            reads = [k for k in reads if k not in bank_r]
            writes = list(writes) + [k for k in bank_r if k not in writes]
        evs = self._events_for(reads, writes)
        self._emit_waits(eng, evs)
        self.cnt[eng] += 1
        ev = ("eng", eng, self.cnt[eng])
        self.prog[eng].append(("op", fn(_REC), self.sem[eng]))
        self._record(ev, reads, writes)
        return ev

    def dma(self, q, out, in_, sem_key, reads=(), writes=(), **kw):
        evs = self._events_for(reads, writes)
        self._emit_waits(q, evs)
        ds = self._dsem(sem_key)
        ds[1] += 16
        ev = ("dma", sem_key, ds[1])
        self.prog[q].append(("dma", out, in_, ds[0], kw))
        self._record(ev, reads, writes)
        return ev

    def dma_fn(self, q, fn, sem_key, reads=(), writes=()):
        evs = self._events_for(reads, writes)
        self._emit_waits(q, evs)
        ds = self._dsem(sem_key)
        ds[1] += 16
        ev = ("dma", sem_key, ds[1])
        self.prog[q].append(("dmafn", fn(_REC), ds[0]))
        self._record(ev, reads, writes)
        return ev

    def barrier(self):
        self.phase_log = getattr(self, "phase_log", [])
        self.phase_log.append(dict(self.cnt))
        for eng in self.h:
            for k in COMPUTE:
                if k != eng and self.cnt[k] > self.waited.get((eng, ("eng", k)), 0):
                    self.waited[(eng, ("eng", k))] = self.cnt[k]
                    self.prog[eng].append(("wait", self.sem[k], self.cnt[k]))
            for key, (sem, cnt) in self.dsem.items():
                if cnt > self.waited.get((eng, ("dma", key)), 0):
                    self.waited[(eng, ("dma", key))] = cnt
                    self.prog[eng].append(("wait", sem, cnt))
        self.track = {}

    def finish(self, final_eng="sp"):
        for key, (sem, cnt) in self.dsem.items():
            if cnt > 0:
                self.prog[final_eng].append(("wait", sem, cnt))
        for k in COMPUTE:
            if self.cnt[k] > 0:
                self.prog[final_eng].append(("wait", self.sem[k], self.cnt[k]))

    def emit(self):
        nc, prog, h = self.nc, self.prog, self.h

        def run(name):
            e = h[name]
            for it in prog[name]:
                if it[0] == "wait":
                    e.wait_ge(it[1], it[2])
                elif it[0] == "op":
                    getattr(e, it[1][0])(*it[1][1], **it[1][2]).then_inc(it[2], 1)
                elif it[0] == "dmafn":
                    getattr(e, it[1][0])(*it[1][1], **it[1][2]).then_inc(it[2], 16)
                else:
                    e.dma_start(out=it[1], in_=it[2], **it[4]).then_inc(it[3], 16)

        with nc.Block() as block:
            @block.sync
            def _(e):
                run("sp")

            @block.tensor
            def _(e):
                run("pe")

            @block.scalar
            def _(e):
                run("act")

            @block.vector
            def _(e):
                run("dve")

            @block.gpsimd
            def _(e):
                run("pool")

    def close(self):
        self.stack.close()


class Arena:
    def __init__(self, t, nbytes):
        self.t = t
        self.n = nbytes
        self.top = 0
        self.marks = []

    def alloc(self, free_shape, dt, parts=128):
        esz = 4 if dt in (F32, I32) else 2
        n = int(np.prod(free_shape))
        nb = (n * esz + 31) // 32 * 32
        assert self.top + nb <= self.n, ("arena overflow", self.top, nb, self.n)
        off = self.top
        self.top += nb
        return self.at(off, free_shape, dt, parts)

    def at(self, off, free_shape, dt, parts=128):
        esz = 4 if dt in (F32, I32) else 2
        n = int(np.prod(free_shape))
        nb = n * esz
        assert off % 4 == 0 and off + nb <= self.n
        v = self.t[0:parts, off // 4:(off + (nb + 3) // 4 * 4) // 4]
        if dt != F32:
            v = v.bitcast(dt)
        v = v[:, 0:n]
        if len(free_shape) == 2:
            v = v.rearrange("p (a b) -> p a b", b=free_shape[1])
        elif len(free_shape) == 3:
            v = v.rearrange("p (a b c) -> p a b c", b=free_shape[1], c=free_shape[2])
        return v

    def push(self):
        self.marks.append(self.top)

    def pop(self):
        import os
        if os.environ.get("KB_LOG"):
            print("ARENA_TOP_KiB", self.top / 1024.0)
        self.top = self.marks.pop()


def bcast(ap, steps):
    a = ap.ap
    return bass.AP(ap.tensor, ap.offset, [list(a[0])] + [list(s) for s in steps])


def build_program(n_phys_rows, phases=("p1", "p3", "p2", "p6", "p4", "p5")):
    nc = bass.Bass("TRN2", target_bir_lowering=False)

    def din(name, shape, dt=F32):
        return nc.dram_tensor(name, list(shape), dt, kind="ExternalInput").ap()

    def dout(name, shape, dt=F32):
        return nc.dram_tensor(name, list(shape), dt, kind="ExternalOutput").ap()

    x_own = din("x_own", [OWN, D])
    x_prev = din("x_prev", [OWN, D])
    x_s = din("x_s", [NS, D])
    w_in = din("w_in", [D, 15360])
    w_ret = din("w_ret", [D, D])
    w_att = din("w_att", [1024, D])
    w_out = din("w_out", [D, D])
    w_up = din("w_up", [D, DFF])
    w_down = din("w_down", [DFF, D])
    cache_k = din("cache_k", [n_phys_rows, 1024])
    cache_v = din("cache_v", [n_phys_rows, 1024])
    state_s = din("state_s", [4, H, DK, DK])
    ptab = din("ptab", [1, 4 * NPG], I32)
    c_ident = din("c_ident", [128, 128])
    c_rot_own = din("c_rot_own", [8, 128, 256])
    c_rot_prev = din("c_rot_prev", [8, 128, 256])
    c_rot_s = din("c_rot_s", [NS, 256])
    c_qd = din("c_qd", [128, 8])
    c_kd = din("c_kd", [128, 8])
    c_maskT = din("c_maskT", [128, 8, 128])
    c_sqd = din("c_sqd", [NS, 8])
    c_skd = din("c_skd", [NS, 8])
    c_smaskT = din("c_smaskT", [NS, 8, NS])
    c_bmP = din("c_bmP", [NS, 4])
    c_bmF = din("c_bmF", [128, 4 * NS])
    c_gn = din("c_gn", [1, D])
    c_causal = din("c_causal", [128, 128])
    c_E = din("c_E", [32, 32 * 128])
    c_negm = din("c_negm", [128, 64])
    c_elig = din("c_elig", [128, 64])
    c_ln = din("c_ln", [4, D])
    c_sown = din("c_sown", [NS, 4 * 64])
    c_hmask = din("c_hmask", [64, 8 * 128])
    c_ident64 = din("c_i64", [64, 64])

    y_own = dout("y_own", [OWN, D])
    y_s = dout("y_s", [NS, D])
    k_own = dout("k_own", [OWN, 1024])
    v_own = dout("v_own", [OWN, 1024])
    ret_p = dout("ret_p", [H, DK, DK])
    k_s = dout("k_s", [NS, 1024])
    v_s = dout("v_s", [NS, 1024])
    ret_s = dout("ret_s", [4, H, DK, DK])

    kb = KB(nc)
    ARENA_BYTES = 204 * 1024
    arena_t = kb.sbuf("arena", [128, ARENA_BYTES // 4], F32)
    A = Arena(arena_t, ARENA_BYTES)
    banks = [kb.psum(f"bank{i}", [128, 512], F32) for i in range(8)]

    def bank_bf(i):
        return banks[i][:, 0:512].bitcast(BF16)

    ident = kb.sbuf("ident", [128, 128], BF16)[:]
    ones_bf = kb.sbuf("ones_bf", [128, 128], BF16)[:]
    kb.dma("pool", ident, c_ident, "c_ident", writes=["ident"])
    kb.op("dve", lambda e: e.memset(ones_bf, 1.0), writes=["ones"])

    KI = 1024
    xT = A.at(0 * KI, [16, NTOK], BF16)
    xTp = A.at(36 * KI, [16, NTOK], BF16)
    oaT = A.at(72 * KI, [8, NTOK], BF16)
    uT = A.at(90 * KI, [16, NTOK], BF16)

    w_in_v = w_in.rearrange("(kc p) n -> p kc n", p=128)

    class Ring:
        def __init__(self, name, slots):
            self.name, self.slots, self.i = name, slots, 0

        def load(self, src_ap, kcs, ncols):
            s = self.i % len(self.slots)
            self.i += 1
            dst = self.slots[s]
            key = (self.name, s)
            kb.dma("pool", dst[:, 0:kcs, 0:ncols], src_ap, f"{self.name}{s}", writes=[key])
            return dst, key

    def phase1():
        A.top = 126 * KI
        A.push()
        xb = [A.alloc([D], BF16) for _ in range(3)]
        srcs = [(x_own, t, 128, xT, t * 128) for t in range(8)] + \
               [(x_prev, t, 128, xTp, t * 128) for t in range(8)] + [(x_s, 0, NS, xT, OWN)]
        for i, (src, t, n, dstT, off) in enumerate(srcs):
            b = xb[i % 3]
            bk = ("xb", i % 3)
            kb.dma("pool", b[0:n, :], src[t * 128:t * 128 + n, :], f"xb{i%3}", writes=[bk], max_dma_last_dim=2048)
            for half in range(2):
                pb = i * 2 + half
                pv = bank_bf(pb % 4).rearrange("p (a b) -> p a b", b=128)
                for j in range(8):
                    kc = half * 8 + j
                    kb.op("pe", lambda e, pv=pv, j=j, kc=kc, b=b, n=n: e.transpose(
                        pv[:, j, 0:n], b[0:n, kc * 128:(kc + 1) * 128], ident[0:n, 0:n]),
                        reads=[bk, "ident"], writes=[("bank", pb % 4)])
                eng = "dve" if half == 0 else "act"
                if eng == "dve":
                    kb.op("dve", lambda e, pv=pv, dstT=dstT, half=half, off=off, n=n: e.tensor_copy(
                        dstT[:, half * 8:half * 8 + 8, off:off + n], pv[:, :, 0:n]),
                        reads=[("bank", pb % 4)], writes=[("xT", id(dstT), off, half)])
                else:
                    kb.op("act", lambda e, pv=pv, dstT=dstT, half=half, off=off, n=n: e.copy(
                        dstT[:, half * 8:half * 8 + 8, off:off + n], pv[:, :, 0:n]),
                        reads=[("bank", pb % 4)], writes=[("xT", id(dstT), off, half)])
        A.pop()
        kb.barrier()

    HG = 4

    def phase3(qaT_s, kaT_s, va_s):
        A.top = 90 * KI
        A.push()
        GC = HG * HD
        ring = Ring("w3", [A.alloc([16, GC], BF16) for _ in range(3)])
        kT = A.alloc([HG, 2048], BF16)
        vA = A.alloc([16, GC], BF16)
        qT = A.alloc([HG, OWN], BF16)
        stg = [A.alloc([GC], F32) for _ in range(2)]
        cbf = [A.alloc([GC], BF16) for _ in range(2)]
        biasT = A.alloc([OWN], BF16)
        pt_sb = [A.alloc([OWN], BF16) for _ in range(2)]
        rden = [A.alloc([128], F32) for _ in range(2)]
        ksf = A.alloc([HG, 8], F32)
        ksb = A.alloc([HG, 8], BF16)
        sm2 = A.alloc([2, HG, 8], F32)
        rank2 = A.alloc([2 * HG, 8], F32)
        bias2 = A.alloc([2, HG * 8], BF16)
        causal = A.alloc([128], BF16)
        Eoh = A.alloc([32 * 128], BF16)
        negm = A.alloc([8, 8], F32)
        elig = A.alloc([8, 8], F32)
        import os
        DBG = os.environ.get("P3_DBG", "")
        if "noconst" not in DBG:
          kb.dma("pool", causal, c_causal, "c_causal", writes=["causal"])
        if "noconst" not in DBG:
          kb.dma("pool", Eoh[0:32, :], c_E, "c_E", writes=["Eoh"], max_dma_last_dim=4096)
        kb.dma("sp", negm, c_negm.rearrange("p (a b) -> p a b", b=8), "c_negm", writes=["negm"])
        kb.dma("sp", elig, c_elig.rearrange("p (a b) -> p a b", b=8), "c_elig", writes=["elig"])
        scale = HD ** -0.5
        tiles_all = [("prev", t, 128, xTp, t * 128) for t in range(8)] + \
                    [("own", t, 128, xT, t * 128) for t in range(8)] + [("s", 0, NS, xT, OWN)]
        ectr = [0]
        pend3 = []

        for hp in range(H // HG):
            cg0 = hp * GC
            for which, cbase in (("k", C_KA), ("v", C_VA), ("q", C_QA)):
                W, wkey = ring.load(w_in_v[:, :, cbase + cg0:cbase + cg0 + GC], 16, GC)
                for ti, (kind, t, n, xsrc, off) in enumerate(tiles_all):
                    if which == "q" and kind == "prev":
                        continue
                    if "nos" in DBG and kind == "s":
                        continue
                    ectr[0] += 1
                    pb = ectr[0] % 2
                    ps = banks[pb]
                    for kc in range(16):
                        kb.op("pe", lambda e, ps=ps, xsrc=xsrc, kc=kc, off=off, n=n, W=W: e.matmul(
                            ps[0:n, :], xsrc[:, kc, off:off + n], W[:, kc, :], start=(kc == 0), stop=(kc == 15)),
                            reads=[("xT", id(xsrc), off, kc // 8), wkey], writes=[("bank", pb)])
                    while pend3:
                        pend3.pop(0)()
                    sl = ectr[0] % 2
                    if which in ("k", "v") and kind != "prev" and "nostg" not in DBG:
                        kb.op("act", lambda e, ps=ps, n=n, sl=sl: e.copy(stg[sl][0:n, :], ps[0:n, :]),
                              reads=[("bank", pb)], writes=[("stg", sl)])
                        dst = {("k", "own"): k_own, ("v", "own"): v_own, ("k", "s"): k_s, ("v", "s"): v_s}[(which, kind)]
                        if "nodmaout" not in DBG:
                            kb.dma(os.environ.get("STQ", "sp"), dst[t * 128:t * 128 + n, cg0:cg0 + GC], stg[sl][0:n, :], f"stg{sl}",
                                   reads=[("stg", sl)])
                    if which == "v":
                        if kind == "s":
                            kb.op("dve", lambda e, ps=ps: e.tensor_copy(va_s[0:NS, cg0:cg0 + GC], ps[0:NS, :]),
                                  reads=[("bank", pb)], writes=[("va_s", hp)])
                        else:
                            kb.op("dve", lambda e, ps=ps, ti=ti: e.tensor_copy(vA[:, ti, :], ps[:, :]),
                                  reads=[("bank", pb)], writes=[("vA", ti)])
                        continue
                    if "notr" in DBG:
                        continue
                    kb.op("dve", lambda e, ps=ps, n=n, sl=sl: e.tensor_copy(cbf[sl][0:n, :], ps[0:n, :]),
                          reads=[("bank", pb)], writes=[("cbf", sl)])
                    def fin3(ec=ectr[0], n=n, sl=sl, kind=kind, which=which, ti=ti, t=t, hp=hp):
                        tb = 2 + (ec % 2)
                        pv = bank_bf(tb).rearrange("p (a b) -> p a b", b=128)
                        for j in range(HG):
                            kb.op("pe", lambda e: e.transpose(
                                pv[:, j, 0:n], cbf[sl][0:n, j * 128:(j + 1) * 128], ident[0:n, 0:n]),
                                reads=[("cbf", sl), "ident"], writes=[("bank", tb)])
                        if kind == "s":
                            dstv = (kaT_s if which == "k" else qaT_s)[:, hp * HG:(hp + 1) * HG, :]
                            kb.op("act", lambda e: e.copy(dstv, pv[:, 0:HG, 0:NS]),
                                  reads=[("bank", tb)], writes=[(which + "aT_s", hp)])
                        elif which == "k":
                            kb.op("act", lambda e: e.copy(kT[:, :, ti * 128:(ti + 1) * 128], pv[:, 0:HG, :]),
                                  reads=[("bank", tb)], writes=[("kT", ti)])
                        else:
                            kb.op("act", lambda e: e.copy(qT[:, :, t * 128:(t + 1) * 128], pv[:, 0:HG, :]),
                                  reads=[("bank", tb)], writes=[("qT", t)])
                    pend3.append(fin3)
            while pend3:
                pend3.pop(0)()
            import os
            if os.environ.get("P3_STOP") == "proj":
                continue
            kb.op("dve", lambda e: e.tensor_reduce(ksf, kT.rearrange("p h (n k) -> p h n k", k=256), AX.X, ALU.add),
                  reads=[("kT", ti) for ti in range(16)], writes=["ksf"])
            kb.op("dve", lambda e: e.tensor_copy(ksb, ksf), reads=["ksf"], writes=["ksb"])
            ps = banks[4]
            for t in range(8):
                for hh in range(HG):
                    kb.op("pe", lambda e: e.matmul(
                        ps[:, (t * HG + hh) * 8:(t * HG + hh + 1) * 8], qT[:, hh, t * 128:(t + 1) * 128], ksb[:, hh, :],
                        start=True, stop=True), reads=[("qT", t), "ksb"], writes=[("bank", 4)])
            cmp2 = stg[0].rearrange("p (a b c) -> p a b c", b=8, c=8)
            for c in range(4):
                psv = ps[:, c * 64:(c + 1) * 64].rearrange("p (t h n) -> p t h n", h=HG, n=8)
                kb.op("dve", lambda e: e.tensor_tensor(
                    sm2, psv, bcast(negm[:, 2 * c, :], [[8, 2], [0, HG], [1, 8]]), ALU.add),
                    reads=[("bank", 4), "negm"], writes=["sm"])
                kb.op("dve", lambda e: e.tensor_tensor(
                    cmp2, bcast(sm2, [[8, 2 * HG], [0, 8], [1, 8]]), bcast(sm2, [[8, 2 * HG], [1, 8], [0, 8]]), ALU.is_gt),
                    reads=["sm"], writes=[("stg", 0)])
                kb.op("dve", lambda e: e.tensor_reduce(rank2, cmp2, AX.X, ALU.add), reads=[("stg", 0)], writes=["rank"])
                kb.op("dve", lambda e: e.tensor_scalar(rank2, rank2, 3.0, None, ALU.is_lt), reads=["rank"], writes=["rank"])
                r3 = rank2.rearrange("p (t h) n -> p t h n", h=HG)
                kb.op("dve", lambda e: e.tensor_tensor(
                    r3, r3, bcast(elig[:, 2 * c, :], [[8, 2], [0, HG], [1, 8]]), ALU.mult), reads=["rank", "elig"], writes=["rank"])
                kb.op("dve", lambda e: e.tensor_scalar(
                    bias2.rearrange("p t (h n) -> p (t h) n", n=8), rank2, -1.0, -NEG, ALU.add, ALU.mult),
                    reads=["rank"], writes=["bias_bf"])
                pvb = bank_bf(5)
                for tt in range(2):
                    kb.op("pe", lambda e: e.transpose(pvb[0:HG * 8, tt * 128:(tt + 1) * 128], bias2[:, tt, :], ident),
                          reads=["bias_bf", "ident"], writes=[("bank", 5)])
                kb.op("act", lambda e: e.copy(biasT[0:HG * 8, 2 * c * 128:(2 * c + 2) * 128], pvb[0:HG * 8, 0:256]),
                      reads=[("bank", 5)], writes=[("biasT", 2 * c), ("biasT", 2 * c + 1)])
            if os.environ.get("P3_STOP") == "sel":
                continue
            for hh in range(HG):
                def LG_(kt, hh=hh):
                    ko = kt - 8
                    lo = 0 if kt < 8 else ko * 128
                    n_blk = kt // 2
                    lbase = 2 * (kt % 2)
                    ptb = pt_sb[kt % 2]
                    for ch in range(2):
                        c0, c1 = max(lo, ch * 512), (ch + 1) * 512
                        if c0 >= c1:
                            continue
                        lb = lbase + ch
                        pl = banks[lb]
                        subs = []
                        if kt < 8:
                            subs.append(("sel", c0, c1))
                        else:
                            for tq in range(c0 // 128, c1 // 128):
                                if tq == ko:
                                    subs.append(("diag", tq * 128, (tq + 1) * 128))
                                elif n_blk < 4 + tq // 2:
                                    if subs and subs[-1][0] == "sel" and subs[-1][2] == tq * 128:
                                        subs[-1] = ("sel", subs[-1][1], (tq + 1) * 128)
                                    else:
                                        subs.append(("sel", tq * 128, (tq + 1) * 128))
                        kb.op("pe", lambda e: e.matmul(
                            pl[:, c0 - ch * 512:c1 - ch * 512], kT[:, hh, kt * 128:(kt + 1) * 128], qT[:, hh, c0:c1],
                            start=True, stop=(len(subs) == 0)),
                            reads=[("kT", kt)] + [("qT", tq) for tq in range(c0 // 128, c1 // 128)], writes=[("bank", lb)])
                        for si, (typ, a0, a1) in enumerate(subs):
                            lastb = si == len(subs) - 1
                            if typ == "sel":
                                r = hh * 8 + n_blk
                                kb.op("pe", lambda e: e.matmul(
                                    pl[:, a0 - ch * 512:a1 - ch * 512], Eoh[0:HG * 8, r * 128:(r + 1) * 128],
                                    biasT[0:HG * 8, a0:a1], start=False, stop=lastb),
                                    reads=["Eoh"] + [("biasT", tq) for tq in range(a0 // 128, a1 // 128)],
                                    writes=[("bank", lb)])
                            else:
                                kb.op("pe", lambda e: e.matmul(
                                    pl[:, a0 - ch * 512:a1 - ch * 512], ident, causal, start=False, stop=lastb),
                                    reads=["ident", "causal"], writes=[("bank", lb)])
                        kb.op("act", lambda e: e.activation(ptb[:, c0:c1], pl[:, c0 - ch * 512:c1 - ch * 512], AF.Exp, scale=scale),
                              reads=[("bank", lb)], writes=[("pt", kt % 2, ch)])

                def PV_(kt, hh=hh):
                    lo = 0 if kt < 8 else (kt - 8) * 128
                    ptb = pt_sb[kt % 2]
                    for ch in range(2):
                        c0, c1 = max(lo, ch * 512), (ch + 1) * 512
                        if c0 >= c1:
                            continue
                        kb.op("pe", lambda e: e.matmul(
                            banks[4 + ch][:, c0 - ch * 512:c1 - ch * 512], vA[:, kt, hh * 128:(hh + 1) * 128], ptb[:, c0:c1],
                            start=(kt == 0), stop=(kt == 15)),
                            reads=[("pt", kt % 2, ch), ("vA", kt)], writes=[("bank", 4 + ch)])
                        kb.op("pe", lambda e: e.matmul(
                            banks[6 + ch][:, c0 - ch * 512:c1 - ch * 512], ones_bf, ptb[:, c0:c1],
                            start=(kt == 0), stop=(kt == 15)),
                            reads=[("pt", kt % 2, ch), "ones"], writes=[("bank", 6 + ch)])
                LG_(0)
                for kt in range(16):
                    if kt + 1 < 16:
                        LG_(kt + 1)
                    PV_(kt)
                for ch in range(2):
                    rd = stg[ch]
                    kb.op("dve", lambda e: e.reciprocal(rd, banks[6 + ch][:, :]), reads=[("bank", 6 + ch)], writes=[("stg", ch)])
                    kb.op("dve", lambda e: e.tensor_tensor(
                        oaT[:, hp * HG + hh, ch * 512:(ch + 1) * 512], banks[4 + ch][:, :], rd, ALU.mult),
                        reads=[("bank", 4 + ch), ("stg", ch)], writes=[("oaT", hp * HG + hh, ch)])
        A.pop()
        kb.barrier()

    LG = [float(np.log1p(-2.0 ** (-5.0 - h))) for h in range(H)]

    def phase2():
        A.top = 126 * KI
        A.push()
        ring = Ring("w2", [A.alloc([16, 256], BF16) for _ in range(4)])
        rot = [A.alloc([256], F32) for _ in range(2)]
        kf = [A.alloc([256], F32) for _ in range(2)]
        qf = [A.alloc([256], F32) for _ in range(2)]
        sgf = A.alloc([256], F32)
        ta = A.alloc([256], F32)
        tb_ = A.alloc([256], F32)
        qr = A.alloc([256], F32)
        onb = A.alloc([256], F32)
        k_bf = [A.alloc([256], BF16) for _ in range(2)]
        kp_bf = [A.alloc([256], BF16) for _ in range(2)]
        qp_bf = [A.alloc([256], BF16) for _ in range(2)]
        v_bf = [A.alloc([256], BF16) for _ in range(2)]
        sgn = [A.alloc([256], BF16) for _ in range(2)]
        u_bf = [A.alloc([256], BF16) for _ in range(2)]
        kqT = [A.alloc([4, 128], BF16) for _ in range(2)]
        PT = [A.alloc([128], BF16) for _ in range(2)]
        S_f = A.alloc([2, 256], F32)
        S_b = A.alloc([2, 256], BF16)
        qd = A.alloc([8], F32)
        kd = A.alloc([8], F32)
        maskT = [A.alloc([128], F32) for _ in range(2)]
        gn_h = [A.alloc([256], F32) for _ in range(2)]
        stats = A.alloc([6], F32)
        mv = A.alloc([2], F32)
        sd = A.alloc([2], F32)
        sS_f = [A.alloc([2, 256], F32) for _ in range(4)]
        sS_b = [A.alloc([2, 256], BF16) for _ in range(4)]
        skf = [("sS_f", i) for i in range(4)]
        skb = [("sS_b", i) for i in range(4)]
        sS_o = [A.alloc([2, 256], F32)] * 2
        qm = A.alloc([4, 2, NS], BF16)
        vm = A.alloc([4, 256], BF16)
        sqd = A.alloc([8], F32)
        skd = A.alloc([8], F32)
        smaskT = A.alloc([8, NS], F32)
        bmP = A.alloc([4], F32)
        bmF = A.alloc([4, NS], F32)
        rot_s = A.alloc([256], F32)
        kb.dma("sp", qd, c_qd, "c_qd", writes=["qd"])
        kb.dma("sp", kd, c_kd, "c_kd", writes=["kd"])
        kb.dma("sp", sqd[0:NS, :], c_sqd, "c_sqd", writes=["sqd"])
        kb.dma("sp", skd[0:NS, :], c_skd, "c_skd", writes=["skd"])
        kb.dma("sp", smaskT[0:NS], c_smaskT, "c_smaskT", writes=["smaskT"])
        kb.dma("sp", bmP[0:NS, :], c_bmP, "c_bmP", writes=["bmP"])
        kb.dma("sp", bmF, c_bmF.rearrange("p (b i) -> p b i", i=NS), "c_bmF", writes=["bmF"])
        kb.dma("sp", rot_s[0:NS, :], c_rot_s, "c_rot_s", writes=["rot_s"])

        def rotary(src, rt, dst, n, keys_r, key_w):
            cosb = bcast(rt[0:n, 0:128], [[0, 2], [1, 128]])
            sinb = bcast(rt[0:n, 128:256], [[0, 2], [1, 128]])
            s3 = src[0:n, :].rearrange("p (a b) -> p a b", b=128)
            kb.op("dve", lambda e: e.tensor_tensor(ta[0:n, :].rearrange("p (a b) -> p a b", b=128), s3, cosb, ALU.mult),
                  reads=keys_r, writes=["ta"])
            kb.op("dve", lambda e: e.tensor_tensor(tb_[0:n, :].rearrange("p (a b) -> p a b", b=128), s3, sinb, ALU.mult),
                  reads=keys_r, writes=["tb"])
            kb.op("dve", lambda e: e.tensor_tensor(dst[0:n, 0:128], ta[0:n, 0:128], tb_[0:n, 128:256], ALU.subtract),
                  reads=["ta", "tb"], writes=[key_w])
            kb.op("dve", lambda e: e.tensor_tensor(dst[0:n, 128:256], tb_[0:n, 0:128], ta[0:n, 128:256], ALU.add),
                  reads=["ta", "tb"], writes=[key_w])

        def group_norm_gate(po_ap, n, sg_ap, sgkey, u_ap, ukey, bank_key):
            kb.op("dve", lambda e: e.bn_stats(stats[0:n, :], po_ap), reads=[bank_key], writes=["gstats"])
            kb.op("dve", lambda e: e.bn_aggr(mv[0:n, :], stats[0:n, :]), reads=["gstats"], writes=["gmv"])
            kb.op("act", lambda e: e.activation(sd[0:n, 0:1], mv[0:n, 1:2], AF.Sqrt, bias=1e-5, scale=1.0),
                  reads=["gmv"], writes=["gsd"])
            kb.op("dve", lambda e: e.reciprocal(sd[0:n, 1:2], sd[0:n, 0:1]), reads=["gsd"], writes=["grs"])
            kb.op("dve", lambda e: e.tensor_scalar(onb[0:n, :], po_ap, mv[0:n, 0:1], sd[0:n, 1:2], ALU.subtract, ALU.mult),
                  reads=[bank_key, "gmv", "grs"], writes=["onb"])
            kb.op("dve", lambda e: e.tensor_tensor(u_ap, onb[0:n, :], sg_ap, ALU.mult),
                  reads=["onb", sgkey], writes=[ukey])

        it = 0
        def wload(cbase, hh_):
            return ring.load(w_in_v[:, :, cbase + hh_ * 256:cbase + (hh_ + 1) * 256], 16, 256)

        nextW = {"k": wload(C_KR, 0), "v": wload(C_VR, 0), "q": wload(C_QR, 0), "g": wload(C_GR, 0)}
        for h in range(H):
            Wk, wkk = nextW["k"]
            Wv, wvk = nextW["v"]
            Wq, wqk = nextW["q"]
            Wg, wgk = nextW["g"]
            nextW = {}
            mT = maskT[h % 2]
            gnh = gn_h[h % 2]
            kb.dma("sp", mT, c_maskT[:, h, :], f"c_mT{h%2}", writes=[("mT", h % 2)])
            kb.dma("sp", gnh, c_gn[0:1, h * 256:(h + 1) * 256].partition_broadcast(128), f"c_gnh{h%2}",
                   writes=[("gnh", h % 2)])
            g128 = float(np.exp(LG[h] * 128.0))
            g8 = float(np.exp(LG[h] * 8.0))
            kb.op("dve", lambda e: e.memset(S_f, 0.0), writes=["S_f"])
            kb.op("act", lambda e: e.copy(S_b, S_f), reads=["S_f"], writes=["S_b"])
            for b in range(4):
                kb.dma("sp", sS_f[b], state_s[b, h].rearrange("(c p) e -> p c e", p=128), f"sSf{b}", writes=[skf[b]])
                kb.dma("pool", sS_b[b], state_s[b, h].rearrange("(c p) e -> p c e", p=128), f"sSb{b}", writes=[skb[b]])
            tiles = [("prev", t, 128, xTp, t * 128, c_rot_prev) for t in range(8)] + \
                    [("own", t, 128, xT, t * 128, c_rot_own) for t in range(8)]

            class Tile:
                pass

            def make_tile(kind, t, xsrc, off, rsrc, sl):
                T = Tile()
                own = kind == "own"
                rt = rot[sl]
                bA, bB = banks[sl], banks[2 + sl]
                pv4 = bank_bf(4).rearrange("p (a b) -> p a b", b=128)
                kq = kqT[sl]
                pt = PT[sl]
                ub = u_bf[sl]

                def proj(W, wk, dst, bk):
                    for kc in range(16):
                        kb.op("pe", lambda e: e.matmul(dst, xsrc[:, kc, off:off + 128], W[:, kc, :],
                                                       start=(kc == 0), stop=(kc == 15)), reads=[wk], writes=[bk])

                def A_kv():
                    kb.dma("sp", rt, rsrc[t], f"rot{sl}", writes=[("rot", sl)])
                    proj(Wk, wkk, bA[:, 0:256], ("bank", sl))
                    proj(Wv, wvk, bA[:, 256:512], ("bank", sl))
                    kb.op("act", lambda e: e.copy(kf[sl], bA[:, 0:256]), reads=[("bank", sl)], writes=[("kf", sl)])
                    kb.op("act", lambda e: e.copy(v_bf[sl], bA[:, 256:512]), reads=[("bank", sl)], writes=[("v_bf", sl)])
                    rotary(kf[sl], rt, k_bf[sl], 128, [("kf", sl), ("rot", sl)], ("k_bf", sl))
                    kb.op("dve", lambda e: e.tensor_scalar(kp_bf[sl], k_bf[sl], kd[:, h:h + 1], None, ALU.mult),
                          reads=[("k_bf", sl), "kd"], writes=[("kp_bf", sl)])

                def A_q():
                    if own:
                        proj(Wq, wqk, bB[:, 0:256], ("bank", 2 + sl))

                def A_g():
                    if not own:
                        return
                    proj(Wg, wgk, bB[:, 256:512], ("bank", 2 + sl))
                    kb.op("act", lambda e: e.copy(qf[sl], bB[:, 0:256]), reads=[("bank", 2 + sl)], writes=[("qf", sl)])
                    kb.op("act", lambda e: e.activation(sgf, bB[:, 256:512], AF.Silu), reads=[("bank", 2 + sl)], writes=["sgf"])
                    rotary(qf[sl], rt, qr, 128, [("qf", sl), ("rot", sl)], "qr")
                    kb.op("dve", lambda e: e.tensor_scalar(qp_bf[sl], qr, qd[:, h:h + 1], None, ALU.mult),
                          reads=["qr", "qd"], writes=[("qp_bf", sl)])
                    kb.op("dve", lambda e: e.tensor_tensor(sgn[sl], sgf, gnh, ALU.mult),
                          reads=["sgf", ("gnh", h % 2)], writes=[("sgn", sl)])

                def B1():
                    if not own:
                        return
                    for c in range(2):
                        kb.op("pe", lambda e: e.transpose(pv4[:, c, :], k_bf[sl][:, c * 128:(c + 1) * 128], ident),
                              reads=[("k_bf", sl), "ident"], writes=[("bank", 4)])
                    for c in range(2):
                        kb.op("pe", lambda e: e.transpose(pv4[:, 2 + c, :], qp_bf[sl][:, c * 128:(c + 1) * 128], ident),
                              reads=[("qp_bf", sl), "ident"], writes=[("bank", 4)])
                    kb.op("act", lambda e: e.copy(kq, pv4[:, 0:4, :]), reads=[("bank", 4)], writes=[("kqT", sl)])

                def B2():
                    if not own:
                        return
                    for c in range(2):
                        kb.op("pe", lambda e: e.matmul(banks[6][:, 0:128], kq[:, c, :], kq[:, 2 + c, :],
                                                       start=(c == 0), stop=(c == 1)),
                              reads=[("kqT", sl)], writes=[("bank", 6)])
                    kb.op("dve", lambda e: e.tensor_tensor(pt, banks[6][:, 0:128], mT, ALU.mult),
                          reads=[("bank", 6), ("mT", h % 2)], writes=[("PT", sl)])

                def B3():
                    for c in range(2):
                        kb.op("pe", lambda e: e.matmul(banks[5][:, c * 256:(c + 1) * 256],
                                                       kp_bf[sl][:, c * 128:(c + 1) * 128], v_bf[sl], start=True, stop=True),
                              reads=[("kp_bf", sl), ("v_bf", sl)], writes=[("bank", 5)])
                    if own:
                        kb.op("pe", lambda e: e.matmul(banks[7][:, 0:256], pt, v_bf[sl], start=True, stop=False),
                              reads=[("PT", sl), ("v_bf", sl)], writes=[("bank", 7)])
                        for c in range(2):
                            kb.op("pe", lambda e: e.matmul(banks[7][:, 0:256], kq[:, 2 + c, :], S_b[:, c, :],
                                                           start=False, stop=(c == 1)),
                                  reads=[("kqT", sl), "S_b"], writes=[("bank", 7)])
                    kb.op("dve", lambda e: e.scalar_tensor_tensor(
                        S_f.rearrange("p a b -> p (a b)"), S_f.rearrange("p a b -> p (a b)"), g128, banks[5][:, :],
                        ALU.mult, ALU.add), reads=[("bank", 5), "S_f"], writes=["S_f"])
                    kb.op("act", lambda e: e.copy(S_b, S_f), reads=["S_f"], writes=["S_b"])
                    if own:
                        group_norm_gate(banks[7][:, 0:256], 128, sgn[sl], ("sgn", sl), ub, ("u_bf", sl), ("bank", 7))

                def B4():
                    if not own:
                        return
                    for c in range(2):
                        kb.op("pe", lambda e: e.transpose(pv4[:, 4 + c, :], ub[:, c * 128:(c + 1) * 128], ident),
                              reads=[("u_bf", sl), "ident"], writes=[("bank", 4)])
                    kb.op("act", lambda e: e.copy(uT[:, h * 2:h * 2 + 2, t * 128:(t + 1) * 128], pv4[:, 4:6, :]),
                          reads=[("bank", 4)], writes=[("uT", t, h)])

                T.A_kv, T.A_q, T.A_g, T.B1, T.B2, T.B3, T.B4 = A_kv, A_q, A_g, B1, B2, B3, B4
                return T

            n = NS
            s_kf, s_qf = qf[1], qf[0]
            s_vbf, s_kbf, s_kpbf, s_qpbf, s_sgn, s_ubf = sgn[1], qp_bf[1], u_bf[1], qp_bf[0], sgn[0], u_bf[0]
            K_kf, K_qf, K_vbf, K_kbf, K_kpbf, K_qpbf, K_sgn, K_ubf = ("qf", 1), ("qf", 0), ("sgn", 1), ("qp_bf", 1), \
                ("u_bf", 1), ("qp_bf", 0), ("sgn", 0), ("u_bf", 0)
            pv4s = bank_bf(4).rearrange("p (a b) -> p a b", b=128)
            kqs = kqT[0]

            def S0():
                bA, bB = banks[2], banks[3]
                for (W, wk, dst, bk) in ((Wk, wkk, bA[0:n, 0:256], ("bank", 2)), (Wv, wvk, bA[0:n, 256:512], ("bank", 2)),
                                         (Wq, wqk, bB[0:n, 0:256], ("bank", 3)), (Wg, wgk, bB[0:n, 256:512], ("bank", 3))):
                    for kc in range(16):
                        kb.op("pe", lambda e: e.matmul(dst, xT[:, kc, OWN:OWN + NS], W[:, kc, :], start=(kc == 0), stop=(kc == 15)),
                              reads=[wk], writes=[bk])
                kb.op("act", lambda e: e.copy(s_kf[0:n, :], bA[0:n, 0:256]), reads=[("bank", 2)], writes=[K_kf])
                kb.op("act", lambda e: e.copy(s_vbf[0:n, :], bA[0:n, 256:512]), reads=[("bank", 2)], writes=[K_vbf])
                kb.op("act", lambda e: e.copy(s_qf[0:n, :], bB[0:n, 0:256]), reads=[("bank", 3)], writes=[K_qf])
                kb.op("act", lambda e: e.activation(sgf[0:n, :], bB[0:n, 256:512], AF.Silu), reads=[("bank", 3)], writes=["sgf"])
                rotary(s_kf, rot_s, s_kbf, n, [K_kf, "rot_s"], K_kbf)
                kb.op("dve", lambda e: e.tensor_scalar(s_kpbf[0:n, :], s_kbf[0:n, :], skd[0:n, h:h + 1], None, ALU.mult),
                      reads=[K_kbf, "skd"], writes=[K_kpbf])
                rotary(s_qf, rot_s, qr, n, [K_qf, "rot_s"], "qr")
                kb.op("dve", lambda e: e.tensor_scalar(s_qpbf[0:n, :], qr[0:n, :], sqd[0:n, h:h + 1], None, ALU.mult),
                      reads=["qr", "sqd"], writes=[K_qpbf])
                kb.op("dve", lambda e: e.tensor_tensor(s_sgn[0:n, :], sgf[0:n, :], gnh[0:n, :], ALU.mult),
                      reads=["sgf", ("gnh", h % 2)], writes=[K_sgn])

            def S1():
                for c in range(2):
                    kb.op("pe", lambda e: e.transpose(pv4s[:, c, 0:n], s_kbf[0:n, c * 128:(c + 1) * 128], ident[0:n, 0:n]),
                          reads=[K_kbf, "ident"], writes=[("bank", 4)])
                for c in range(2):
                    kb.op("pe", lambda e: e.transpose(pv4s[:, 2 + c, 0:n], s_qpbf[0:n, c * 128:(c + 1) * 128], ident[0:n, 0:n]),
                          reads=[K_qpbf, "ident"], writes=[("bank", 4)])
                kb.op("act", lambda e: e.copy(kqs[:, :, 0:n], pv4s[:, 0:4, 0:n]), reads=[("bank", 4)], writes=[("kqT", 0)])
                kb.op("dve", lambda e: e.tensor_tensor(
                    qm, bcast(kqs[:, 2:4, 0:n], [[0, 4], [128, 2], [1, n]]), bcast(bmF, [[n, 4], [0, 2], [1, n]]), ALU.mult),
                    reads=[("kqT", 0), "bmF"], writes=["qm"])
                kb.op("dve", lambda e: e.tensor_tensor(
                    vm[0:n], bcast(s_vbf[0:n, :], [[0, 4], [1, 256]]), bcast(bmP[0:n, :], [[1, 4], [0, 256]]), ALU.mult),
                    reads=[K_vbf, "bmP"], writes=["vm"])

            def S2():
                for c in range(2):
                    kb.op("pe", lambda e: e.matmul(banks[6][0:n, 0:n], kqs[:, c, 0:n], kqs[:, 2 + c, 0:n],
                                                   start=(c == 0), stop=(c == 1)),
                          reads=[("kqT", 0)], writes=[("bank", 6)])
                kb.op("dve", lambda e: e.tensor_tensor(PT[0][0:n, 0:n], banks[6][0:n, 0:n], smaskT[0:n, h, :], ALU.mult),
                      reads=[("bank", 6), "smaskT"], writes=[("PT", 0)])

            def S3():
                kb.op("pe", lambda e: e.matmul(banks[7][0:n, 0:256], PT[0][0:n, 0:n], s_vbf[0:n, :], start=True, stop=False),
                      reads=[("PT", 0), K_vbf], writes=[("bank", 7)])
                for b in range(4):
                    for c in range(2):
                        kb.op("pe", lambda e: e.matmul(
                            banks[7][0:n, 0:256], qm[:, b, c, :], sS_b[b][:, c, :], start=False, stop=(b == 3 and c == 1)),
                            reads=["qm", skb[b]], writes=[("bank", 7)])

            def S4(bs):
                for b in bs:
                    sbk = 5 + b % 2
                    for c in range(2):
                        kb.op("pe", lambda e: e.matmul(
                            banks[sbk][:, c * 256:(c + 1) * 256], s_kpbf[0:n, c * 128:(c + 1) * 128], vm[0:n, b, :],
                            start=True, stop=True),
                            reads=[K_kpbf, "vm"], writes=[("bank", sbk)])
                    kb.op("dve", lambda e: e.scalar_tensor_tensor(
                        sS_o[0].rearrange("p a b -> p (a b)"), sS_f[b].rearrange("p a b -> p (a b)"), g8, banks[sbk][:, :],
                        ALU.mult, ALU.add),
                        reads=[("bank", sbk), skf[b]], writes=[("sS_o", 0)])
                    kb.dma("pool", ret_s[b, h].rearrange("(c p) e -> p c e", p=128), sS_o[0], "sSo0", reads=[("sS_o", 0)])

            def S5():
                group_norm_gate(banks[7][0:n, 0:256], n, s_sgn[0:n, :], K_sgn, s_ubf[0:n, :], K_ubf, ("bank", 7))

            def S6():
                for c in range(2):
                    kb.op("pe", lambda e: e.transpose(pv4s[:, 4 + c, 0:n], s_ubf[0:n, c * 128:(c + 1) * 128], ident[0:n, 0:n]),
                          reads=[K_ubf, "ident"], writes=[("bank", 4)])
                kb.op("act", lambda e: e.copy(uT[:, h * 2:h * 2 + 2, OWN:OWN + NS], pv4s[:, 4:6, 0:n]),
                      reads=[("bank", 4)], writes=[("uT", 8, h)])

            SST = [S0, S1, S2, S3, lambda: S4([0, 1]), lambda: S4([2, 3]), lambda: (S5(), S6())]
            tl = []
            for (kind, t, n_, xsrc, off, rsrc) in tiles:
                it += 1
                tl.append(make_tile(kind, t, xsrc, off, rsrc, it % 2))
            tl[0].A_kv(); tl[0].A_q(); tl[0].A_g()
            for i in range(16):
                nxt = tl[i + 1] if i + 1 < 16 else None
                if nxt:
                    nxt.A_kv()
                if i == 14 and h + 1 < H:
                    nextW["k"] = wload(C_KR, h + 1)
                    nextW["v"] = wload(C_VR, h + 1)
                tl[i].B1()
                if nxt:
                    nxt.A_q()
                if i == 14 and h + 1 < H:
                    nextW["q"] = wload(C_QR, h + 1)
                tl[i].B2()
                if nxt:
                    nxt.A_g()
                if i == 14 and h + 1 < H:
                    nextW["g"] = wload(C_GR, h + 1)
                if i >= 1:
                    tl[i - 1].B4()
                tl[i].B3()
                if i < len(SST):
                    SST[i]()
            tl[15].B4()
            kb.dma("pool", ret_p[h].rearrange("(c p) e -> p c e", p=128), S_f, "retp", reads=["S_f"])

        A.pop()
        kb.barrier()

    def phase6(qaT_s, kaT_s, va_s):
        A.top = 126 * KI
        A.push()
        scale = HD ** -0.5
        NPB = 6
        Kpg = [A.alloc([1024], BF16) for _ in range(NPB)]
        Vpg = [A.alloc([1024], BF16) for _ in range(NPB)]
        KT = [A.alloc([8, 128], BF16) for _ in range(2)]
        PTb = [A.alloc([NPG, 64], BF16) for _ in range(2)]
        pti = A.alloc([4 * NPG], I32)
        ptf = A.alloc([4 * NPG], F32)
        iot = A.alloc([1], F32)
        idx = A.alloc([4 * NPG], I32)
        ksm = A.alloc([8, 32], F32)
        kmb = A.alloc([8, 32], BF16)
        Qm = A.alloc([8, 64], BF16)
        sm = A.alloc([32], F32)
        cmpb = A.alloc([32, 32], F32)
        rank = A.alloc([32], F32)
        Dm = A.alloc([32, 64], BF16)
        selrep = A.alloc([32, 64], BF16)
        hmask = A.alloc([1024], BF16)
        i64 = A.alloc([64], F32)
        sown = A.alloc([4, 64], F32)
        pown = A.alloc([64], F32)
        PTo = A.alloc([64], BF16)
        tmp = A.alloc([1024], F32)
        osel = A.alloc([128], F32)
        rdn = A.alloc([1], F32)
        o_bf = A.alloc([128], BF16)
        kb.dma("sp", pti, ptab.partition_broadcast(128), "c_pt", writes=["pti"])
        kb.dma("pool", hmask[0:64, :], c_hmask, "c_hmask", writes=["hmask"], max_dma_last_dim=4096)
        kb.dma("sp", i64[0:64, :], c_ident64, "c_i64", writes=["i64"])
        kb.dma("sp", sown[0:NS], c_sown.rearrange("p (b c) -> p b c", c=64), "c_sown", writes=["sown"])
        kb.op("pool", lambda e: e.iota(iot, [[0, 1]], base=0, channel_multiplier=1, allow_small_or_imprecise_dtypes=True),
              writes=["iot"])
        kb.op("dve", lambda e: e.tensor_copy(ptf, pti), reads=["pti"], writes=["ptf"])
        kb.op("dve", lambda e: e.tensor_scalar(idx, ptf, 128.0, iot[:, 0:1], ALU.mult, ALU.add),
              reads=["ptf", "iot"], writes=["idx"])
        kb.op("dve", lambda e: e.memset(Qm, 0.0), writes=["Qm"])
        ig = 0
        for b in range(4):
            PT = PTb[b % 2]
            ptk = ("PTb", b % 2)
            kb.op("dve", lambda e: e.tensor_copy(
                bcast(Qm, [[64 + 8, 8], [1, 8]]), qaT_s[:, :, b * 8:(b + 1) * 8]),
                reads=[("qaT_s", 0), ("qaT_s", 1)], writes=["Qm"])
            pinfo = {}

            def T1(pg):
                nonlocal ig
                ig += 1
                ks = ig % NPB
                kp = Kpg[ks]
                j = b * NPG + pg
                kb.dma_fn("pool", lambda e: e.indirect_dma_start(
                    out=kp, out_offset=None, in_=cache_k,
                    in_offset=bass.IndirectOffsetOnAxis(ap=idx[:, j:j + 1], axis=0)),
                    f"kpg{ks}", reads=["idx"], writes=[("Kpg", ks)])
                tbk = ig % 2
                pvT = bank_bf(tbk).rearrange("p (a b) -> p a b", b=128)
                for h in range(8):
                    kb.op("pe", lambda e: e.transpose(pvT[:, h, :], kp[:, h * 128:(h + 1) * 128], ident),
                          reads=[("Kpg", ks), "ident"], writes=[("bank", tbk)])
                kt = KT[ig % 2]
                if ig % 2 == 0:
                    kb.op("act", lambda e: e.copy(kt, pvT), reads=[("bank", tbk)], writes=[("KT", ig % 2)])
                else:
                    kb.op("dve", lambda e: e.tensor_copy(kt, pvT), reads=[("bank", tbk)], writes=[("KT", ig % 2)])
                pinfo[pg] = (ks, kp, kt, ig % 2)

            def L1(pg):
                ks, kp, kt, par = pinfo[pg]
                for h in range(8):
                    kb.op("pe", lambda e: e.matmul(banks[2][:, h * 64 + pg:h * 64 + pg + 1],
                                                   kp[:, h * 128:(h + 1) * 128], ones_bf[:, 0:1], start=True, stop=True),
                          reads=[("Kpg", ks), "ones"], writes=[("bank", 2)])
                for h in range(8):
                    kb.op("pe", lambda e: e.matmul(banks[3][:, h * 8:(h + 1) * 8], kt[:, h, :],
                                                   qaT_s[:, h, b * 8:(b + 1) * 8], start=True, stop=True),
                          reads=[("KT", par)], writes=[("bank", 3)])
                kb.op("act", lambda e: e.activation(PT[:, pg, :], banks[3][:, 0:64], AF.Exp, scale=scale),
                      reads=[("bank", 3)], writes=[ptk])

            T1(0)
            for pg in range(NPG):
                if pg + 1 < NPG:
                    T1(pg + 1)
                L1(pg)
            kb.op("dve", lambda e: e.tensor_reduce(
                ksm, banks[2][:, :].rearrange("p (h n t) -> p h n t", n=32, t=2), AX.X, ALU.add),
                reads=[("bank", 2)], writes=["ksm"])
            kb.op("dve", lambda e: e.tensor_copy(kmb, ksm), reads=["ksm"], writes=["kmb"])
            for h in range(8):
                kb.op("pe", lambda e, h=h: e.matmul(banks[7][0:64, 0:32], Qm[:, h, :], kmb[:, h, :],
                                                    start=(h == 0), stop=(h == 7)),
                      reads=["Qm", "kmb"], writes=[("bank", 7)])
            kb.op("dve", lambda e: e.tensor_copy(sm[0:64, :], banks[7][0:64, 0:32]), reads=[("bank", 7)], writes=["sm6"])
            kb.op("dve", lambda e: e.tensor_tensor(
                cmpb[0:64], bcast(sm[0:64, :], [[0, 32], [1, 32]]), bcast(sm[0:64, :], [[1, 32], [0, 32]]), ALU.is_gt),
                reads=["sm6"], writes=["cmp6"])
            kb.op("dve", lambda e: e.tensor_reduce(rank[0:64, :], cmpb[0:64], AX.X, ALU.add), reads=["cmp6"], writes=["rank6"])
            kb.op("dve", lambda e: e.tensor_scalar(rank[0:64, :], rank[0:64, :], 3.0, None, ALU.is_lt),
                  reads=["rank6"], writes=["rank6"])
            kb.op("dve", lambda e: e.tensor_tensor(
                Dm[0:64], bcast(rank[0:64, :], [[1, 32], [0, 64]]), bcast(i64[0:64, :], [[0, 32], [1, 64]]), ALU.mult),
                reads=["rank6", "i64"], writes=["Dm"])
            Dflat = Dm[0:64].rearrange("p a b -> p (a b)")
            for c4 in range(4):
                kb.op("pe", lambda e, c4=c4: e.matmul(banks[4 + c4][:, :], ones_bf[0:64, :], Dflat[:, c4 * 512:(c4 + 1) * 512],
                                                      start=True, stop=True),
                      reads=["Dm", "ones"], writes=[("bank", 4 + c4)])
                kb.op("act" if c4 % 2 else "dve", (lambda e, c4=c4: e.copy(
                    selrep.rearrange("p a b -> p (a b)")[:, c4 * 512:(c4 + 1) * 512], banks[4 + c4][:, :])) if c4 % 2 else
                    (lambda e, c4=c4: e.tensor_copy(
                        selrep.rearrange("p a b -> p (a b)")[:, c4 * 512:(c4 + 1) * 512], banks[4 + c4][:, :])),
                    reads=[("bank", 4 + c4)], writes=[("selrep", c4)])
            kb.op("dve", lambda e: e.tensor_tensor(
                PT.rearrange("p (n t) c -> p n t c", t=2), PT.rearrange("p (n t) c -> p n t c", t=2),
                bcast(selrep, [[64, 32], [0, 2], [1, 64]]), ALU.mult),
                reads=[ptk] + [("selrep", c4) for c4 in range(4)], writes=[ptk])
            for h in range(8):
                kb.op("pe", lambda e, h=h: e.matmul(banks[7][0:NS, 64 + h * 8:64 + (h + 1) * 8], kaT_s[:, h, :],
                                                    qaT_s[:, h, b * 8:(b + 1) * 8], start=True, stop=True),
                      reads=[("kaT_s", 0), ("kaT_s", 1)], writes=[("bank", 7)])
            kb.op("act", lambda e: e.activation(pown[0:NS, :], banks[7][0:NS, 64:128], AF.Exp, scale=scale),
                  reads=[("bank", 7)], writes=["pown"])
            kb.op("dve", lambda e: e.tensor_tensor(PTo[0:NS, :], pown[0:NS, :], sown[0:NS, b, :], ALU.mult),
                  reads=["pown", "sown"], writes=["PTo"])
            for pg in range(NPG):
                ig += 1
                vs = ig % NPB
                vp = Vpg[vs]
                j = b * NPG + pg
                kb.dma_fn("pool", lambda e, vp=vp, j=j: e.indirect_dma_start(
                    out=vp, out_offset=None, in_=cache_v,
                    in_offset=bass.IndirectOffsetOnAxis(ap=idx[:, j:j + 1], axis=0)),
                    f"vpg{vs}", reads=["idx"], writes=[("Vpg", vs)])
                for c2 in range(2):
                    kb.op("pe", lambda e, c2=c2: e.matmul(banks[4 + c2][0:64, :], PT[:, pg, :], vp[:, c2 * 512:(c2 + 1) * 512],
                                                          start=(pg == 0), stop=False),
                          reads=[ptk, ("Vpg", vs)], writes=[("bank", 4 + c2)])
                kb.op("pe", lambda e: e.matmul(banks[6][0:64, 0:1], PT[:, pg, :], ones_bf[:, 0:1], start=(pg == 0), stop=False),
                      reads=[ptk, "ones"], writes=[("bank", 6)])
            for c2 in range(2):
                kb.op("pe", lambda e, c2=c2: e.matmul(banks[4 + c2][0:64, :], PTo[0:NS, :], va_s[0:NS, c2 * 512:(c2 + 1) * 512],
                                                      start=False, stop=True),
                      reads=["PTo", ("va_s", 0), ("va_s", 1)], writes=[("bank", 4 + c2)])
            kb.op("pe", lambda e: e.matmul(banks[6][0:64, 0:1], PTo[0:NS, :], ones_bf[0:NS, 0:1], start=False, stop=True),
                  reads=["PTo", "ones"], writes=[("bank", 6)])
            for c2 in range(2):
                kb.op("dve", lambda e, c2=c2: e.tensor_tensor(tmp[0:64, c2 * 512:(c2 + 1) * 512], banks[4 + c2][0:64, :],
                                                              hmask[0:64, c2 * 512:(c2 + 1) * 512], ALU.mult),
                      reads=[("bank", 4 + c2), "hmask"], writes=[("tmp6", c2)])
            kb.op("dve", lambda e: e.tensor_reduce(osel[0:64, :], tmp[0:64, :].rearrange("p (h d) -> p d h", d=128), AX.X, ALU.add),
                  reads=[("tmp6", 0), ("tmp6", 1)], writes=["osel"])
            kb.op("dve", lambda e: e.reciprocal(rdn[0:64, :], banks[6][0:64, 0:1]), reads=[("bank", 6)], writes=["rdn"])
            kb.op("dve", lambda e: e.tensor_scalar(o_bf[0:64, :], osel[0:64, :], rdn[0:64, 0:1], None, ALU.mult),
                  reads=["osel", "rdn"], writes=["o_bf6"])
            pvo = bank_bf(7)
            kb.op("pe", lambda e: e.transpose(pvo[:, 512:576], o_bf[0:64, :], ident[0:64, 0:64]),
                  reads=["o_bf6", "ident"], writes=[("bank", 7)])
            kb.op("act", lambda e: e.copy(oaT[:, :, OWN + b * 8:OWN + (b + 1) * 8],
                                          pvo[:, 512:576].rearrange("p (h q) -> p h q", q=8)),
                  reads=[("bank", 7)], writes=[("oaT_s", b)])
        A.pop()
        kb.barrier()

    def phase6_stub(qaT_s, kaT_s, va_s):
        kb.op("dve", lambda e: e.memset(oaT[:, :, OWN:OWN + NS], 0.0), writes=[("oaT_s",)])
        kb.barrier()

    TOKT = [(t, 128, t * 128) for t in range(8)] + [(8, NS, OWN)]
    mixT = xTp
    hT = xT
    racc = A.at(72 * KI, [9, D], F32)
    w_ret_v = w_ret.rearrange("(kc p) n -> p kc n", p=128)
    w_att_v = w_att.rearrange("(kc p) n -> p kc n", p=128)
    w_out_v = w_out.rearrange("(kc p) n -> p kc n", p=128)
    w_up_v = w_up.rearrange("(kc p) n -> p kc n", p=128)
    w_down_v = w_down.rearrange("(fc p) n -> p fc n", p=128)

    def layer_norm_tile(t, n, lng, lnb, scr):
        stats, mv, sd = scr
        r = racc[0:n, t, :]
        for c4 in range(4):
            kb.op("dve", lambda e, c4=c4, r=r: e.bn_stats(stats[0:n, c4, :], r[:, c4 * 512:(c4 + 1) * 512]),
                  reads=[("racc", t)], writes=["ln_stats"])
        kb.op("dve", lambda e: e.bn_aggr(mv[0:n, :], stats[0:n, :, :].rearrange("p a b -> p (a b)")),
              reads=["ln_stats"], writes=["ln_mv"])
        kb.op("act", lambda e: e.activation(sd[0:n, 0:1], mv[0:n, 1:2], AF.Sqrt, bias=1e-5, scale=1.0),
              reads=["ln_mv"], writes=["ln_sd"])
        kb.op("dve", lambda e: e.reciprocal(sd[0:n, 1:2], sd[0:n, 0:1]), reads=["ln_sd"], writes=["ln_rs"])
        kb.op("dve", lambda e, r=r: e.scalar_tensor_tensor(r, r, mv[0:n, 0:1], lng[0:n, :], ALU.subtract, ALU.mult),
              reads=[("racc", t), "ln_mv", "lnp"], writes=[("racc", t)])
        kb.op("dve", lambda e, r=r: e.scalar_tensor_tensor(r, r, sd[0:n, 1:2], lnb[0:n, :], ALU.mult, ALU.add),
              reads=[("racc", t), "ln_rs", "lnp"], writes=[("racc", t)])

    def phase4a():
        A.top = 126 * KI
        A.push()
        GW = 256
        Wgr = [A.alloc([16, GW], BF16) for _ in range(2)]
        Wga = [A.alloc([16, GW], BF16) for _ in range(2)]
        Wr = [A.alloc([16, GW], BF16) for _ in range(2)]
        Wa = [A.alloc([8, GW], BF16) for _ in range(2)]
        sg = [A.alloc([2 * GW], F32) for _ in range(2)]
        prod = A.alloc([2 * GW], F32)
        mixb = [A.alloc([GW], BF16) for _ in range(2)]
        it = 0
        pending = []
        for g in range(D // GW):
            cs = slice(g * GW, (g + 1) * GW)
            s2 = g % 2
            kb.dma("pool", Wgr[s2], w_in_v[:, :, C_GBR + g * GW:C_GBR + (g + 1) * GW], f"Wgr{s2}", writes=[("Wgr", s2)])
            kb.dma("pool", Wga[s2], w_in_v[:, :, C_GBA + g * GW:C_GBA + (g + 1) * GW], f"Wga{s2}", writes=[("Wga", s2)])
            kb.dma("pool", Wr[s2], w_ret_v[:, :, cs], f"Wr{s2}", writes=[("Wr", s2)])
            kb.dma("pool", Wa[s2], w_att_v[:, :, cs], f"Wa{s2}", writes=[("Wa", s2)])
            for (t, n, off) in TOKT:
                it += 1
                bx, by = (it % 2) * 2, (it % 2) * 2 + 1
                _run_pending_after_mm = True
                for (bi, c0, src, W, wk, kcs) in ((bx, 0, xT, Wgr[s2], ("Wgr", s2), 16), (bx, GW, xT, Wga[s2], ("Wga", s2), 16),
                                                  (by, 0, uT, Wr[s2], ("Wr", s2), 16), (by, GW, oaT, Wa[s2], ("Wa", s2), 8)):
                    for kc in range(kcs):
                        kb.op("pe", lambda e: e.matmul(
                            banks[bi][0:n, c0:c0 + GW], src[:, kc, off:off + n], W[:, kc, :], start=(kc == 0), stop=(kc == kcs - 1)),
                            reads=[wk], writes=[("bank", bi)])
                while pending:
                    pending.pop(0)()
                sgt = sg[it % 2]
                kb.op("act", lambda e: e.activation(sgt[0:n, :], banks[bx][0:n, :], AF.Sigmoid),
                      reads=[("bank", bx)], writes=[("sg", it % 2)])
                kb.op("dve", lambda e: e.tensor_tensor(prod[0:n, :], sgt[0:n, :], banks[by][0:n, :], ALU.mult),
                      reads=[("sg", it % 2), ("bank", by)], writes=["prod"])
                mb = mixb[it % 2]
                kb.op("dve", lambda e: e.tensor_tensor(mb[0:n, :], prod[0:n, 0:GW], prod[0:n, GW:2 * GW], ALU.add),
                      reads=["prod"], writes=[("mixb", it % 2)])
                def fin(it=it, mb=mb, n=n, off=off, g=g, t=t):
                    tb = 4 + it % 2
                    pv = bank_bf(tb).rearrange("p (a b) -> p a b", b=128)
                    for j in range(2):
                        kb.op("pe", lambda e: e.transpose(pv[:, j, 0:n], mb[0:n, j * 128:(j + 1) * 128], ident[0:n, 0:n]),
                              reads=[("mixb", it % 2), "ident"], writes=[("bank", tb)])
                    kb.op("act", lambda e: e.copy(mixT[:, g * 2:(g + 1) * 2, off:off + n], pv[:, 0:2, 0:n]),
                          reads=[("bank", tb)], writes=[("mixT", t, g)])
                pending.append(fin)
        while pending:
            pending.pop(0)()
        A.pop()
        kb.barrier()

    def phase4b():
        A.top = 144 * KI
        A.push()
        ring = Ring("w4b", [A.alloc([16, 512], BF16) for _ in range(2)])
        lng = A.alloc([D], F32)
        lnb = A.alloc([D], F32)
        xs = [A.alloc([512], F32) for _ in range(2)]
        hb = A.alloc([D], BF16)
        stats = A.alloc([4, 6], F32)
        mv = A.alloc([2], F32)
        sd = A.alloc([2], F32)
        kb.dma("sp", lng, c_ln[0:1, :].partition_broadcast(128), "c_lng", writes=["lnp"])
        kb.dma("sp", lnb, c_ln[1:2, :].partition_broadcast(128), "c_lnb", writes=["lnp"])
        kb.op("act", lambda e: e.activation(lng, lng, AF.Copy, scale=ALPHA), reads=["lnp"], writes=["lnp"])
        kb.op("act", lambda e: e.activation(lnb, lnb, AF.Copy, scale=ALPHA), reads=["lnp"], writes=["lnp"])
        it = 0
        pend4 = []
        for g in range(4):
            cs = slice(g * 512, (g + 1) * 512)
            W, wkey = ring.load(w_out_v[:, :, cs], 16, 512)
            for (t, n, off) in TOKT:
                it += 1
                pb = it % 2
                for kc in range(16):
                    kb.op("pe", lambda e, pb=pb, kc=kc, n=n, off=off, W=W: e.matmul(
                        banks[pb][0:n, :], mixT[:, kc, off:off + n], W[:, kc, :], start=(kc == 0), stop=(kc == 15)),
                        reads=[wkey], writes=[("bank", pb)])
                xsl = xs[it % 2]
                src = x_own[t * 128:t * 128 + n, cs] if t < 8 else x_s[0:n, cs]
                kb.dma("sp", xsl[0:n, :], src, f"xs{it%2}", writes=[("xs", it % 2)])
                kb.op("dve", lambda e, xsl=xsl, n=n, pb=pb, t=t, cs=cs: e.scalar_tensor_tensor(
                    racc[0:n, t, cs], xsl[0:n, :], ALPHA, banks[pb][0:n, :], ALU.mult, ALU.add),
                    reads=[("xs", it % 2), ("bank", pb)], writes=[("racc", t)])
                if g == 3:
                    while pend4:
                        pend4.pop(0)()
                    layer_norm_tile(t, n, lng, lnb, (stats, mv, sd))
                    kb.op("act", lambda e, n=n, t=t: e.activation(hb[0:n, :], racc[0:n, t, :], AF.Copy, scale=1.0 / ALPHA),
                          reads=[("racc", t)], writes=["hb"])

                    def fin4(t=t, n=n, off=off):
                        for half in range(2):
                            tb = 2 + half
                            pv = bank_bf(tb).rearrange("p (a b) -> p a b", b=128)
                            for j in range(8):
                                kc = half * 8 + j
                                kb.op("pe", lambda e: e.transpose(
                                    pv[:, j, 0:n], hb[0:n, kc * 128:(kc + 1) * 128], ident[0:n, 0:n]),
                                    reads=["hb", "ident"], writes=[("bank", tb)])
                            kb.op("act", lambda e: e.copy(hT[:, half * 8:half * 8 + 8, off:off + n], pv[:, :, 0:n]),
                                  reads=[("bank", tb)], writes=[("hT", t, half)])
                    pend4.append(fin4)
        while pend4:
            pend4.pop(0)()
        A.pop()
        kb.barrier()

    def phase5():
        A.top = 144 * KI
        A.push()
        ring_up = Ring("wup", [A.at(36 * KI, [16, 512], BF16), A.at(52 * KI, [16, 512], BF16)])
        ring_dn = Ring("wdn", [A.alloc([4, D], BF16) for _ in range(2)])
        aT = [A.alloc([4, NTOK], BF16) for _ in range(2)]
        rl = [A.at(68 * KI, [512], F32), A.at(70 * KI, [512], F32)]
        TG = [(0, 512), (512, 512), (1024, NS)]
        iu = 0
        idn = 0
        for fg in range(16):
            Wu, uk = ring_up.load(w_up_v[:, :, fg * 512:(fg + 1) * 512], 16, 512)
            Wd, dk = ring_dn.load(w_down_v[:, fg * 4:(fg + 1) * 4, :], 4, D)
            a = aT[fg % 2]
            for fc in range(4):
                for (t0, n) in TG:
                    iu += 1
                    pb = iu % 2
                    for kc in range(16):
                        kb.op("pe", lambda e, pb=pb, kc=kc, fc=fc, t0=t0, n=n, Wu=Wu: e.matmul(
                            banks[pb][:, 0:n], Wu[:, kc, fc * 128:(fc + 1) * 128], hT[:, kc, t0:t0 + n],
                            start=(kc == 0), stop=(kc == 15)),
                            reads=[uk], writes=[("bank", pb)])
                    r = rl[iu % 2]
                    kb.op("act", lambda e, pb=pb, n=n, r=r: e.activation(r[:, 0:n], banks[pb][:, 0:n], AF.Relu),
                          reads=[("bank", pb)], writes=[("rl", iu % 2)])
                    kb.op("pool", lambda e, a=a, fc=fc, t0=t0, n=n, r=r: e.tensor_tensor(
                        a[:, fc, t0:t0 + n], r[:, 0:n], r[:, 0:n], ALU.mult),
                        reads=[("rl", iu % 2)], writes=[("aT", fg % 2, fc, t0)])
            for (t, n, off) in TOKT:
                t0 = 0 if off < 512 else (512 if off < 1024 else 1024)
                for cg in range(4):
                    idn += 1
                    pb = 2 + idn % 6
                    for fc in range(4):
                        kb.op("pe", lambda e, pb=pb, fc=fc, n=n, off=off, cg=cg, a=a, Wd=Wd: e.matmul(
                            banks[pb][0:n, :], a[:, fc, off:off + n], Wd[:, fc, cg * 512:(cg + 1) * 512],
                            start=(fc == 0), stop=(fc == 3)),
                            reads=[dk, ("aT", fg % 2, fc, t0)], writes=[("bank", pb)])
                    kb.op("dve", lambda e, pb=pb, n=n, t=t, cg=cg: e.tensor_tensor(
                        racc[0:n, t, cg * 512:(cg + 1) * 512], racc[0:n, t, cg * 512:(cg + 1) * 512],
                        banks[pb][0:n, :], ALU.add),
                        reads=[("bank", pb), ("racc", t)], writes=[("racc", t)])
        A.pop()
        kb.barrier()
        A.top = 144 * KI
        A.push()
        lng = A.alloc([D], F32)
        lnb = A.alloc([D], F32)
        stats = A.alloc([4, 6], F32)
        mv = A.alloc([2], F32)
        sd = A.alloc([2], F32)
        kb.dma("sp", lng, c_ln[2:3, :].partition_broadcast(128), "c_lng2", writes=["lnp"])
        kb.dma("sp", lnb, c_ln[3:4, :].partition_broadcast(128), "c_lnb2", writes=["lnp"])
        for (t, n, off) in TOKT:
            layer_norm_tile(t, n, lng, lnb, (stats, mv, sd))
            dst = y_own[t * 128:(t + 1) * 128, :] if t < 8 else y_s[:, :]
            kb.dma("sp", dst, racc[0:n, t, :], "yout", reads=[("racc", t)])
        A.pop()
        kb.barrier()

    qaT_s = kb.sbuf("qaT_s", [128, H, NS], BF16)[:]
    kaT_s = kb.sbuf("kaT_s", [128, H, NS], BF16)[:]
    va_s = kb.sbuf("va_s", [128, 1024], BF16)[:]

    if "p1" in phases:
        phase1()
    if "p3" in phases:
        phase3(qaT_s, kaT_s, va_s)
    if "p2" in phases:
        phase2()
    if "p6" in phases:
        phase6(qaT_s, kaT_s, va_s)
    if "p6s" in phases:
        phase6_stub(qaT_s, kaT_s, va_s)
    if "p4" in phases:
        phase4a()
        phase4b()
    if "p5" in phases:
        phase5()

    kb.finish("sp")
    import os
    if os.environ.get("KB_LOG"):
        print("PHASE_LOG", kb.phase_log)
        print("QUEUE_LEN", {k: len(v) for k, v in kb.prog.items()})
    kb.emit()
    kb.close()
    return nc


def _host_consts(core):
    half = core % 2
    f32 = np.float32
    c = {}
    c["c_ident"] = np.eye(128, dtype=f32)
    c["c_i64"] = np.eye(64, dtype=f32)
    inv = (np.float32(10000.0) ** (-np.arange(128, dtype=f32) / np.float32(128))).astype(f32)

    def rot(pos):
        ang = pos.astype(f32)[:, None] * inv[None, :]
        return np.concatenate([np.cos(ang), np.sin(ang)], axis=-1).astype(f32)

    p_own = half * OWN + np.arange(OWN)
    c["c_rot_own"] = rot(p_own).reshape(8, 128, 256)
    c["c_rot_prev"] = rot(np.arange(OWN)).reshape(8, 128, 256)
    c["c_rot_s"] = rot(PAST + (np.arange(NS) % 8))
    lg = np.log1p(-np.exp2(-5.0 - np.arange(H, dtype=np.float64)))
    i = np.arange(128, dtype=np.float64)
    c["c_qd"] = np.exp(lg[None, :] * (i[:, None] + 1.0)).astype(f32)
    c["c_kd"] = (np.exp(lg[None, :] * (127.0 - i[:, None])) / 16.0).astype(f32)
    jj, ii = np.meshgrid(i, i, indexing="ij")
    m = np.where(ii[:, None, :] >= jj[:, None, :], np.exp(-lg[None, :, None] * (jj[:, None, :] + 1.0)), 0.0) / 16.0
    c["c_maskT"] = m.astype(f32)
    il = (np.arange(NS) % 8).astype(np.float64)
    bb = np.arange(NS) // 8
    c["c_sqd"] = np.exp(lg[None, :] * (il[:, None] + 1.0)).astype(f32)
    c["c_skd"] = (np.exp(lg[None, :] * (7.0 - il[:, None])) / 16.0).astype(f32)
    same = (bb[:, None] == bb[None, :])
    caus = (il[None, :] >= il[:, None])
    sm = np.where((same & caus)[:, None, :], np.exp(-lg[None, :, None] * (il[:, None, None] + 1.0)), 0.0) / 16.0
    c["c_smaskT"] = sm.astype(f32)
    c["c_bmP"] = (bb[:, None] == np.arange(4)[None, :]).astype(f32)
    c["c_bmF"] = np.tile((np.arange(4)[:, None] == bb[None, :]).astype(f32).reshape(1, 4 * NS), (128, 1))
    kk, qq = np.meshgrid(np.arange(128), np.arange(128), indexing="ij")
    c["c_causal"] = np.where(qq >= kk, 0.0, NEG).astype(f32)
    E = np.zeros((32, 32, 128), f32)
    for r in range(32):
        E[r, r, :] = 1.0
    c["c_E"] = E.reshape(32, 32 * 128)
    el = np.zeros((8, 8), f32)
    for t in range(8):
        for n in range(8):
            if n < 4:
                el[t, n] = 1.0 if half == 1 else 0.0
            else:
                el[t, n] = 1.0 if (n - 4) < t // 2 else 0.0
    c["c_elig"] = np.tile(el.reshape(1, 64), (128, 1)).astype(f32)
    c["c_negm"] = np.where(c["c_elig"] > 0, 0.0, -1e30).astype(f32)
    so = np.zeros((NS, 4, 8, 8), f32)
    for j in range(NS):
        for q in range(8):
            if (j % 8) <= q:
                so[j, j // 8, :, q] = 1.0
    c["c_sown"] = so.reshape(NS, 256)
    hm = np.zeros((8, 8, 8, 128), f32)
    for h in range(8):
        hm[h, :, h, :] = 1.0
    c["c_hmask"] = hm.reshape(64, 1024)
    return c


_PROG_CACHE = {}


def kernel(x_prompt, x_sample, cache_k, cache_v, state_ret, page_table,
           w_in, ret_gn_gain, w_ret_br, w_att_br, w_out, ln1_g, ln1_b,
           w_up, w_down, ln2_g, ln2_b):
    f32 = np.float32
    n_phys = cache_k.shape[1]
    ck = np.ascontiguousarray(cache_k[0].reshape(n_phys * PAGE, 1024))
    cv = np.ascontiguousarray(cache_v[0].reshape(n_phys * PAGE, 1024))
    key = n_phys
    if key not in _PROG_CACHE:
        import os
        ph = os.environ.get("KPHASES")
        _PROG_CACHE[key] = build_program(n_phys * PAGE, tuple(ph.split(","))) if ph else build_program(n_phys * PAGE)
    nc = _PROG_CACHE[key]
    shared = {
        "w_in": np.ascontiguousarray(w_in[0]), "w_ret": np.ascontiguousarray(w_ret_br[0]),
        "w_att": np.ascontiguousarray(w_att_br[0]), "w_out": np.ascontiguousarray(w_out[0]),
        "w_up": np.ascontiguousarray(w_up[0]), "w_down": np.ascontiguousarray(w_down[0]),
        "cache_k": ck, "cache_v": cv,
        "c_gn": np.ascontiguousarray(ret_gn_gain[0].reshape(1, D)),
        "c_ln": np.ascontiguousarray(np.stack([ln1_g[0], ln1_b[0], ln2_g[0], ln2_b[0]], 0)),
    }
    in_maps = []
    zeros_prev = np.zeros((OWN, D), f32)
    for c in range(NCORE):
        b, half = c // 2, c % 2
        m = dict(shared)
        m["x_own"] = np.ascontiguousarray(x_prompt[b, half * OWN:(half + 1) * OWN])
        m["x_prev"] = np.ascontiguousarray(x_prompt[b, 0:OWN]) if half == 1 else zeros_prev
        m["x_s"] = np.ascontiguousarray(x_sample[4 * c:4 * c + 4].reshape(NS, D))
        m["state_s"] = np.ascontiguousarray(state_ret[0, 4 * c:4 * c + 4])
        m["ptab"] = np.ascontiguousarray(page_table[4 * c:4 * c + 4].reshape(1, 4 * NPG)).astype(np.int32)
        m.update(_host_consts(c))
        in_maps.append(m)
    import os as _os
    if _os.environ.get("KTRACE"):
        res = run_bass_kernel_spmd(nc, in_maps, core_ids=list(range(NCORE)), trace=True)
        print("KTRACE exec_time_ns", res.exec_time_ns)
    else:
        res = run_bass_kernel_spmd(nc, in_maps, core_ids=list(range(NCORE)))
    R = res.results
    B = x_prompt.shape[0]
    y_p = np.zeros((B, SEQ, D), f32)
    k_p = np.zeros((1, B, SEQ, H, HD), f32)
    v_p = np.zeros((1, B, SEQ, H, HD), f32)
    s_p = np.zeros((1, B, H, DK, DK), f32)
    y_s = np.zeros((32, 8, D), f32)
    k_s = np.zeros((1, 32, 8, H, HD), f32)
    v_s = np.zeros((1, 32, 8, H, HD), f32)
    s_s = np.zeros((1, 32, H, DK, DK), f32)
    for c in range(NCORE):
        b, half = c // 2, c % 2
        sl = slice(half * OWN, (half + 1) * OWN)
        y_p[b, sl] = R[c]["y_own"]
        k_p[0, b, sl] = R[c]["k_own"].reshape(OWN, H, HD)
        v_p[0, b, sl] = R[c]["v_own"].reshape(OWN, H, HD)
        if half == 1:
            s_p[0, b] = R[c]["ret_p"]
        y_s[4 * c:4 * c + 4] = R[c]["y_s"].reshape(4, 8, D)
        k_s[0, 4 * c:4 * c + 4] = R[c]["k_s"].reshape(4, 8, H, HD)
        v_s[0, 4 * c:4 * c + 4] = R[c]["v_s"].reshape(4, 8, H, HD)
        s_s[0, 4 * c:4 * c + 4] = R[c]["ret_s"]
    return (y_p, y_s, k_p, v_p, s_p, k_s, v_s, s_s)
```
